# Optimizing a Trainium2 kernel written in Bass

```python
import math
import jax, jax.numpy as jnp
from jax import lax
import numpy as np

D_MODEL = 1024
BATCH = 16
SEQ = 256
DEPTH = 4
DEC_BATCH = 2
DEC_SEQ = 1024
PAST_LEN = 256

GRID_W = 64
DN_HEADS = D_MODEL // 128
DN_DK = 128
DN_DV = 128
DN_CHUNK = 64
CONV_K = 3
SC_WIDTH = D_MODEL
ATT_HEADS = D_MODEL // 128
ATT_KV_HEADS = ATT_HEADS // 4
HEAD_DIM = 128
Q_BLOCK = 128
ROPE_THETA = 10000.0
MLP_HIDDEN = 4 * D_MODEL
N_BRANCH = 3
RMS_EPS = 1e-6
IN_SIZES = (DN_HEADS * DN_DK, DN_HEADS * DN_DK, DN_HEADS * DN_DV, DN_HEADS * DN_DV, 2 * DN_HEADS, 2 * DN_HEADS, SC_WIDTH, SC_WIDTH, SC_WIDTH, ATT_HEADS * HEAD_DIM, ATT_KV_HEADS * HEAD_DIM, ATT_KV_HEADS * HEAD_DIM, N_BRANCH * D_MODEL)
IN_TOTAL = sum(IN_SIZES)

kernel_name = "hybrid_flow_prefix_trunk_step"


def split_cols(t, sizes):
    out, start = [], 0
    for s in sizes:
        out.append(t[..., start:start + s])
        start += s
    return out


def rmsnorm(x, g):
    xf = x.astype(jnp.float32)
    y = xf * lax.rsqrt(jnp.mean(xf * xf, axis=-1, keepdims=True) + RMS_EPS)
    return (y * g.astype(jnp.float32)).astype(x.dtype)


def l2norm(x):
    xf = x.astype(jnp.float32)
    return (xf * lax.rsqrt(jnp.sum(xf * xf, axis=-1, keepdims=True) + RMS_EPS)).astype(x.dtype)


def dwconv_centred(x, w):
    K, C = w.shape
    return lax.conv_general_dilated(x, w.reshape(K, 1, C).astype(x.dtype), window_strides=(1,), padding=((K // 2, K // 2),), dimension_numbers=("NWC", "WIO", "NWC"), feature_group_count=C)


def gated_delta_rule(q, k, v, g, beta, s0):
    B, L, H, DK = q.shape
    DV = v.shape[-1]
    C = DN_CHUNK
    N = L // C
    f32 = jnp.float32

    def chunks(t):
        t = t.astype(f32).reshape((B, N, C, H) + t.shape[3:])
        return jnp.moveaxis(t, (1, 3), (0, 2))

    qc = chunks(q) * (DK ** -0.5)
    kc = chunks(k)
    vc = chunks(v)
    bc = chunks(beta)
    gc = jnp.cumsum(chunks(g), axis=-1)
    incl = jnp.tril(jnp.ones((C, C), dtype=bool))
    strict = jnp.tril(jnp.ones((C, C), dtype=bool), -1)
    diff = gc[..., :, None] - gc[..., None, :]
    decay = jnp.where(incl, jnp.exp(jnp.where(incl, diff, 0.0)), 0.0)
    kb = kc * bc[..., None]
    m = jnp.where(strict, jnp.einsum("nbhid,nbhjd->nbhij", kb, kc) * decay, 0.0)
    a = m + jnp.eye(C, dtype=f32)
    u = lax.linalg.triangular_solve(a, vc * bc[..., None], left_side=True, lower=True)
    w = lax.linalg.triangular_solve(a, kb * jnp.exp(gc)[..., None], left_side=True, lower=True)
    qk = jnp.einsum("nbhid,nbhjd->nbhij", qc, kc) * decay
    q_dec = qc * jnp.exp(gc)[..., None]
    k_dec = kc * jnp.exp(gc[..., -1:] - gc)[..., None]
    g_last = jnp.exp(gc[..., -1])[..., None, None]

    def step(s, xs):
        u_n, w_n, qk_n, qd_n, kd_n, gl_n = xs
        v_new = u_n - jnp.einsum("bhcd,bhdv->bhcv", w_n, s)
        o_n = jnp.einsum("bhcd,bhdv->bhcv", qd_n, s) + jnp.einsum("bhij,bhjv->bhiv", qk_n, v_new)
        s = s * gl_n + jnp.einsum("bhcd,bhcv->bhdv", kd_n, v_new)
        return s, o_n

    s_fin, o = lax.scan(step, s0.astype(f32), (u, w, qk, q_dec, k_dec, g_last))
    o = jnp.moveaxis(o, (0, 2), (1, 3)).reshape(B, L, H, DV)
    return o.astype(v.dtype), s_fin.astype(s0.dtype)


def axial_angles(n_tokens):
    rows = n_tokens // GRID_W
    row = jnp.repeat(jnp.arange(rows), GRID_W).astype(jnp.float32)
    col = jnp.tile(jnp.arange(GRID_W), rows).astype(jnp.float32)
    n_freq = HEAD_DIM // 4
    freqs = ROPE_THETA ** (-jnp.arange(n_freq, dtype=jnp.float32) / n_freq)
    return row[:, None] * freqs, col[:, None] * freqs


def rope_axis(x, ang):
    m = ang.shape[-1]
    cos = jnp.cos(ang)[None, :, None, :]
    sin = jnp.sin(ang)[None, :, None, :]
    x1 = x[..., :m].astype(jnp.float32)
    x2 = x[..., m:].astype(jnp.float32)
    return jnp.concatenate([x1 * cos - x2 * sin, x1 * sin + x2 * cos], axis=-1).astype(x.dtype)


def rope2d(x, ang_row, ang_col):
    h = x.shape[-1] // 2
    return jnp.concatenate([rope_axis(x[..., :h], ang_row), rope_axis(x[..., h:], ang_col)], axis=-1)


def block_attention(q, keys, vals):
    B, Lq, H, HD = q.shape
    KV = keys.shape[2]
    G = H // KV
    nb = Lq // Q_BLOCK
    qb = jnp.moveaxis(q.reshape(B, nb, Q_BLOCK, KV, G, HD), 1, 0)

    def one_block(qblk):
        s = jnp.einsum("bqkgd,bskd->bkgqs", qblk, keys).astype(jnp.float32) * (HD ** -0.5)
        p = jax.nn.softmax(s, axis=-1).astype(vals.dtype)
        return jnp.einsum("bkgqs,bskd->bqkgd", p, vals)

    o = lax.map(one_block, qb)
    return jnp.moveaxis(o, 0, 1).reshape(B, Lq, H * HD)


def trunk_layer(x, cond, lw, ctx=None, rope=None):
    B, L, _ = x.shape
    mod = (jnp.dot(jax.nn.silu(cond), lw["w_mod"]) + lw["b_mod"]).reshape(-1, 1, 6 * D_MODEL)
    sh1, sc1, gt1, sh2, sc2, gt2 = jnp.split(mod, 6, axis=-1)
    h = rmsnorm(x, lw["g_pre_mix"]) * (1.0 + sc1) + sh1
    (dq, dk, dv, dgate, dbeta, dalpha, sb, scg, sx, aq, ak, av, mg) = split_cols(jnp.dot(h, lw["w_in"]), IN_SIZES)

    qkv = jax.nn.silu(dwconv_centred(jnp.concatenate([dq, dk, dv], axis=-1), lw["w_conv_dn"]))
    q, k, v = split_cols(qkv, IN_SIZES[:3])
    q = l2norm(q.reshape(B, L, DN_HEADS, DN_DK))
    k = l2norm(k.reshape(B, L, DN_HEADS, DN_DK))
    v = v.reshape(B, L, DN_HEADS, DN_DV)
    beta = jax.nn.sigmoid(dbeta.reshape(B, L, 2, DN_HEADS))
    g = -jnp.exp(lw["a_log"]) * jax.nn.softplus(dalpha.reshape(B, L, 2, DN_HEADS) + lw["dt_bias"])
    if ctx is None:
        s0 = jnp.zeros((B, 2, DN_HEADS, DN_DK, DN_DV), x.dtype)
    else:
        s0 = ctx[2]
    o_f, s_f = gated_delta_rule(q, k, v, g[:, :, 0], beta[:, :, 0], s0[:, 0])
    rev = lambda t: jnp.flip(t, axis=1)
    o_b, s_b = gated_delta_rule(rev(q), rev(k), rev(v), rev(g[:, :, 1]), rev(beta[:, :, 1]), s0[:, 1])
    o_dn = rmsnorm(o_f + rev(o_b), lw["g_dn_out"]) * jax.nn.silu(dgate.reshape(B, L, DN_HEADS, DN_DV))
    y_a = jnp.dot(o_dn.reshape(B, L, DN_HEADS * DN_DV), lw["w_br_dn"])

    y_b = jnp.dot(sb * dwconv_centred(scg * sx, lw["w_conv_sc"]), lw["w_br_sc"])

    qa = rmsnorm(aq.reshape(B, L, ATT_HEADS, HEAD_DIM), lw["g_q"])
    ka = rmsnorm(ak.reshape(B, L, ATT_KV_HEADS, HEAD_DIM), lw["g_k"])
    va = av.reshape(B, L, ATT_KV_HEADS, HEAD_DIM)
    if ctx is None:
        o_att = block_attention(qa, ka, va)
    else:
        ang_row, ang_col = rope
        keys = jnp.concatenate([ctx[0], rope2d(ka, ang_row, ang_col)], axis=1)
        vals = jnp.concatenate([ctx[1], va], axis=1)
        o_att = block_attention(rope2d(qa, ang_row, ang_col), keys, vals)
    y_c = jnp.dot(o_att, lw["w_br_att"])

    g_a, g_b, g_c = jnp.split(jax.nn.sigmoid(mg), 3, axis=-1)
    mix = jnp.dot(g_a * y_a + g_b * y_b + g_c * y_c, lw["w_out"])
    x = x + gt1 * rmsnorm(mix, lw["g_post_mix"])

    h2 = rmsnorm(x, lw["g_pre_mlp"]) * (1.0 + sc2) + sh2
    u = jax.nn.relu(jnp.dot(h2, lw["w_mlp_up"]))
    x = x + gt2 * rmsnorm(jnp.dot(u * u, lw["w_mlp_down"]), lw["g_post_mlp"])
    if ctx is None:
        return x, (ka, va, jnp.stack([s_f, s_b], axis=1))
    return x


def setup_inputs(seed: int = 0) -> dict:
    key = jax.random.key(seed)
    ks = jax.random.split(key, 32)
    f32 = jnp.float32

    def nrm(k, shape, scale):
        return scale * jax.random.normal(k, shape, f32)

    def gain(k, shape):
        return 1.0 + 0.05 * jax.random.normal(k, shape, f32)

    dt = jnp.exp(jax.random.uniform(ks[16], (DEPTH, 2, DN_HEADS), f32, math.log(1e-3), math.log(1e-1)))
    dt_bias = dt + jnp.log(-jnp.expm1(-dt))
    a_log = jnp.log(jax.random.uniform(ks[17], (DEPTH, 2, DN_HEADS), f32, 1.0, 16.0))
    return {
        "x_prompt": nrm(ks[0], (BATCH, SEQ, D_MODEL), 1.0),
        "x_sample": nrm(ks[1], (DEC_BATCH, DEC_SEQ, D_MODEL), 1.0),
        "cache_k": nrm(ks[2], (DEC_BATCH, DEPTH, PAST_LEN, ATT_KV_HEADS, HEAD_DIM), 1.0),
        "cache_v": nrm(ks[3], (DEC_BATCH, DEPTH, PAST_LEN, ATT_KV_HEADS, HEAD_DIM), 1.0),
        "state_dn": nrm(ks[4], (DEC_BATCH, DEPTH, 2, DN_HEADS, DN_DK, DN_DV), 0.1),
        "c": nrm(ks[5], (DEC_BATCH, D_MODEL), 1.0),
        "c_ctx": nrm(ks[6], (D_MODEL,), 1.0),
        "w_mod": nrm(ks[7], (DEPTH, D_MODEL, 6 * D_MODEL), 0.5 * D_MODEL ** -0.5),
        "b_mod": nrm(ks[8], (DEPTH, 6 * D_MODEL), 0.02),
        "g_pre_mix": gain(ks[9], (DEPTH, D_MODEL)),
        "g_post_mix": gain(ks[10], (DEPTH, D_MODEL)),
        "g_pre_mlp": gain(ks[11], (DEPTH, D_MODEL)),
        "g_post_mlp": gain(ks[12], (DEPTH, D_MODEL)),
        "w_in": nrm(ks[13], (DEPTH, D_MODEL, IN_TOTAL), D_MODEL ** -0.5),
        "w_conv_dn": nrm(ks[14], (DEPTH, CONV_K, 2 * DN_HEADS * DN_DK + DN_HEADS * DN_DV), CONV_K ** -0.5),
        "a_log": a_log,
        "dt_bias": dt_bias,
        "g_dn_out": gain(ks[15], (DEPTH, DN_DV)),
        "w_conv_sc": nrm(ks[18], (DEPTH, CONV_K, SC_WIDTH), CONV_K ** -0.5),
        "g_q": gain(ks[19], (DEPTH, HEAD_DIM)),
        "g_k": gain(ks[20], (DEPTH, HEAD_DIM)),
        "w_br_dn": nrm(ks[21], (DEPTH, DN_HEADS * DN_DV, D_MODEL), (DN_HEADS * DN_DV) ** -0.5),
        "w_br_sc": nrm(ks[22], (DEPTH, SC_WIDTH, D_MODEL), SC_WIDTH ** -0.5),
        "w_br_att": nrm(ks[23], (DEPTH, ATT_HEADS * HEAD_DIM, D_MODEL), (ATT_HEADS * HEAD_DIM) ** -0.5),
        "w_out": nrm(ks[24], (DEPTH, D_MODEL, D_MODEL), D_MODEL ** -0.5),
        "w_mlp_up": nrm(ks[25], (DEPTH, D_MODEL, MLP_HIDDEN), D_MODEL ** -0.5),
        "w_mlp_down": nrm(ks[26], (DEPTH, MLP_HIDDEN, D_MODEL), MLP_HIDDEN ** -0.5),
    }


def reference(x_prompt, x_sample, cache_k, cache_v, state_dn, c, c_ctx, w_mod, b_mod, g_pre_mix, g_post_mix, g_pre_mlp, g_post_mlp, w_in, w_conv_dn, a_log, dt_bias, g_dn_out, w_conv_sc, g_q, g_k, w_br_dn, w_br_sc, w_br_att, w_out, w_mlp_up, w_mlp_down):
    def layer_weights(l):
        return {"w_mod": w_mod[l], "b_mod": b_mod[l], "g_pre_mix": g_pre_mix[l], "g_post_mix": g_post_mix[l], "g_pre_mlp": g_pre_mlp[l], "g_post_mlp": g_post_mlp[l], "w_in": w_in[l], "w_conv_dn": w_conv_dn[l], "a_log": a_log[l], "dt_bias": dt_bias[l], "g_dn_out": g_dn_out[l], "w_conv_sc": w_conv_sc[l], "g_q": g_q[l], "g_k": g_k[l], "w_br_dn": w_br_dn[l], "w_br_sc": w_br_sc[l], "w_br_att": w_br_att[l], "w_out": w_out[l], "w_mlp_up": w_mlp_up[l], "w_mlp_down": w_mlp_down[l]}

    y_prompt = x_prompt
    ks, vs, ss = [], [], []
    for l in range(DEPTH):
        y_prompt, (k_l, v_l, s_l) = trunk_layer(y_prompt, c_ctx, layer_weights(l))
        ks.append(k_l)
        vs.append(v_l)
        ss.append(s_l)
    new_cache_k = jnp.stack(ks, axis=1)
    new_cache_v = jnp.stack(vs, axis=1)
    new_state_dn = jnp.stack(ss, axis=1)

    ang_row, ang_col = axial_angles(x_sample.shape[1])
    y_sample = x_sample
    for l in range(DEPTH):
        y_sample = trunk_layer(y_sample, c, layer_weights(l), ctx=(cache_k[:, l], cache_v[:, l], state_dn[:, l]), rope=(ang_row, ang_col))

    return (y_prompt, y_sample, new_cache_k, new_cache_v, new_state_dn)
```

```python
import math
from contextlib import ExitStack

import numpy as np
import concourse.bass as bass
import concourse.mybir as mybir
from concourse.bass_utils import run_bass_kernel_spmd

F32 = mybir.dt.float32
F32R = mybir.dt.float32r
AF = mybir.ActivationFunctionType
ALU = mybir.AluOpType

D = 1024
T = 1024
NCK = 8
DEPTH = 4
NSEG = 4
SEG = 256
HD = 128
PAST = 256
NKEY = PAST + T
NKT = NKEY // 128
EPS = 1e-6
NEG = -30000.0
BLK = 256
N_CORES = 8
NW = 2
import os as _os
DN_STAGE = int(_os.environ.get("DN_STAGE", "9"))
DN_SUB = int(_os.environ.get("DN_SUB", "9"))
DN_STEPS = int(_os.environ.get("DN_STEPS", "5"))
DN_X = int(_os.environ.get("DN_X", "0"))
DEBUG_DUMPS = False
ENABLE_DN = True

IN_OFF = {}
_o = 0
for _n, _s in (("dq", 1024), ("dk", 1024), ("dv", 1024), ("dgate", 1024), ("dbeta", 16), ("dalpha", 16),
               ("sb", 1024), ("scg", 1024), ("sx", 1024), ("aq", 1024), ("ak", 256), ("av", 256), ("mg", 3072)):
    IN_OFF[_n] = _o
    _o += _s
IN_TOTAL = _o


def _cols(name, start, n):
    return list(range(IN_OFF[name] + start, IN_OFF[name] + start + n))


def layer_blocks(enable_dn=True):
    blocks = []
    for m in range(24):
        blocks.append(("mod", m))
    if enable_dn:
        blocks.append(("gates", None))
        for h in range(8):
            blocks.append(("dn_qk", h))
            blocks.append(("dn_vg", h))
        for j in range(4):
            blocks.append(("mg", (0, j)))
            blocks.append(("br_dn", j))
    for jp in range(4):
        blocks.append(("sc_x", jp))
        blocks.append(("sc_bg", 2 * jp))
        blocks.append(("sc_bg", 2 * jp + 1))
    for j in range(4):
        blocks.append(("mg", (1, j)))
        blocks.append(("br_sc", j))
    blocks.append(("ak", None))
    blocks.append(("av", None))
    for hp in range(4):
        blocks.append(("aq", hp))
    for j in range(4):
        blocks.append(("mg", (2, j)))
        blocks.append(("br_att", j))
    for j in range(4):
        blocks.append(("out", j))
    for q in range(4):
        for j in range(4):
            blocks.append(("up", (q, j)))
        for j in range(4):
            blocks.append(("down", (q, j)))
    return blocks


def build_block(kind, arg, lw):
    w_in = lw["w_in"]
    if kind == "mod":
        return lw["w_mod"][:, arg * 256:(arg + 1) * 256]
    if kind == "gates":
        blk = np.zeros((1024, 256), np.float32)
        b0, a0 = IN_OFF["dbeta"], IN_OFF["dalpha"]
        blk[:, 0:16] = w_in[:, b0:b0 + 16]
        blk[:, 16:32] = w_in[:, a0:a0 + 16]
        return blk
    if kind == "dn_qk":
        return w_in[:, _cols("dq", arg * 128, 128) + _cols("dk", arg * 128, 128)]
    if kind == "dn_vg":
        return w_in[:, _cols("dv", arg * 128, 128) + _cols("dgate", arg * 128, 128)]
    if kind == "mg":
        br, j = arg
        return w_in[:, _cols("mg", br * 1024 + j * 256, 256)]
    if kind == "br_dn":
        return lw["w_br_dn"][:, arg * 256:(arg + 1) * 256]
    if kind == "br_sc":
        return lw["w_br_sc"][:, arg * 256:(arg + 1) * 256]
    if kind == "br_att":
        return lw["w_br_att"][:, arg * 256:(arg + 1) * 256]
    if kind == "sc_bg":
        return w_in[:, _cols("sb", arg * 128, 128) + _cols("scg", arg * 128, 128)]
    if kind == "sc_x":
        return w_in[:, _cols("sx", arg * 256, 256)]
    if kind == "ak":
        return w_in[:, _cols("ak", 0, 256)]
    if kind == "av":
        return w_in[:, _cols("av", 0, 256)]
    if kind == "aq":
        return w_in[:, _cols("aq", arg * 256, 256)]
    if kind == "out":
        return lw["w_out"][:, arg * 256:(arg + 1) * 256]
    if kind == "up":
        q, j = arg
        return lw["w_mlp_up"][:, q * 1024 + j * 256: q * 1024 + (j + 1) * 256]
    if kind == "down":
        q, j = arg
        return lw["w_mlp_down"][q * 1024:(q + 1) * 1024, j * 256:(j + 1) * 256]
    raise ValueError(kind)


class VecLayout:
    def __init__(self):
        self.off = {}
        self.n = 0

    def add(self, name, width):
        self.off[name] = (self.n, width)
        self.n += width

    def sl(self, name):
        o, w = self.off[name]
        return slice(o, o + w)


def make_vec_layout(depth):
    V = VecLayout()
    V.add("cond", 8)
    V.add("fl", 1)
    V.add("nb", 1)
    V.add("nfl", 1)
    for l in range(depth):
        V.add(f"bmod{l}", 48)
        for g in ("g_pre_mix", "g_post_mix", "g_pre_mlp", "g_post_mlp"):
            V.add(f"{g}{l}", 8)
        V.add(f"wc_dn{l}", 72)
        V.add(f"wc_sc{l}", 24)
        V.add(f"g_dn{l}", 1)
        V.add(f"g_q{l}", 1)
        V.add(f"g_k{l}", 1)
        V.add(f"alog{l}", 16)
        V.add(f"dtb{l}", 16)
    return V


class ConstLayout(VecLayout):
    pass


def make_const_layout():
    C = ConstLayout()
    for nm in ("ident", "ones", "rotT", "ni_f", "ni_b", "ns_f", "ns_b", "tri_f", "tri_b",
               "last_f", "last_b", "all_f0", "all_f1", "all_b0", "all_b1"):
        C.add(nm, 128)
    return C


def build_consts():
    C = make_const_layout()
    a = np.zeros((128, C.n), np.float32)
    a[:, C.sl("ident")] = np.eye(128, dtype=np.float32)
    a[:, C.sl("ones")] = 1.0
    R = np.zeros((128, 128), np.float32)
    for d in range(128):
        idx = d % 64
        if idx < 32:
            R[d, d + 32] = -1.0
        else:
            R[d, d - 32] = 1.0
    a[:, C.sl("rotT")] = R.T
    p = np.arange(128)[:, None]
    f = np.arange(128)[None, :]
    same = (p // 64) == (f // 64)
    a[:, C.sl("ni_f")] = np.where(same & (f >= p), 0.0, NEG)
    a[:, C.sl("ni_b")] = np.where(same & (f <= p), 0.0, NEG)
    a[:, C.sl("ns_f")] = np.where(same & (f < p), 0.0, NEG)
    a[:, C.sl("ns_b")] = np.where(same & (f > p), 0.0, NEG)
    a[:, C.sl("tri_f")] = np.where(same & (f >= p), 1.0, 0.0)
    a[:, C.sl("tri_b")] = np.where(same & (f <= p), 1.0, 0.0)
    cend = (f // 64) * 64 + 63
    cstart = (f // 64) * 64
    a[:, C.sl("last_f")] = (p == cend).astype(np.float32)
    a[:, C.sl("last_b")] = (p == cstart).astype(np.float32)
    a[:, C.sl("all_f0")] = (p == 63).astype(np.float32) * np.ones((1, 128), np.float32)
    a[:, C.sl("all_f1")] = (p == 127).astype(np.float32) * np.ones((1, 128), np.float32)
    a[:, C.sl("all_b0")] = (p == 0).astype(np.float32) * np.ones((1, 128), np.float32)
    a[:, C.sl("all_b1")] = (p == 64).astype(np.float32) * np.ones((1, 128), np.float32)
    return a


def I(method, **kw):
    return lambda e: getattr(e, method)(**kw)


class WB:
    def __init__(self, b0, b1):
        self.bufs = [b0, b1]


def _flat(lst):
    out = []
    for b in lst:
        if isinstance(b, WB):
            out.extend(b.bufs)
        else:
            out.append(b)
    return out


class Buf:
    __slots__ = ("name", "writers", "readers", "prev", "sem", "dma_n")

    def __init__(self, name):
        self.name = name
        self.writers = []
        self.readers = []
        self.prev = []
        self.sem = None
        self.dma_n = 0


class Op:
    __slots__ = ("eng", "fn", "waits", "milestone", "count", "dma_sem", "is_dma", "epoch")

    def __init__(self, eng, fn):
        self.eng = eng
        self.fn = fn
        self.epoch = 0
        self.waits = []
        self.milestone = False
        self.count = 0
        self.dma_sem = None
        self.is_dma = False


class Prog:
    ENGS = ("pe", "act", "dve", "pool", "sp")

    def __init__(self, nc, stack):
        self.nc = nc
        self.stack = stack
        self.ops = {e: [] for e in self.ENGS}
        self.sems = {}
        self.epoch = 0
        self.out_tokens = []

    def esem(self, eng, epoch):
        k = (eng, epoch)
        if k not in self.sems:
            self.sems[k] = self.stack.enter_context(self.nc.semaphore(f"s_{eng}_{epoch}"))
        return self.sems[k]

    def buf(self, name):
        return Buf(name)

    def _dma_sem(self, b):
        if b.sem is None:
            b.sem = self.stack.enter_context(self.nc.semaphore(f"d_{b.name}"))
        return b.sem

    def _add(self, eng, fn, r, w, dma_buf=None):
        op = Op(eng, fn)
        op.epoch = self.epoch
        r = _flat(r)
        w = _flat(w)
        waits = []
        for b in r:
            waits.extend(b.writers)
        for b in w:
            if b.readers:
                b.prev = b.writers + b.readers
                b.writers = []
                b.readers = []
            waits.extend(b.prev)
        seen = set()
        for tok in waits:
            if tok[0] == "e":
                o = tok[1]
                if o.eng == eng and eng in ("pe", "sp"):
                    continue
                if id(o) in seen:
                    continue
                seen.add(id(o))
                o.milestone = True
                op.waits.append(tok)
            else:
                key = (id(tok[1]), tok[2])
                if key in seen:
                    continue
                seen.add(key)
                op.waits.append(tok)
        if dma_buf is not None:
            op.is_dma = True
            op.dma_sem = self._dma_sem(dma_buf)
            dma_buf.dma_n += 1
            tok = ("d", op.dma_sem, 16 * dma_buf.dma_n)
        else:
            tok = ("e", op)
        for b in r:
            b.readers.append(tok)
        for b in w:
            b.writers.append(tok)
        self.ops[eng].append(op)
        return tok

    def pe(self, fn, r=(), w=()):
        return self._add("pe", fn, r, w)

    def act(self, fn, r=(), w=()):
        return self._add("act", fn, r, w)

    def dve(self, fn, r=(), w=()):
        return self._add("dve", fn, r, w)

    def pool(self, fn, r=(), w=()):
        return self._add("pool", fn, r, w)

    def dma(self, eng, out, in_, r=(), w=(), sem_buf=None, is_output=False):
        sb = sem_buf if sem_buf is not None else (w[0] if w else r[0])
        tok = self._add(eng, lambda e: e.dma_start(out=out, in_=in_), r, w, dma_buf=sb)
        if is_output:
            self.out_tokens.append(tok)
        return tok

    def emit(self, block):
        for e in self.ENGS:
            cnt = {}
            for op in self.ops[e]:
                if op.milestone:
                    cnt[op.epoch] = cnt.get(op.epoch, 0) + 1
                    op.count = cnt[op.epoch]
                    self.esem(e, op.epoch)
        final = {}
        for tok in self.out_tokens:
            k = id(tok[1])
            if k not in final or final[k][2] < tok[2]:
                final[k] = tok
        finals = list(final.values())
        handles = {"pe": block.tensor, "act": block.scalar, "dve": block.vector, "pool": block.gpsimd,
                   "sp": block.sync}
        for e in self.ENGS:
            ops = self.ops[e]
            sems = self.sems

            def body(eng, ops=ops, e=e):
                seen = {}
                for op in ops:
                    need = {}
                    for tok in op.waits:
                        if tok[0] == "e":
                            o = tok[1]
                            s, v = sems[(o.eng, o.epoch)], o.count
                        else:
                            s, v = tok[1], tok[2]
                        k = id(s)
                        if k not in need or need[k][1] < v:
                            need[k] = (s, v)
                    for k, (s, v) in need.items():
                        if seen.get(k, 0) >= v:
                            continue
                        seen[k] = v
                        eng.wait_ge(s, v)
                    ins = op.fn(eng)
                    if op.is_dma:
                        ins.then_inc(op.dma_sem, 16)
                    elif op.milestone:
                        ins.then_inc(sems[(e, op.epoch)], 1)
                if e == "sp":
                    for tok in finals:
                        eng.wait_ge(tok[1], tok[2])

            handles[e](body)


def build_program(depth=DEPTH, enable_dn=True, enable_sc=True, enable_att=True):
    nc = bass.Bass("TRN2", target_bir_lowering=False)
    nc.dge_precook = False
    V = make_vec_layout(depth)
    C = make_const_layout()
    blocks_per_layer = layer_blocks(enable_dn)
    NB = len(blocks_per_layer) * depth

    d_x = nc.dram_tensor("xT", [128, NCK, T], F32, kind="ExternalInput").ap()
    d_vecs = nc.dram_tensor("vecs", [128, V.n], F32, kind="ExternalInput").ap()
    d_consts = nc.dram_tensor("consts", [128, C.n], F32, kind="ExternalInput").ap()
    d_ws = nc.dram_tensor("ws", [NB, 2, 128, NCK * 128], F32, kind="ExternalInput").ap()
    d_rope = nc.dram_tensor("rope", [128, 2, T], F32, kind="ExternalInput").ap()
    d_ckT = nc.dram_tensor("ckT", [128, depth, 2, PAST], F32, kind="ExternalInput").ap()
    d_cv = nc.dram_tensor("cv", [128, depth, 2, 256], F32, kind="ExternalInput").ap()
    d_s0 = nc.dram_tensor("s0", [128, depth, 16, 128], F32, kind="ExternalInput").ap()
    d_y = nc.dram_tensor("yT", [128, NCK, T], F32, kind="ExternalOutput").ap()
    d_kout = nc.dram_tensor("kT_out", [128, depth, 2, T], F32, kind="ExternalOutput").ap()
    d_vout = nc.dram_tensor("v_out", [128, depth, 8, 256], F32, kind="ExternalOutput").ap()
    d_st = nc.dram_tensor("st_out", [128, depth, 16, NSEG, 128], F32, kind="ExternalOutput").ap()
    if DEBUG_DUMPS:
        d_dbgA = nc.dram_tensor("dbgA", [128, NCK, T], F32, kind="ExternalOutput").ap()
        d_dbgM = nc.dram_tensor("dbgM", [128, 64], F32, kind="ExternalOutput").ap()

    with ExitStack() as stack:
        P = Prog(nc, stack)

        def sb(name, shape, dt=F32):
            return stack.enter_context(nc.sbuf_tensor(name, shape, dt))

        X = sb("X", [128, NCK, T])
        A = sb("A", [128, NCK, T])
        B = sb("B", [128, NCK, T])
        Cc = sb("C", [128, NCK, T])
        W = sb("W", [128, NW, NCK * BLK])
        VEC = sb("VEC", [128, V.n])
        CON = sb("CON", [128, C.n])
        SCR = sb("SCR", [128, 8, T])
        TMP = sb("TMP", [128, 4, 512])
        SQ = sb("SQ", [128, 2, 512])
        SMALL = sb("SMALL", [128, 256])
        DNS = sb("DNS", [128, 2008])
        _o = [0]

        def carve(n):
            ap_ = DNS[:, _o[0]:_o[0] + n]
            _o[0] += n
            return ap_
        GRAW = carve(256).rearrange("p (t n) -> p t n", t=8)
        BETA = carve(128).rearrange("p (t n) -> p t n", t=8)
        GC = carve(128).rearrange("p (t n) -> p t n", t=8)
        BGC = carve(128).rearrange("p (t n) -> p t n", t=8)
        EKD = carve(128).rearrange("p (t n) -> p t n", t=8)
        ZG = carve(128).rearrange("p (t n) -> p t n", t=8)
        GLB = carve(256).rearrange("p (c n) -> p c n", c=16)
        WCN = carve(72)
        NA = carve(16)
        DIAG = carve(256).rearrange("p (a n) -> p a n", a=2)
        SST = carve(512).rearrange("p (d a n) -> p d a n", d=2, a=2)
        b_GRAW, b_BETA, b_GC, b_BGC, b_EKD, b_ZG, b_GLB, b_WCN, b_NA = [P.buf(n_) for n_ in
            ("GRAW", "BETA", "GC", "BGC", "EKD", "ZG", "GLB", "WCN", "NA")]
        b_DIAG = [P.buf("DIAG0"), P.buf("DIAG1")]
        b_SST = [[P.buf("S00"), P.buf("S01")], [P.buf("S10"), P.buf("S11")]]
        PS = stack.enter_context(nc.psum_tensor("PS", [128, 8, 512], F32))

        bX = [P.buf(f"X{c}") for c in range(NCK)]
        bA = [P.buf(f"A{c}") for c in range(NCK)]
        bB = [P.buf(f"B{c}") for c in range(NCK)]
        bC = [P.buf(f"C{c}") for c in range(NCK)]
        bW = [WB(P.buf(f"W{i}a"), P.buf(f"W{i}b")) for i in range(NW)]
        bVEC = P.buf("VEC")
        bCON = P.buf("CON")
        bSCR = [P.buf(f"SCR{i}") for i in range(8)]
        bTMP = [P.buf(f"TMP{i}") for i in range(4)]
        bSQ = [P.buf(f"SQ{i}") for i in range(2)]
        bSM = {}
        bPS = [P.buf(f"PS{i}") for i in range(8)]
        rr = {"ps": 0, "tmp": 0, "sq": 0, "w": 0, "el": 0}

        def bank():
            i = rr["ps"] % 6
            rr["ps"] += 1
            return PS[:, i, :], bPS[i]

        def held_bank(i):
            return PS[:, 6 + i, :], bPS[6 + i]

        def tmp():
            i = rr["tmp"] % 4
            rr["tmp"] += 1
            return TMP[:, i, :], bTMP[i]

        def sq():
            i = rr["sq"] % 2
            rr["sq"] += 1
            return SQ[:, i, :], bSQ[i]

        def small(name, width):
            if name not in bSM:
                o = sum(w for (_, w, _) in bSM.values())
                assert o + width <= 256
                bSM[name] = (o, width, P.buf("sm_" + name))
            o, w, b = bSM[name]
            return SMALL[:, o:o + w], b

        def cst(name):
            return CON[:, C.sl(name)]

        def vec(name):
            return VEC[:, V.sl(name)]

        wstate = {"i": 0}

        def next_block(expect_kind=None, expect_arg=None, avoid=None):
            i = wstate["i"]
            kind, arg = blocks_per_layer[i % len(blocks_per_layer)]
            if expect_kind is not None:
                assert kind == expect_kind and arg == expect_arg, (kind, arg, expect_kind, expect_arg)
            s = (wstate.get("last", -1) + 1) % NW
            if avoid is not None and s == avoid:
                s = (s + 1) % NW
            wstate["last"] = s
            wstate["slot"] = s
            wv3 = W[:, s, :].rearrange("p (k n) -> p k n", k=NCK)
            for c_ in range(2):
                P.dma("sp", wv3[:, :, c_ * 128:(c_ + 1) * 128].bitcast(F32R),
                      d_ws[i, c_].rearrange("p (k n) -> p k n", k=NCK).bitcast(F32R), w=[bW[s].bufs[c_]])
            wstate["i"] = i + 1
            return W[:, s, :].rearrange("p (k n) -> p k n", k=NCK), bW[s]

        def ew():
            rr["el"] += 1
            return P.dve if rr["el"] % 2 else P.pool

        P.dma("pool", VEC[:], d_vecs, w=[bVEC])
        P.dma("pool", CON[:].bitcast(F32R), d_consts.bitcast(F32R), w=[bCON])
        for c in range(NCK):
            P.dma("act", X[:, c, :], d_x[:, c, :], w=[bX[c]])

        ident = cst("ident")
        ones_r = cst("ones").bitcast(F32R)

        scond, b_scond = small("scond", 8)
        P.act(I("activation", out=scond, in_=vec("cond"), func=AF.Silu), r=[bVEC], w=[b_scond])

        def mm(out, lhsT, rhs, start, stop):
            return lambda e: e.matmul(out, lhsT, rhs, start=start, stop=stop)

        def proj(wblk, bw, col0, src, bsrc, th, nk=NCK):
            ps, bps = bank()
            for k in range(nk):
                P.pe(mm(ps, wblk[:, k, col0:col0 + 128].bitcast(F32R),
                        src[:, k, th * 512:(th + 1) * 512].bitcast(F32R), k == 0, k == nk - 1),
                     r=[bw.bufs[col0 // 128] if isinstance(bw, WB) else bw] + list(bsrc[:nk]), w=[bps])
            return ps, bps

        def rstd_of(src_fn, bsrcs, nch, th, scale, out_name):
            ss, bss = bank()
            for c in range(nch):
                s_ap, bs = sq()
                src = src_fn(c)
                P.act(I("activation", out=s_ap.bitcast(F32R), in_=src, func=AF.Square),
                      r=[bsrcs[c]], w=[bs])
                P.pe(mm(ss, ones_r, s_ap.bitcast(F32R), c == 0, c == nch - 1), r=[bs, bCON], w=[bss])
            sd, bsd = tmp()
            P.act(I("activation", out=sd, in_=ss, func=AF.Sqrt, bias=EPS, scale=scale), r=[bss], w=[bsd])
            rs, brs = bank()
            P.dve(I("reciprocal", out=rs, in_=sd), r=[bsd], w=[brs])
            return rs, brs

        for l in range(depth):
            P.epoch = l
            modps, bmodps = bank()
            for m2 in range(24):
                wb, bw = next_block("mod", m2)
                for cc in range(2):
                    m = m2 * 2 + cc
                    for k in range(NCK):
                        P.pe(mm(modps[:, m:m + 1], wb[:, k, cc * 128:(cc + 1) * 128], scond[:, k:k + 1],
                                k == 0, k == NCK - 1), r=[bw.bufs[cc], b_scond], w=[bmodps])
            modv, bmodv = small("modv", 48)
            P.dve(I("tensor_tensor", out=modv, in0=modps[:, 0:48], in1=vec(f"bmod{l}"), op=ALU.add),
                  r=[bmodps, bVEC], w=[bmodv])
            a1, ba1 = small("a1", 8)
            a2, ba2 = small("a2", 8)
            g1, bg1 = small("g1", 8)
            g2, bg2 = small("g2", 8)
            P.dve(I("scalar_tensor_tensor", out=a1, in0=modv[:, 8:16], scalar=1.0, in1=vec(f"g_pre_mix{l}"),
                                                        op0=ALU.add, op1=ALU.mult), r=[bmodv, bVEC], w=[ba1])
            P.dve(I("scalar_tensor_tensor", out=a2, in0=modv[:, 32:40], scalar=1.0, in1=vec(f"g_pre_mlp{l}"),
                                                        op0=ALU.add, op1=ALU.mult), r=[bmodv, bVEC], w=[ba2])
            P.dve(I("tensor_tensor", out=g1, in0=modv[:, 16:24], in1=vec(f"g_post_mix{l}"), op=ALU.mult),
                  r=[bmodv, bVEC], w=[bg1])
            P.dve(I("tensor_tensor", out=g2, in0=modv[:, 40:48], in1=vec(f"g_post_mlp{l}"), op=ALU.mult),
                  r=[bmodv, bVEC], w=[bg2])

            def pre_norm(scale_ap, bscale, shift_lo):
                for th in range(2):
                    hs = slice(th * 512, (th + 1) * 512)
                    rs, brs = rstd_of(lambda c: X[:, c, hs], bX, NCK, th, 1.0 / D, "pre")
                    for c in range(NCK):
                        t_ap, bt = tmp()
                        P.dve(I("tensor_tensor", out=t_ap, in0=X[:, c, hs], in1=rs, op=ALU.mult),
                              r=[bX[c], brs], w=[bt])
                        P.act(I("activation",
                            out=A[:, c, hs].bitcast(F32R), in_=t_ap, func=AF.Identity,
                            scale=scale_ap[:, c:c + 1], bias=modv[:, shift_lo + c:shift_lo + c + 1]),
                            r=[bt, bscale, bmodv], w=[bA[c]])

            def post_norm_add(SRC, bSRC, gate_ap, bgate):
                for th in range(2):
                    hs = slice(th * 512, (th + 1) * 512)
                    rs, brs = rstd_of(lambda c: SRC[:, c, hs], bSRC, NCK, th, 1.0 / D, "post")
                    for c in range(NCK):
                        t_ap, bt = tmp()
                        P.dve(I("tensor_tensor", out=t_ap, in0=SRC[:, c, hs], in1=rs, op=ALU.mult),
                              r=[bSRC[c], brs], w=[bt])
                        P.dve(I("scalar_tensor_tensor",
                            out=X[:, c, hs], in0=t_ap, scalar=gate_ap[:, c:c + 1], in1=X[:, c, hs],
                            op0=ALU.mult, op1=ALU.add), r=[bt, bgate, bX[c]], w=[bX[c]])

            pre_norm(a1, ba1, 0)
            if DEBUG_DUMPS and l == 0:
                for c in range(NCK):
                    P.dma("pool", d_dbgA[:, c, :], A[:, c, :], r=[bA[c]], sem_buf=bA[c], is_output=True)
                P.dma("pool", d_dbgM[:, 0:48], modv, r=[bmodv], sem_buf=bmodv, is_output=True)
                P.dma("pool", d_dbgM[:, 48:56], scond, r=[b_scond], sem_buf=b_scond, is_output=True)
                P.dma("pool", d_dbgM[:, 56:64], a1, r=[ba1], sem_buf=ba1, is_output=True)

            first_branch = {"v": True}

            def merge_branch(br, wname):
                first = first_branch["v"]
                first_branch["v"] = False
                for j in range(4):
                    wg, bwg = next_block("mg", (br, j))
                    wp, bwp = next_block(wname, j)
                    for cc in range(2):
                        c = 2 * j + cc
                        for th in range(2):
                            hs = slice(th * 512, (th + 1) * 512)
                            ps1, bps1 = proj(wg, bwg, cc * 128, A, bA, th)
                            g_ap, bg = tmp()
                            P.act(I("activation", out=g_ap, in_=ps1, func=AF.Sigmoid),
                                  r=[bps1], w=[bg])
                            ps, bps = proj(wp, bwp, cc * 128, B, bB, th)
                            if first:
                                P.dve(I("tensor_tensor",
                                    out=Cc[:, c, hs].bitcast(F32R), in0=ps, in1=g_ap, op=ALU.mult), r=[bps, bg], w=[bC[c]])
                            else:
                                t_ap, bt = tmp()
                                P.dve(I("tensor_tensor",
                                    out=t_ap, in0=ps, in1=g_ap, op=ALU.mult), r=[bps, bg], w=[bt])
                                P.pool(I("tensor_tensor",
                                    out=Cc[:, c, hs].bitcast(F32R), in0=Cc[:, c, hs], in1=t_ap, op=ALU.add),
                                    r=[bt, bC[c]], w=[bC[c]])

            if enable_dn:
                F = lambda ap: ap.bitcast(F32R)
                wgt, bwgt = next_block("gates", None)
                for tt in range(8):
                    ps, bps = bank()
                    for k in range(NCK):
                        P.pe(mm(ps[:, 0:256], A[:, k, tt * 128:(tt + 1) * 128].bitcast(F32R),
                                wgt[:, k, :].bitcast(F32R), k == 0, k == NCK - 1), r=[bwgt, bA[k]], w=[bps])
                    P.act(I("activation", out=GRAW[:, tt, :], in_=ps[:, 0:32], func=AF.Identity), r=[bps], w=[b_GRAW])
                P.act(I("activation", out=BETA, in_=GRAW[:, :, 0:16], func=AF.Sigmoid), r=[b_GRAW], w=[b_BETA])
                for tt in range(8):
                    P.dve(I("tensor_tensor", out=ZG[:, tt, :], in0=GRAW[:, tt, 16:32], in1=vec(f"dtb{l}"), op=ALU.add),
                          r=[b_GRAW, bVEC], w=[b_ZG])
                P.act(I("activation", out=ZG, in_=ZG, func=AF.Exp), r=[b_ZG], w=[b_ZG])
                P.act(I("activation", out=ZG, in_=ZG, func=AF.Ln, bias=1.0), r=[b_ZG], w=[b_ZG])
                P.act(I("activation", out=NA, in_=vec(f"alog{l}"), func=AF.Exp), r=[bVEC], w=[b_NA])
                P.dve(I("tensor_scalar", out=NA, in0=NA, scalar1=-1.0, scalar2=None, op0=ALU.mult), r=[b_NA], w=[b_NA])
                for tt in range(8):
                    P.dve(I("tensor_tensor", out=ZG[:, tt, :], in0=ZG[:, tt, :], in1=NA, op=ALU.mult),
                          r=[b_ZG, b_NA], w=[b_ZG])
                ps, bps = bank()
                psv = ps[:, 0:256].rearrange("p (t d n) -> p t d n", t=8, d=2)
                for tt in range(8):
                    P.pe(mm(psv[:, tt, 0, :], cst("tri_f"), ZG[:, tt, :], True, True), r=[bCON, b_ZG], w=[bps])
                    P.pe(mm(psv[:, tt, 1, :], cst("tri_b"), ZG[:, tt, :], True, True), r=[bCON, b_ZG], w=[bps])
                P.act(I("activation", out=GC[:, :, 0:8], in_=psv[:, :, 0, 0:8], func=AF.Identity), r=[bps], w=[b_GC])
                P.act(I("activation", out=GC[:, :, 8:16], in_=psv[:, :, 1, 8:16], func=AF.Identity), r=[bps], w=[b_GC])
                ps, bps = bank()
                psv = ps[:, 0:256].rearrange("p (t d n) -> p t d n", t=8, d=2)
                for tt in range(8):
                    P.pe(mm(psv[:, tt, 0, :], cst("last_f"), GC[:, tt, :], True, True), r=[bCON, b_GC], w=[bps])
                    P.pe(mm(psv[:, tt, 1, :], cst("last_b"), GC[:, tt, :], True, True), r=[bCON, b_GC], w=[bps])
                P.dve(I("tensor_tensor", out=EKD[:, :, 0:8], in0=psv[:, :, 0, 0:8], in1=GC[:, :, 0:8], op=ALU.subtract),
                      r=[bps, b_GC], w=[b_EKD])
                P.dve(I("tensor_tensor", out=EKD[:, :, 8:16], in0=psv[:, :, 1, 8:16], in1=GC[:, :, 8:16], op=ALU.subtract),
                      r=[bps, b_GC], w=[b_EKD])
                P.act(I("activation", out=EKD, in_=EKD, func=AF.Exp), r=[b_EKD], w=[b_EKD])
                ps, bps = bank()
                psv = ps[:, 0:512].rearrange("p (c d n) -> p c d n", c=16, d=2)
                for tt in range(8):
                    for cc in range(2):
                        P.pe(mm(psv[:, 2 * tt + cc, 0, :], cst(f"all_f{cc}"), GC[:, tt, :], True, True),
                             r=[bCON, b_GC], w=[bps])
                        P.pe(mm(psv[:, 2 * tt + cc, 1, :], cst(f"all_b{cc}"), GC[:, tt, :], True, True),
                             r=[bCON, b_GC], w=[bps])
                P.act(I("activation", out=GLB[:, :, 0:8], in_=psv[:, :, 0, 0:8], func=AF.Exp), r=[bps], w=[b_GLB])
                P.act(I("activation", out=GLB[:, :, 8:16], in_=psv[:, :, 1, 8:16], func=AF.Exp), r=[bps], w=[b_GLB])
                P.act(I("activation", out=BGC, in_=GC, func=AF.Exp), r=[b_GC], w=[b_BGC])
                P.dve(I("tensor_tensor", out=BGC, in0=BGC, in1=BETA, op=ALU.mult), r=[b_BGC, b_BETA], w=[b_BGC])
                P.dve(I("tensor_scalar", out=WCN, in0=vec(f"wc_dn{l}"), scalar1=vec("nfl"), scalar2=None, op0=ALU.mult),
                      r=[bVEC], w=[b_WCN])
                wcd = vec(f"wc_dn{l}")

                def slot(i):
                    return (Cc[:, i, :], bC[i]) if i < 8 else (SCR[:, i - 8, :], bSCR[i - 8])

                RAW, b_RAW = slot(0)
                ACC, b_ACC = slot(1)
                QT, b_QT = slot(2)
                KT, b_KT = slot(3)
                KTOK, b_KTOK = slot(4)
                VTOK, b_VTOK = slot(5)
                QKTM, b_QKTM = slot(6)
                UU, b_UU = slot(7)
                WT, b_WT = slot(8)
                KDEC, b_KDEC = slot(9)
                QDT, b_QDT = slot(10)
                OT, b_OT = slot(11)
                SIL, b_SIL = slot(12)
                t3 = lambda ap: ap.rearrange("p (t n) -> p t n", t=8)
                KTOK3, VTOK3, QKTM3, UU3, KDEC3 = t3(KTOK), t3(VTOK), t3(QKTM), t3(UU), t3(KDEC)
                ring = []
                for si in (13, 14, 15):
                    ap_, _b = slot(si)
                    for q_ in range(8):
                        ring.append((ap_[:, q_ * 128:(q_ + 1) * 128], P.buf(f"ut{l}_{si}_{q_}"), _b))
                rstate = {"i": 0}

                def ut():
                    i = rstate["i"] % len(ring)
                    rstate["i"] += 1
                    return ring[i][0], ring[i][1]

                for si in (13, 14, 15):
                    ap_, _b = slot(si)
                    for q_ in range(8):
                        rb = ring[(si - 13) * 8 + q_][1]
                        rb.prev = list(_b.writers) + list(_b.readers)
                        rb.writers = []
                        rb.readers = []

                def conv_silu(ps_list, chunk_idx, dst, bdst, do_conv=True):
                    if not do_conv:
                        for th, (ps, bps) in enumerate(ps_list):
                            hs = slice(th * 512, (th + 1) * 512)
                            P.act(I("activation", out=F(dst[:, hs]), in_=ps, func=AF.Silu), r=[bps], w=[bdst])
                        return
                    for th, (ps, bps) in enumerate(ps_list):
                        hs = slice(th * 512, (th + 1) * 512)
                        P.act(I("activation", out=F(RAW[:, hs]), in_=ps, func=AF.Identity), r=[bps], w=[b_RAW])
                    w0 = wcd[:, chunk_idx:chunk_idx + 1]
                    w1 = wcd[:, 24 + chunk_idx:25 + chunk_idx]
                    w2 = wcd[:, 48 + chunk_idx:49 + chunk_idx]
                    P.dve(I("tensor_scalar", out=F(ACC), in0=RAW, scalar1=w1, scalar2=None, op0=ALU.mult),
                          r=[b_RAW, bVEC], w=[b_ACC])
                    P.dve(I("scalar_tensor_tensor", out=F(ACC[:, 1:T]), in0=RAW[:, 0:T - 1], scalar=w0, in1=ACC[:, 1:T],
                            op0=ALU.mult, op1=ALU.add), r=[b_RAW, bVEC, b_ACC], w=[b_ACC])
                    P.dve(I("scalar_tensor_tensor", out=F(ACC[:, 0:T - 1]), in0=RAW[:, 1:T], scalar=w2, in1=ACC[:, 0:T - 1],
                            op0=ALU.mult, op1=ALU.add), r=[b_RAW, bVEC, b_ACC], w=[b_ACC])
                    r4 = RAW.rearrange("p (s w) -> p s w", w=SEG)
                    a4 = ACC.rearrange("p (s w) -> p s w", w=SEG)
                    P.dve(I("scalar_tensor_tensor", out=F(a4[:, 1:4, 0:1]), in0=r4[:, 0:3, SEG - 1:SEG],
                            scalar=WCN[:, chunk_idx:chunk_idx + 1], in1=a4[:, 1:4, 0:1], op0=ALU.mult, op1=ALU.add),
                          r=[b_RAW, b_WCN, b_ACC], w=[b_ACC])
                    P.dve(I("scalar_tensor_tensor", out=F(a4[:, 0:3, SEG - 1:SEG]), in0=r4[:, 1:4, 0:1],
                            scalar=WCN[:, 48 + chunk_idx:49 + chunk_idx], in1=a4[:, 0:3, SEG - 1:SEG],
                            op0=ALU.mult, op1=ALU.add), r=[b_RAW, b_WCN, b_ACC], w=[b_ACC])
                    P.act(I("activation", out=F(dst), in_=ACC, func=AF.Silu), r=[b_ACC], w=[bdst])

                def l2n(src, bsrc, dst, bdst, mul):
                    for th in range(2):
                        hs = slice(th * 512, (th + 1) * 512)
                        rs, brs = rstd_of(lambda c: src[:, hs], [bsrc], 1, th, 1.0, "l2")
                        P.dve(I("scalar_tensor_tensor", out=F(dst[:, hs]), in0=src[:, hs], scalar=mul, in1=rs,
                                op0=ALU.mult, op1=ALU.mult), r=[bsrc, brs], w=[bdst])

                def to_tok(src, bsrc, dst3, bdst):
                    for g_ in range(2):
                        ps, bps = bank()
                        for q_ in range(4):
                            tt = 4 * g_ + q_
                            P.pe(I("transpose", out=ps[:, q_ * 128:(q_ + 1) * 128], in_=src[:, tt * 128:(tt + 1) * 128],
                                   identity=ident), r=[bsrc, bCON], w=[bps])
                        P.act(I("activation", out=F(dst3[:, 4 * g_:4 * g_ + 4, :]),
                                in_=ps.rearrange("p (t n) -> p t n", t=4), func=AF.Identity), r=[bps], w=[bdst])

                for h in range(8):
                    wqk, bwqk = next_block("dn_qk", h)
                    wvg, bwvg = next_block("dn_vg", h)
                    if DN_STAGE < 2:
                        continue
                    pl = [proj(wqk, bwqk, 0, A, bA, th) for th in range(2)]
                    conv_silu(pl, h, SIL, b_SIL)
                    l2n(SIL, b_SIL, QT, b_QT, 1.0 / math.sqrt(128.0))
                    pl = [proj(wqk, bwqk, 128, A, bA, th) for th in range(2)]
                    conv_silu(pl, 8 + h, SIL, b_SIL)
                    l2n(SIL, b_SIL, KT, b_KT, 1.0)
                    to_tok(KT, b_KT, KTOK3, b_KTOK)
                    pl = [proj(wvg, bwvg, 0, A, bA, th) for th in range(2)]
                    conv_silu(pl, 16 + h, SIL, b_SIL)
                    to_tok(SIL, b_SIL, VTOK3, b_VTOK)
                    pl = [proj(wvg, bwvg, 128, A, bA, th) for th in range(2)]
                    conv_silu(pl, 0, B[:, h, :], bB[h], do_conv=False)

                    for d in range(2 if DN_STAGE >= 3 else 0):
                        dh = d * 8 + h
                        ni = cst("ni_f" if d == 0 else "ni_b")
                        ns = cst("ns_f" if d == 0 else "ns_b")
                        nsT = cst("ns_b" if d == 0 else "ns_f")
                        for tt in range(8):
                            cs = slice(tt * 128, (tt + 1) * 128)
                            gcp = GC[:, tt, dh:dh + 1]
                            di = rr["el"] % 2
                            rr["el"] += 1
                            dg, bdg = DIAG[:, di, :], b_DIAG[di]
                            P.dve(I("tensor_scalar", out=dg, in0=ident, scalar1=gcp, scalar2=None, op0=ALU.mult),
                                  r=[bCON, b_GC], w=[bdg])
                            G_, bG = bank()
                            P.pe(mm(G_[:, 0:128], cst("ones"), dg, True, True), r=[bCON, bdg], w=[bG])
                            Gv = G_[:, 0:128]
                            tA, btA = ut()
                            P.dve(I("scalar_tensor_tensor", out=F(tA), in0=Gv, scalar=gcp, in1=ns, op0=ALU.subtract,
                                    op1=ALU.subtract), r=[bG, b_GC, bCON], w=[btA])
                            P.act(I("activation", out=F(tA), in_=tA, func=AF.Exp, scale=-1.0), r=[btA], w=[btA])
                            tQ, btQ = ut()
                            P.dve(I("scalar_tensor_tensor", out=F(tQ), in0=Gv, scalar=gcp, in1=ni, op0=ALU.subtract,
                                    op1=ALU.add), r=[bG, b_GC, bCON], w=[btQ])
                            P.act(I("activation", out=F(tQ), in_=tQ, func=AF.Exp), r=[btQ], w=[btQ])
                            eg, beg = ut()
                            P.act(I("activation", out=F(eg), in_=Gv, func=AF.Exp), r=[bG], w=[beg])
                            P.dve(I("tensor_tensor", out=F(QDT[:, cs]), in0=QT[:, cs], in1=eg, op=ALU.mult),
                                   r=[b_QT, beg], w=[b_QDT])
                            if DN_SUB < 2:
                                continue
                            kk, bkk = bank()
                            P.pe(mm(kk[:, 0:128], F(KT[:, cs]), F(KT[:, cs]), True, True), r=[b_KT], w=[bkk])
                            Am, bAm = ut()
                            P.dve(I("scalar_tensor_tensor", out=F(Am), in0=kk[:, 0:128], scalar=BETA[:, tt, dh:dh + 1],
                                    in1=tA, op0=ALU.mult, op1=ALU.mult), r=[bkk, b_BETA, btA], w=[bAm])
                            kq, bkq = bank()
                            P.pe(mm(kq[:, 0:128], F(KT[:, cs]), F(QT[:, cs]), True, True), r=[b_KT, b_QT], w=[bkq])
                            P.dve(I("tensor_tensor", out=F(QKTM3[:, tt, :]), in0=kq[:, 0:128], in1=tQ, op=ALU.mult),
                                  r=[bkq, btQ], w=[b_QKTM])
                            if DN_SUB < 3:
                                continue
                            di2 = rr["el"] % 2
                            rr["el"] += 1
                            dg2, bdg2 = DIAG[:, di2, :], b_DIAG[di2]
                            P.dve(I("tensor_scalar", out=dg2, in0=ident, scalar1=BETA[:, tt, dh:dh + 1], scalar2=None,
                                    op0=ALU.mult), r=[bCON, b_BETA], w=[bdg2])
                            Bw, bBw = bank()
                            P.pe(mm(Bw[:, 0:128], cst("ones"), dg2, True, True), r=[bCON, bdg2], w=[bBw])
                            tN, btN = ut()
                            P.dve(I("scalar_tensor_tensor", out=F(tN), in0=Gv, scalar=gcp, in1=nsT, op0=ALU.subtract,
                                    op1=ALU.add), r=[bG, b_GC, bCON], w=[btN])
                            P.act(I("activation", out=F(tN), in_=tN, func=AF.Exp), r=[btN], w=[btN])
                            P.dve(I("tensor_tensor", out=F(tN), in0=kk[:, 0:128], in1=tN, op=ALU.mult),
                                  r=[bkk, btN], w=[btN])
                            Nm, bNm = ut()
                            P.dve(I("tensor_tensor", out=F(Nm), in0=Bw[:, 0:128], in1=tN, op=ALU.mult),
                                  r=[bBw, btN], w=[bNm])
                            Rm, bRm = ut()
                            P.dve(I("tensor_tensor", out=F(Rm), in0=ident, in1=Nm, op=ALU.subtract),
                                  r=[bCON, bNm], w=[bRm])
                            Np, bNp, Ap, bAp = Nm, bNm, Am, bAm
                            for step in range(DN_STEPS):
                                asq, basq = bank()
                                P.pe(mm(asq[:, 0:128], F(Np), F(Ap), True, True), r=[bNp, bAp], w=[basq])
                                An, bAn = ut()
                                P.act(I("activation", out=F(An), in_=asq[:, 0:128], func=AF.Identity), r=[basq], w=[bAn])
                                if step < DN_STEPS - 1:
                                    n2, bn2 = bank()
                                    P.pe(mm(n2[:, 0:128], F(Ap), F(Np), True, True), r=[bNp, bAp], w=[bn2])
                                    Nn, bNn = ut()
                                    P.dve(I("tensor_scalar", out=F(Nn), in0=n2[:, 0:128], scalar1=1.0, scalar2=None, op0=ALU.mult), r=[bn2], w=[bNn])
                                r2, br2 = bank()
                                P.pe(mm(r2[:, 0:128], F(An), F(Rm), True, True), r=[bAn, bRm], w=[br2])
                                Rn, bRn = ut()
                                P.dve(I("tensor_tensor", out=F(Rn), in0=r2[:, 0:128], in1=Rm, op=ALU.add),
                                      r=[br2, bRm], w=[bRn])
                                Rm, bRm = Rn, bRn
                                Ap, bAp = An, bAn
                                if step < DN_STEPS - 1:
                                    Np, bNp = Nn, bNn
                            if DN_SUB < 4:
                                continue
                            bv, bbv = ut()
                            P.dve(I("tensor_scalar", out=F(bv), in0=VTOK3[:, tt, :], scalar1=BETA[:, tt, dh:dh + 1],
                                     scalar2=None, op0=ALU.mult), r=[b_VTOK, b_BETA], w=[bbv])
                            up, bup = bank()
                            P.pe(mm(up[:, 0:128], F(Rm), F(bv), True, True), r=[bRm, bbv], w=[bup])
                            P.act(I("activation", out=F(UU3[:, tt, :]), in_=up[:, 0:128], func=AF.Identity), r=[bup], w=[b_UU])
                            kb, bkb = ut()
                            P.dve(I("tensor_scalar", out=F(kb), in0=KTOK3[:, tt, :], scalar1=BGC[:, tt, dh:dh + 1],
                                     scalar2=None, op0=ALU.mult), r=[b_KTOK, b_BGC], w=[bkb])
                            wp_, bwp_ = bank()
                            P.pe(mm(wp_[:, 0:128], F(kb), F(Rm), True, True), r=[bRm, bkb], w=[bwp_])
                            P.act(I("activation", out=F(WT[:, cs]), in_=wp_[:, 0:128], func=AF.Identity), r=[bwp_], w=[b_WT])
                            P.dve(I("tensor_scalar", out=F(KDEC3[:, tt, :]), in0=KTOK3[:, tt, :],
                                     scalar1=EKD[:, tt, dh:dh + 1], scalar2=None, op0=ALU.mult),
                                   r=[b_KTOK, b_EKD], w=[b_KDEC])
                        cur = 0
                        Sb = [(SST[:, d, 0, :], b_SST[d][0]), (SST[:, d, 1, :], b_SST[d][1])]
                        P.dma("pool", Sb[0][0], d_s0[:, l, dh, :], w=[Sb[0][1]])
                        order = list(range(16)) if d == 0 else list(range(15, -1, -1))
                        if DN_STAGE < 4:
                            order = []
                        for ci, c in enumerate(order):
                            tt, r0 = c // 2, (c % 2) * 64
                            S_, bS = Sb[cur]
                            Sn, bSn = Sb[1 - cur]
                            if ci > 0 and ci % 4 == 0:
                                P.dve(I("tensor_scalar", out=S_, in0=S_, scalar1=vec("fl"), scalar2=None, op0=ALU.mult),
                                      r=[bS, bVEC], w=[bS])
                            cs = slice(tt * 128, (tt + 1) * 128)
                            ccs = slice(c * 64, (c + 1) * 64)
                            ws_, bws = bank()
                            P.pe(mm(ws_[:, 0:128], WT[:, cs], S_, True, True), r=[b_WT, bS], w=[bws])
                            vn, bvn = ut()
                            P.dve(I("tensor_tensor", out=F(vn[r0:r0 + 64, :]), in0=UU3[r0:r0 + 64, tt, :],
                                    in1=ws_[r0:r0 + 64, 0:128], op=ALU.subtract), r=[b_UU, bws], w=[bvn])
                            op_, bop = bank()
                            P.pe(mm(op_[:, 0:64], S_, QDT[:, ccs], True, False), r=[bS, b_QDT], w=[bop])
                            P.pe(mm(op_[:, 0:64], vn[r0:r0 + 64, :], QKTM3[r0:r0 + 64, tt, r0:r0 + 64], False, True),
                                 r=[bvn, b_QKTM], w=[bop])
                            if d == 0:
                                P.act(I("activation", out=F(OT[:, ccs]), in_=op_[:, 0:64], func=AF.Identity),
                                      r=[bop], w=[b_OT])
                            else:
                                P.dve(I("tensor_tensor", out=F(OT[:, ccs]), in0=op_[:, 0:64], in1=OT[:, ccs], op=ALU.add),
                                      r=[bop, b_OT], w=[b_OT])
                            sp_, bsp = bank()
                            P.pe(mm(sp_[:, 0:128], KDEC3[r0:r0 + 64, tt, :], vn[r0:r0 + 64, :], True, True),
                                 r=[b_KDEC, bvn], w=[bsp])
                            P.dve(I("scalar_tensor_tensor", out=Sn, in0=S_, scalar=GLB[:, c, dh:dh + 1], in1=sp_[:, 0:128],
                                    op0=ALU.mult, op1=ALU.add), r=[bS, b_GLB, bsp], w=[bSn])
                            if ci % 4 == 3:
                                P.dma("pool", d_st[:, l, dh, c // 4, :], Sn, r=[bSn], sem_buf=bSn, is_output=True)
                            cur = 1 - cur
                    for th in range(2):
                        hs = slice(th * 512, (th + 1) * 512)
                        rs, brs = rstd_of(lambda c: OT[:, hs], [b_OT], 1, th, 1.0 / 128.0, "dno")
                        t_ap, bt = tmp()
                        P.dve(I("scalar_tensor_tensor", out=t_ap, in0=OT[:, hs], scalar=vec(f"g_dn{l}"), in1=rs,
                                op0=ALU.mult, op1=ALU.mult), r=[b_OT, bVEC, brs], w=[bt])
                        P.dve(I("tensor_tensor", out=F(B[:, h, hs]), in0=t_ap, in1=B[:, h, hs], op=ALU.mult),
                              r=[bt, bB[h]], w=[bB[h]])
                for (_ap, rb, _b) in ring:
                    _b.readers.extend(rb.writers + rb.readers)
                merge_branch(0, "br_dn")

            if enable_sc:
                RAW = SCR[:, 0, :]
                rawp = SCR[:, 0:2, :].rearrange("p a t -> p (a t)")[:, 0:NSEG * 258].rearrange("p (s w) -> p s w", s=NSEG)
                b_raw = bSCR[0]
                acc = SCR[:, 2, :].rearrange("p (s w) -> p s w", s=NSEG)
                b_acc = bSCR[2]
                gcp = SCR[:, 3, :]
                b_gcp = bSCR[3]
                if True:
                    zsrc = cst("ones")[:, 0:4].rearrange("p (s w) -> p s w", w=1)
                    P.dve(I("tensor_scalar", out=rawp[:, :, 0:1].bitcast(F32R), in0=zsrc, scalar1=0.0, scalar2=None,
                                                    op0=ALU.mult), r=[bCON], w=[bSCR[0], bSCR[1]])
                    P.dve(I("tensor_scalar", out=rawp[:, :, 257:258].bitcast(F32R), in0=zsrc, scalar1=0.0,
                                                    scalar2=None, op0=ALU.mult), r=[bCON], w=[bSCR[0], bSCR[1]])
                wc = vec(f"wc_sc{l}")
                for jp in range(4):
                    blkx = next_block("sc_x", jp)
                    xslot = wstate["slot"]
                    for cc in range(2):
                        j = 2 * jp + cc
                        wbg, bwbg = next_block("sc_bg", j, avoid=xslot)
                        wx, bwx = blkx
                        for th in range(2):
                            hs = slice(th * 512, (th + 1) * 512)
                            ps, bps = proj(wbg, bwbg, 128, A, bA, th)
                            P.act(I("activation", out=gcp[:, hs].bitcast(F32R), in_=ps, func=AF.Identity),
                                  r=[bps], w=[b_gcp])
                        for th in range(2):
                            ps, bps = proj(wx, bwx, cc * 128, A, bA, th)
                            P.dve(I("tensor_tensor",
                                out=rawp[:, 2 * th:2 * th + 2, 1:257].bitcast(F32R),
                                in0=ps.rearrange("p (s w) -> p s w", s=2),
                                in1=gcp[:, th * 512:(th + 1) * 512].rearrange("p (s w) -> p s w", s=2),
                                op=ALU.mult), r=[bps, b_gcp], w=[bSCR[0], bSCR[1]])
                        P.dve(I("tensor_scalar", out=rawp[:, 1:4, 0:1].bitcast(F32R), in0=rawp[:, 0:3, 256:257],
                                                        scalar1=vec("fl"), scalar2=None, op0=ALU.mult),
                              r=[bSCR[0], bSCR[1], bVEC], w=[bSCR[0], bSCR[1]])
                        P.dve(I("tensor_scalar", out=rawp[:, 0:3, 257:258].bitcast(F32R), in0=rawp[:, 1:4, 1:2],
                                                        scalar1=vec("fl"), scalar2=None, op0=ALU.mult),
                              r=[bSCR[0], bSCR[1], bVEC], w=[bSCR[0], bSCR[1]])
                        P.dve(I("tensor_scalar", out=acc.bitcast(F32R), in0=rawp[:, :, 1:257], scalar1=wc[:, 8 + j:9 + j],
                                                             scalar2=None, op0=ALU.mult), r=[bSCR[0], bSCR[1], bVEC], w=[b_acc])
                        P.dve(I("scalar_tensor_tensor", out=acc.bitcast(F32R), in0=rawp[:, :, 0:256], scalar=wc[:, j:j + 1],
                                                                    in1=acc, op0=ALU.mult, op1=ALU.add),
                              r=[bSCR[0], bSCR[1], bVEC, b_acc], w=[b_acc])
                        P.dve(I("scalar_tensor_tensor", out=acc.bitcast(F32R), in0=rawp[:, :, 2:258],
                                                                    scalar=wc[:, 16 + j:17 + j],
                                                                    in1=acc, op0=ALU.mult, op1=ALU.add),
                              r=[bSCR[0], bSCR[1], bVEC, b_acc], w=[b_acc])
                        accf = SCR[:, 2, :]
                        for th in range(2):
                            hs = slice(th * 512, (th + 1) * 512)
                            ps, bps = proj(wbg, bwbg, 0, A, bA, th)
                            P.dve(I("tensor_tensor",
                                out=B[:, j, hs].bitcast(F32R), in0=ps, in1=accf[:, hs], op=ALU.mult),
                                r=[bps, b_acc], w=[bB[j]])
                merge_branch(1, "br_sc")

            if enable_att:
                KV5 = SCR[:, 2:7, :].rearrange("p a t -> p (a t)")
                KT = KV5[:, 0:2 * NKEY].rearrange("p (k n) -> p k n", k=2)
                b_KT = [bSCR[2], bSCR[3], bSCR[4]]
                VV = KV5[:, 2 * NKEY:2 * NKEY + NKT * 256].rearrange("p (k n) -> p k n", k=NKT)
                b_VV = [bSCR[4], bSCR[5], bSCR[6]]
                QR = SCR[:, 0, :]
                b_QR = bSCR[0]
                cosT = SCR[:, 1, :]
                sinT = SCR[:, 7, :]
                P.dma("pool", cosT.bitcast(F32R), d_rope[:, 0, :].bitcast(F32R), w=[bSCR[1]])
                P.dma("pool", sinT.bitcast(F32R), d_rope[:, 1, :].bitcast(F32R), w=[bSCR[7]])
                P.dma("pool", KT[:, :, 0:PAST].bitcast(F32R), d_ckT[:, l, :, :].bitcast(F32R), w=b_KT)
                P.dma("pool", VV[:, 0:2, :].bitcast(F32R), d_cv[:, l, :, :].bitcast(F32R), w=b_VV)
                rotT_r = cst("rotT")

                def norm_rope(wblk, bw, col0, gname, dst_fn, bdst, kout=None):
                    for th in range(2):
                        hs = slice(th * 512, (th + 1) * 512)
                        ps, bps = proj(wblk, bw, col0, A, bA, th)
                        qraw, bqraw = tmp()
                        P.act(I("activation", out=qraw, in_=ps, func=AF.Identity),
                              r=[bps], w=[bqraw])
                        rs, brs = rstd_of(lambda c, qraw=qraw: qraw, [bqraw], 1, th, 1.0 / HD, "qk")
                        qn, bqn = tmp()
                        P.dve(I("scalar_tensor_tensor",
                            out=qn, in0=qraw, scalar=vec(gname), in1=rs, op0=ALU.mult, op1=ALU.mult),
                            r=[bqraw, bVEC, brs], w=[bqn])
                        if kout is not None:
                            P.dma("pool", kout[:, hs], qn, r=[bqn], sem_buf=bqn, is_output=True)
                        rp, brp = bank()
                        P.pe(mm(rp, rotT_r, qn, True, True), r=[bqn, bCON], w=[brp])
                        t1, bt1 = tmp()
                        P.dve(I("tensor_tensor", out=t1, in0=rp, in1=sinT[:, hs], op=ALU.mult),
                              r=[brp, bSCR[7]], w=[bt1])
                        t2, bt2 = tmp()
                        P.pool(I("tensor_tensor", out=t2, in0=qn, in1=cosT[:, hs], op=ALU.mult),
                               r=[bqn, bSCR[1]], w=[bt2])
                        P.dve(I("tensor_tensor", out=dst_fn(hs).bitcast(F32R), in0=t1, in1=t2,
                                                                            op=ALU.add), r=[bt1, bt2], w=bdst)

                wk, bwk = next_block("ak", None)
                for kv in range(2):
                    norm_rope(wk, bwk, kv * 128, f"g_k{l}",
                              lambda hs, kv=kv: KT[:, kv, PAST + hs.start:PAST + hs.stop], b_KT,
                              kout=d_kout[:, l, kv, :])
                wv, bwv = next_block("av", None)
                for tt in range(8):
                    ps, bps = bank()
                    for k in range(NCK):
                        P.pe(mm(ps[:, 0:256], A[:, k, tt * 128:(tt + 1) * 128].bitcast(F32R),
                                wv[:, k, :].bitcast(F32R), k == 0, k == NCK - 1), r=[bwv, bA[k]], w=[bps])
                    P.act(I("activation", out=VV[:, 2 + tt, :].bitcast(F32R), in_=ps[:, 0:256], func=AF.Identity),
                          r=[bps], w=b_VV)
                P.dma("pool", d_vout[:, l, :, :], VV[:, 2:10, :], r=b_VV, sem_buf=b_VV[0], is_output=True)
                sm_scale = 1.0 / math.sqrt(HD)
                for hp in range(4):
                    wq, bwq = next_block("aq", hp)
                    for cc in range(2):
                        h = 2 * hp + cc
                        kv = h // 4
                        norm_rope(wq, bwq, cc * 128, f"g_q{l}", lambda hs: QR[:, hs], [b_QR])
                        for th in range(2):
                            hs = slice(th * 512, (th + 1) * 512)
                            ops_, bops = held_bank(0)
                            den, bden = held_bank(1)
                            for kt in range(NKT):
                                st, bst = bank()
                                P.pe(mm(st, KT[:, kv, kt * 128:(kt + 1) * 128].bitcast(F32R), QR[:, hs].bitcast(F32R),
                                        True, True), r=b_KT + [b_QR], w=[bst])
                                pt, ptb = sq()
                                if kt < 2:
                                    segs = [(0, 512, True)]
                                else:
                                    sk = (kt - 2) // 2
                                    segs = []
                                    for s_ in (2 * th, 2 * th + 1):
                                        segs.append(((s_ - 2 * th) * 256, (s_ - 2 * th + 1) * 256, s_ != sk))
                                    if segs[0][2] == segs[1][2]:
                                        segs = [(0, 512, segs[0][2])]
                                for (lo, hi, masked) in segs:
                                    if masked:
                                        P.act(I("activation",
                                            out=pt[:, lo:hi].bitcast(F32R), in_=st[:, lo:hi], func=AF.Exp,
                                            bias=vec("nb"), scale=sm_scale), r=[bst, bVEC], w=[ptb])
                                    else:
                                        P.act(I("activation",
                                            out=pt[:, lo:hi].bitcast(F32R), in_=st[:, lo:hi], func=AF.Exp,
                                            scale=sm_scale), r=[bst], w=[ptb])
                                P.pe(mm(ops_, VV[:, kt, kv * 128:(kv + 1) * 128].bitcast(F32R), pt.bitcast(F32R),
                                        kt == 0, kt == NKT - 1), r=b_VV + [ptb], w=[bops])
                                P.pe(mm(den, ones_r, pt.bitcast(F32R), kt == 0, kt == NKT - 1), r=[ptb, bCON], w=[bden])
                            rd, brd = tmp()
                            P.dve(I("reciprocal", out=rd, in_=den), r=[bden], w=[brd])
                            P.dve(I("tensor_tensor",
                                out=B[:, h, hs].bitcast(F32R), in0=ops_, in1=rd, op=ALU.mult), r=[bops, brd], w=[bB[h]])
                merge_branch(2, "br_att")

            for j in range(4):
                wo, bwo = next_block("out", j)
                for cc in range(2):
                    c = 2 * j + cc
                    for th in range(2):
                        hs = slice(th * 512, (th + 1) * 512)
                        ps, bps = proj(wo, bwo, cc * 128, Cc, bC, th)
                        P.act(I("activation", out=B[:, c, hs].bitcast(F32R), in_=ps, func=AF.Identity),
                              r=[bps], w=[bB[c]])
            post_norm_add(B, bB, g1, bg1)

            pre_norm(a2, ba2, 24)
            for q in range(4):
                for j in range(4):
                    wu, bwu = next_block("up", (q, j))
                    for cc in range(2):
                        c = 2 * j + cc
                        for th in range(2):
                            hs = slice(th * 512, (th + 1) * 512)
                            ps, bps = proj(wu, bwu, cc * 128, A, bA, th)
                            t_ap, bt = tmp()
                            P.act(I("activation", out=t_ap, in_=ps, func=AF.Relu),
                                  r=[bps], w=[bt])
                            ew()(I("tensor_tensor",
                                out=B[:, c, hs].bitcast(F32R), in0=t_ap, in1=t_ap, op=ALU.mult), r=[bt], w=[bB[c]])
                for j in range(4):
                    wd, bwd = next_block("down", (q, j))
                    for cc in range(2):
                        c = 2 * j + cc
                        for th in range(2):
                            hs = slice(th * 512, (th + 1) * 512)
                            ps, bps = proj(wd, bwd, cc * 128, B, bB, th)
                            if q == 0:
                                P.act(I("activation", out=Cc[:, c, hs].bitcast(F32R), in_=ps, func=AF.Identity),
                                      r=[bps], w=[bC[c]])
                            else:
                                P.dve(I("tensor_tensor",
                                    out=Cc[:, c, hs].bitcast(F32R), in0=ps, in1=Cc[:, c, hs], op=ALU.add), r=[bps, bC[c]], w=[bC[c]])
            post_norm_add(Cc, bC, g2, bg2)

        for c in range(NCK):
            P.dma("pool", d_y[:, c, :], X[:, c, :], r=[bX[c]], sem_buf=P.buf(f"yout{c}"), is_output=True)

        with nc.Block() as block:
            P.emit(block)
    return nc


def _pm(v):
    v = np.asarray(v, np.float32)
    return np.ascontiguousarray(v.reshape(-1, 128).T)


def _rope_tables(sample):
    tab = np.zeros((128, 2, T), np.float32)
    if not sample:
        tab[:, 0, :] = 1.0
        return tab
    t = np.arange(T)
    row = (t // 64).astype(np.float32)
    col = (t % 64).astype(np.float32)
    freqs = (np.float32(10000.0) ** (-np.arange(32, dtype=np.float32) / np.float32(32))).astype(np.float32)
    for d in range(128):
        pos = row if d // 64 == 0 else col
        ang = (pos * freqs[(d % 64) % 32]).astype(np.float32)
        tab[d, 0, :] = np.cos(ang)
        tab[d, 1, :] = np.sin(ang)
    return tab


def prepare_inputs(inp, depth=DEPTH, enable_dn=True):
    V = make_vec_layout(depth)
    blocks = layer_blocks(enable_dn)
    ws = np.empty((len(blocks) * depth, 2, 128, NCK * 128), np.float32)
    i = 0
    for l in range(depth):
        lw = {k: np.asarray(inp[k][l], np.float32) for k in
              ("w_mod", "w_in", "w_br_dn", "w_br_sc", "w_br_att", "w_out", "w_mlp_up", "w_mlp_down")}
        for kind, arg in blocks:
            blk = build_block(kind, arg, lw)
            ws[i] = np.asarray(blk, np.float32).reshape(NCK, 128, 2, 128).transpose(2, 1, 0, 3).reshape(2, 128, NCK * 128)
            i += 1
    consts = build_consts()

    def vecs_for(cond, sample):
        a = np.zeros((128, V.n), np.float32)
        a[:, V.sl("cond")] = _pm(cond)
        a[:, V.sl("fl")] = 1.0 if sample else 0.0
        a[:, V.sl("nb")] = 0.0 if sample else NEG
        a[:, V.sl("nfl")] = 0.0 if sample else -1.0
        for l in range(depth):
            a[:, V.sl(f"bmod{l}")] = _pm(inp["b_mod"][l])
            for g in ("g_pre_mix", "g_post_mix", "g_pre_mlp", "g_post_mlp"):
                a[:, V.sl(f"{g}{l}")] = _pm(inp[g][l])
            wc = np.asarray(inp["w_conv_dn"][l], np.float32)
            a[:, V.sl(f"wc_dn{l}")] = np.concatenate([_pm(wc[k]) for k in range(3)], axis=1)
            wc = np.asarray(inp["w_conv_sc"][l], np.float32)
            a[:, V.sl(f"wc_sc{l}")] = np.concatenate([_pm(wc[k]) for k in range(3)], axis=1)
            a[:, V.sl(f"g_dn{l}")] = np.asarray(inp["g_dn_out"][l], np.float32)[:, None]
            a[:, V.sl(f"g_q{l}")] = np.asarray(inp["g_q"][l], np.float32)[:, None]
            a[:, V.sl(f"g_k{l}")] = np.asarray(inp["g_k"][l], np.float32)[:, None]
            for nm, key in (("alog", "a_log"), ("dtb", "dt_bias")):
                row = np.asarray(inp[key][l], np.float32).reshape(16)
                a[:, V.sl(f"{nm}{l}")] = np.broadcast_to(row[None, :], (128, 16))
        return a

    xp = np.asarray(inp["x_prompt"], np.float32)
    xs = np.asarray(inp["x_sample"], np.float32)
    maps = []
    for core in range(N_CORES):
        slot = core if core < 6 else 0
        sample = slot >= 4
        if sample:
            b = slot - 4
            xx = xs[b]
            cond = np.asarray(inp["c"], np.float32)[b]
            ck = np.asarray(inp["cache_k"], np.float32)[b, :depth]
            cvv = np.asarray(inp["cache_v"], np.float32)[b, :depth]
            st = np.asarray(inp["state_dn"], np.float32)[b, :depth]
            ckT = np.ascontiguousarray(ck.transpose(3, 0, 2, 1))
            cv = np.ascontiguousarray(cvv.reshape(depth, 2, 128, 256).transpose(2, 0, 1, 3))
            s0 = np.ascontiguousarray(st.reshape(depth, 16, 128, 128).transpose(2, 0, 1, 3))
        else:
            xx = xp[4 * slot:4 * slot + 4].reshape(T, D)
            cond = np.asarray(inp["c_ctx"], np.float32)
            ckT = np.zeros((128, depth, 2, PAST), np.float32)
            cv = np.zeros((128, depth, 2, 256), np.float32)
            s0 = np.zeros((128, depth, 16, 128), np.float32)
        xT = np.ascontiguousarray(xx.reshape(T, NCK, 128).transpose(2, 1, 0))
        maps.append({"xT": xT, "vecs": vecs_for(cond, sample), "consts": consts, "ws": ws,
                     "rope": _rope_tables(sample), "ckT": ckT, "cv": cv, "s0": s0})
    return maps


def assemble_outputs(results, depth=DEPTH):
    y_prompt = np.zeros((16, SEG, D), np.float32)
    y_sample = np.zeros((2, T, D), np.float32)
    nk = np.zeros((16, depth, SEG, 2, HD), np.float32)
    nv = np.zeros((16, depth, SEG, 2, HD), np.float32)
    ns = np.zeros((16, depth, 2, 8, 128, 128), np.float32)
    for slot in range(6):
        r = results[slot]
        y = np.asarray(r["yT"]).transpose(2, 1, 0).reshape(T, D)
        if slot >= 4:
            y_sample[slot - 4] = y
            continue
        y_prompt[4 * slot:4 * slot + 4] = y.reshape(4, SEG, D)
        kT = np.asarray(r["kT_out"])
        vo = np.asarray(r["v_out"])
        so = np.asarray(r["st_out"])
        kk = kT.transpose(1, 3, 2, 0).reshape(depth, 4, SEG, 2, HD)
        nk[4 * slot:4 * slot + 4] = kk.transpose(1, 0, 2, 3, 4)
        vv = vo.transpose(1, 2, 0, 3).reshape(depth, T, 2, HD).reshape(depth, 4, SEG, 2, HD)
        nv[4 * slot:4 * slot + 4] = vv.transpose(1, 0, 2, 3, 4)
        ss = so.transpose(3, 1, 2, 0, 4).reshape(4, depth, 2, 8, 128, 128)
        ns[4 * slot:4 * slot + 4] = ss
    return y_prompt, y_sample, nk, nv, ns


_PROG_CACHE = {}


def kernel(**inputs):
    key = "full"
    if key not in _PROG_CACHE:
        _PROG_CACHE[key] = build_program(enable_dn=ENABLE_DN)
    nc = _PROG_CACHE[key]
    maps = prepare_inputs(inputs, enable_dn=ENABLE_DN)
    res = run_bass_kernel_spmd(nc, maps, core_ids=list(range(N_CORES)))
    return assemble_outputs(res.results)
```

```python
import math
from contextlib import ExitStack

import numpy as np
import concourse.bass as bass
import concourse.mybir as mybir
from concourse.bass_utils import run_bass_kernel_spmd

F32 = mybir.dt.float32
F32R = mybir.dt.float32r
AF = mybir.ActivationFunctionType
ALU = mybir.AluOpType

D = 1024
T = 1024
NCK = 8
DEPTH = 4
NSEG = 4
SEG = 256
HD = 128
PAST = 256
NKEY = PAST + T
NKT = NKEY // 128
EPS = 1e-6
NEG = -30000.0
BLK = 256
N_CORES = 8
NW = 2
import os as _os
DN_STAGE = int(_os.environ.get("DN_STAGE", "9"))
DN_SUB = int(_os.environ.get("DN_SUB", "9"))
DN_STEPS = int(_os.environ.get("DN_STEPS", "5"))
DN_X = int(_os.environ.get("DN_X", "0"))
DEBUG_DUMPS = False
ENABLE_DN = True

IN_OFF = {}
_o = 0
for _n, _s in (("dq", 1024), ("dk", 1024), ("dv", 1024), ("dgate", 1024), ("dbeta", 16), ("dalpha", 16),
               ("sb", 1024), ("scg", 1024), ("sx", 1024), ("aq", 1024), ("ak", 256), ("av", 256), ("mg", 3072)):
    IN_OFF[_n] = _o
    _o += _s
IN_TOTAL = _o


def _cols(name, start, n):
    return list(range(IN_OFF[name] + start, IN_OFF[name] + start + n))


def layer_blocks(enable_dn=True):
    blocks = []
    for m in range(24):
        blocks.append(("mod", m))
    if enable_dn:
        blocks.append(("gates", None))
        for h in range(8):
            blocks.append(("dn_qk", h))
            blocks.append(("dn_vg", h))
        for j in range(4):
            blocks.append(("mg", (0, j)))
            blocks.append(("br_dn", j))
    for jp in range(4):
        blocks.append(("sc_x", jp))
        blocks.append(("sc_bg", 2 * jp))
        blocks.append(("sc_bg", 2 * jp + 1))
    for j in range(4):
        blocks.append(("mg", (1, j)))
        blocks.append(("br_sc", j))
    blocks.append(("ak", None))
    blocks.append(("av", None))
    for hp in range(4):
        blocks.append(("aq", hp))
    for j in range(4):
        blocks.append(("mg", (2, j)))
        blocks.append(("br_att", j))
    for j in range(4):
        blocks.append(("out", j))
    for q in range(4):
        for j in range(4):
            blocks.append(("up", (q, j)))
        for j in range(4):
            blocks.append(("down", (q, j)))
    return blocks


def build_block(kind, arg, lw):
    w_in = lw["w_in"]
    if kind == "mod":
        return lw["w_mod"][:, arg * 256:(arg + 1) * 256]
    if kind == "gates":
        blk = np.zeros((1024, 256), np.float32)
        b0, a0 = IN_OFF["dbeta"], IN_OFF["dalpha"]
        blk[:, 0:16] = w_in[:, b0:b0 + 16]
        blk[:, 16:32] = w_in[:, a0:a0 + 16]
        return blk
    if kind == "dn_qk":
        return w_in[:, _cols("dq", arg * 128, 128) + _cols("dk", arg * 128, 128)]
    if kind == "dn_vg":
        return w_in[:, _cols("dv", arg * 128, 128) + _cols("dgate", arg * 128, 128)]
    if kind == "mg":
        br, j = arg
        return w_in[:, _cols("mg", br * 1024 + j * 256, 256)]
    if kind == "br_dn":
        return lw["w_br_dn"][:, arg * 256:(arg + 1) * 256]
    if kind == "br_sc":
        return lw["w_br_sc"][:, arg * 256:(arg + 1) * 256]
    if kind == "br_att":
        return lw["w_br_att"][:, arg * 256:(arg + 1) * 256]
    if kind == "sc_bg":
        return w_in[:, _cols("sb", arg * 128, 128) + _cols("scg", arg * 128, 128)]
    if kind == "sc_x":
        return w_in[:, _cols("sx", arg * 256, 256)]
    if kind == "ak":
        return w_in[:, _cols("ak", 0, 256)]
    if kind == "av":
        return w_in[:, _cols("av", 0, 256)]
    if kind == "aq":
        return w_in[:, _cols("aq", arg * 256, 256)]
    if kind == "out":
        return lw["w_out"][:, arg * 256:(arg + 1) * 256]
    if kind == "up":
        q, j = arg
        return lw["w_mlp_up"][:, q * 1024 + j * 256: q * 1024 + (j + 1) * 256]
    if kind == "down":
        q, j = arg
        return lw["w_mlp_down"][q * 1024:(q + 1) * 1024, j * 256:(j + 1) * 256]
    raise ValueError(kind)


class VecLayout:
    def __init__(self):
        self.off = {}
        self.n = 0

    def add(self, name, width):
        self.off[name] = (self.n, width)
        self.n += width

    def sl(self, name):
        o, w = self.off[name]
        return slice(o, o + w)


def make_vec_layout(depth):
    V = VecLayout()
    V.add("cond", 8)
    V.add("fl", 1)
    V.add("nb", 1)
    V.add("nfl", 1)
    for l in range(depth):
        V.add(f"bmod{l}", 48)
        for g in ("g_pre_mix", "g_post_mix", "g_pre_mlp", "g_post_mlp"):
            V.add(f"{g}{l}", 8)
        V.add(f"wc_dn{l}", 72)
        V.add(f"wc_sc{l}", 24)
        V.add(f"g_dn{l}", 1)
        V.add(f"g_q{l}", 1)
        V.add(f"g_k{l}", 1)
        V.add(f"alog{l}", 16)
        V.add(f"dtb{l}", 16)
    return V


class ConstLayout(VecLayout):
    pass


def make_const_layout():
    C = ConstLayout()
    for nm in ("ident", "ones", "rotT", "ni_f", "ni_b", "ns_f", "ns_b", "tri_f", "tri_b",
               "last_f", "last_b", "all_f0", "all_f1", "all_b0", "all_b1"):
        C.add(nm, 128)
    return C


def build_consts():
    C = make_const_layout()
    a = np.zeros((128, C.n), np.float32)
    a[:, C.sl("ident")] = np.eye(128, dtype=np.float32)
    a[:, C.sl("ones")] = 1.0
    R = np.zeros((128, 128), np.float32)
    for d in range(128):
        idx = d % 64
        if idx < 32:
            R[d, d + 32] = -1.0
        else:
            R[d, d - 32] = 1.0
    a[:, C.sl("rotT")] = R.T
    p = np.arange(128)[:, None]
    f = np.arange(128)[None, :]
    same = (p // 64) == (f // 64)
    a[:, C.sl("ni_f")] = np.where(same & (f >= p), 0.0, NEG)
    a[:, C.sl("ni_b")] = np.where(same & (f <= p), 0.0, NEG)
    a[:, C.sl("ns_f")] = np.where(same & (f < p), 0.0, NEG)
    a[:, C.sl("ns_b")] = np.where(same & (f > p), 0.0, NEG)
    a[:, C.sl("tri_f")] = np.where(same & (f >= p), 1.0, 0.0)
    a[:, C.sl("tri_b")] = np.where(same & (f <= p), 1.0, 0.0)
    cend = (f // 64) * 64 + 63
    cstart = (f // 64) * 64
    a[:, C.sl("last_f")] = (p == cend).astype(np.float32)
    a[:, C.sl("last_b")] = (p == cstart).astype(np.float32)
    a[:, C.sl("all_f0")] = (p == 63).astype(np.float32) * np.ones((1, 128), np.float32)
    a[:, C.sl("all_f1")] = (p == 127).astype(np.float32) * np.ones((1, 128), np.float32)
    a[:, C.sl("all_b0")] = (p == 0).astype(np.float32) * np.ones((1, 128), np.float32)
    a[:, C.sl("all_b1")] = (p == 64).astype(np.float32) * np.ones((1, 128), np.float32)
    return a


def I(method, **kw):
    return lambda e: getattr(e, method)(**kw)


class WB:
    def __init__(self, b0, b1):
        self.bufs = [b0, b1]


def _flat(lst):
    out = []
    for b in lst:
        if isinstance(b, WB):
            out.extend(b.bufs)
        else:
            out.append(b)
    return out


class Buf:
    __slots__ = ("name", "writers", "readers", "prev", "sem", "dma_n")

    def __init__(self, name):
        self.name = name
        self.writers = []
        self.readers = []
        self.prev = []
        self.sem = None
        self.dma_n = 0


class Op:
    __slots__ = ("eng", "fn", "waits", "milestone", "count", "dma_sem", "is_dma", "epoch")

    def __init__(self, eng, fn):
        self.eng = eng
        self.fn = fn
        self.epoch = 0
        self.waits = []
        self.milestone = False
        self.count = 0
        self.dma_sem = None
        self.is_dma = False


class Prog:
    ENGS = ("pe", "act", "dve", "pool", "sp")

    def __init__(self, nc, stack):
        self.nc = nc
        self.stack = stack
        self.ops = {e: [] for e in self.ENGS}
        self.sems = {}
        self.epoch = 0
        self.out_tokens = []

    def esem(self, eng, epoch):
        k = (eng, epoch)
        if k not in self.sems:
            self.sems[k] = self.stack.enter_context(self.nc.semaphore(f"s_{eng}_{epoch}"))
        return self.sems[k]

    def buf(self, name):
        return Buf(name)

    def _dma_sem(self, b):
        if b.sem is None:
            b.sem = self.stack.enter_context(self.nc.semaphore(f"d_{b.name}"))
        return b.sem

    def _add(self, eng, fn, r, w, dma_buf=None):
        op = Op(eng, fn)
        op.epoch = self.epoch
        r = _flat(r)
        w = _flat(w)
        waits = []
        for b in r:
            waits.extend(b.writers)
        for b in w:
            if b.readers:
                b.prev = b.writers + b.readers
                b.writers = []
                b.readers = []
            waits.extend(b.prev)
        seen = set()
        for tok in waits:
            if tok[0] == "e":
                o = tok[1]
                if o.eng == eng and eng in ("pe", "sp"):
                    continue
                if id(o) in seen:
                    continue
                seen.add(id(o))
                o.milestone = True
                op.waits.append(tok)
            else:
                key = (id(tok[1]), tok[2])
                if key in seen:
                    continue
                seen.add(key)
                op.waits.append(tok)
        if dma_buf is not None:
            op.is_dma = True
            op.dma_sem = self._dma_sem(dma_buf)
            dma_buf.dma_n += 1
            tok = ("d", op.dma_sem, 16 * dma_buf.dma_n)
        else:
            tok = ("e", op)
        for b in r:
            b.readers.append(tok)
        for b in w:
            b.writers.append(tok)
        self.ops[eng].append(op)
        return tok

    def pe(self, fn, r=(), w=()):
        return self._add("pe", fn, r, w)

    def act(self, fn, r=(), w=()):
        return self._add("act", fn, r, w)

    def dve(self, fn, r=(), w=()):
        return self._add("dve", fn, r, w)

    def pool(self, fn, r=(), w=()):
        return self._add("pool", fn, r, w)

    def dma(self, eng, out, in_, r=(), w=(), sem_buf=None, is_output=False):
        sb = sem_buf if sem_buf is not None else (w[0] if w else r[0])
        tok = self._add(eng, lambda e: e.dma_start(out=out, in_=in_), r, w, dma_buf=sb)
        if is_output:
            self.out_tokens.append(tok)
        return tok

    def emit(self, block):
        for e in self.ENGS:
            cnt = {}
            for op in self.ops[e]:
                if op.milestone:
                    cnt[op.epoch] = cnt.get(op.epoch, 0) + 1
                    op.count = cnt[op.epoch]
                    self.esem(e, op.epoch)
        final = {}
        for tok in self.out_tokens:
            k = id(tok[1])
            if k not in final or final[k][2] < tok[2]:
                final[k] = tok
        finals = list(final.values())
        handles = {"pe": block.tensor, "act": block.scalar, "dve": block.vector, "pool": block.gpsimd,
                   "sp": block.sync}
        for e in self.ENGS:
            ops = self.ops[e]
            sems = self.sems

            def body(eng, ops=ops, e=e):
                seen = {}
                for op in ops:
                    need = {}
                    for tok in op.waits:
                        if tok[0] == "e":
                            o = tok[1]
                            s, v = sems[(o.eng, o.epoch)], o.count
                        else:
                            s, v = tok[1], tok[2]
                        k = id(s)
                        if k not in need or need[k][1] < v:
                            need[k] = (s, v)
                    for k, (s, v) in need.items():
                        if seen.get(k, 0) >= v:
                            continue
                        seen[k] = v
                        eng.wait_ge(s, v)
                    ins = op.fn(eng)
                    if op.is_dma:
                        ins.then_inc(op.dma_sem, 16)
                    elif op.milestone:
                        ins.then_inc(sems[(e, op.epoch)], 1)
                if e == "sp":
                    for tok in finals:
                        eng.wait_ge(tok[1], tok[2])

            handles[e](body)


def build_program(depth=DEPTH, enable_dn=True, enable_sc=True, enable_att=True):
    nc = bass.Bass("TRN2", target_bir_lowering=False)
    nc.dge_precook = False
    V = make_vec_layout(depth)
    C = make_const_layout()
    blocks_per_layer = layer_blocks(enable_dn)
    NB = len(blocks_per_layer) * depth

    d_x = nc.dram_tensor("xT", [128, NCK, T], F32, kind="ExternalInput").ap()
    d_vecs = nc.dram_tensor("vecs", [128, V.n], F32, kind="ExternalInput").ap()
    d_consts = nc.dram_tensor("consts", [128, C.n], F32, kind="ExternalInput").ap()
    d_ws = nc.dram_tensor("ws", [NB, 2, 128, NCK * 128], F32, kind="ExternalInput").ap()
    d_rope = nc.dram_tensor("rope", [128, 2, T], F32, kind="ExternalInput").ap()
    d_ckT = nc.dram_tensor("ckT", [128, depth, 2, PAST], F32, kind="ExternalInput").ap()
    d_cv = nc.dram_tensor("cv", [128, depth, 2, 256], F32, kind="ExternalInput").ap()
    d_s0 = nc.dram_tensor("s0", [128, depth, 16, 128], F32, kind="ExternalInput").ap()
    d_y = nc.dram_tensor("yT", [128, NCK, T], F32, kind="ExternalOutput").ap()
    d_kout = nc.dram_tensor("kT_out", [128, depth, 2, T], F32, kind="ExternalOutput").ap()
    d_vout = nc.dram_tensor("v_out", [128, depth, 8, 256], F32, kind="ExternalOutput").ap()
    d_st = nc.dram_tensor("st_out", [128, depth, 16, NSEG, 128], F32, kind="ExternalOutput").ap()
    if DEBUG_DUMPS:
        d_dbgA = nc.dram_tensor("dbgA", [128, NCK, T], F32, kind="ExternalOutput").ap()
        d_dbgM = nc.dram_tensor("dbgM", [128, 64], F32, kind="ExternalOutput").ap()

    with ExitStack() as stack:
        P = Prog(nc, stack)

        def sb(name, shape, dt=F32):
            return stack.enter_context(nc.sbuf_tensor(name, shape, dt))

        X = sb("X", [128, NCK, T])
        A = sb("A", [128, NCK, T])
        B = sb("B", [128, NCK, T])
        Cc = sb("C", [128, NCK, T])
        W = sb("W", [128, NW, NCK * BLK])
        VEC = sb("VEC", [128, V.n])
        CON = sb("CON", [128, C.n])
        SCR = sb("SCR", [128, 8, T])
        TMP = sb("TMP", [128, 4, 512])
        SQ = sb("SQ", [128, 2, 512])
        SMALL = sb("SMALL", [128, 256])
        DNS = sb("DNS", [128, 2008])
        _o = [0]

        def carve(n):
            ap_ = DNS[:, _o[0]:_o[0] + n]
            _o[0] += n
            return ap_
        GRAW = carve(256).rearrange("p (t n) -> p t n", t=8)
        BETA = carve(128).rearrange("p (t n) -> p t n", t=8)
        GC = carve(128).rearrange("p (t n) -> p t n", t=8)
        BGC = carve(128).rearrange("p (t n) -> p t n", t=8)
        EKD = carve(128).rearrange("p (t n) -> p t n", t=8)
        ZG = carve(128).rearrange("p (t n) -> p t n", t=8)
        GLB = carve(256).rearrange("p (c n) -> p c n", c=16)
        WCN = carve(72)
        NA = carve(16)
        DIAG = carve(256).rearrange("p (a n) -> p a n", a=2)
        SST = carve(512).rearrange("p (d a n) -> p d a n", d=2, a=2)
        b_GRAW, b_BETA, b_GC, b_BGC, b_EKD, b_ZG, b_GLB, b_WCN, b_NA = [P.buf(n_) for n_ in
            ("GRAW", "BETA", "GC", "BGC", "EKD", "ZG", "GLB", "WCN", "NA")]
        b_DIAG = [P.buf("DIAG0"), P.buf("DIAG1")]
        b_SST = [[P.buf("S00"), P.buf("S01")], [P.buf("S10"), P.buf("S11")]]
        PS = stack.enter_context(nc.psum_tensor("PS", [128, 8, 512], F32))

        bX = [P.buf(f"X{c}") for c in range(NCK)]
        bA = [P.buf(f"A{c}") for c in range(NCK)]
        bB = [P.buf(f"B{c}") for c in range(NCK)]
        bC = [P.buf(f"C{c}") for c in range(NCK)]
        bW = [WB(P.buf(f"W{i}a"), P.buf(f"W{i}b")) for i in range(NW)]
        bVEC = P.buf("VEC")
        bCON = P.buf("CON")
        bSCR = [P.buf(f"SCR{i}") for i in range(8)]
        bTMP = [P.buf(f"TMP{i}") for i in range(4)]
        bSQ = [P.buf(f"SQ{i}") for i in range(2)]
        bSM = {}
        bPS = [P.buf(f"PS{i}") for i in range(8)]
        rr = {"ps": 0, "tmp": 0, "sq": 0, "w": 0, "el": 0}

        def bank():
            i = rr["ps"] % 6
            rr["ps"] += 1
            return PS[:, i, :], bPS[i]

        def held_bank(i):
            return PS[:, 6 + i, :], bPS[6 + i]

        def tmp():
            i = rr["tmp"] % 4
            rr["tmp"] += 1
            return TMP[:, i, :], bTMP[i]

        def sq():
            i = rr["sq"] % 2
            rr["sq"] += 1
            return SQ[:, i, :], bSQ[i]

        def small(name, width):
            if name not in bSM:
                o = sum(w for (_, w, _) in bSM.values())
                assert o + width <= 256
                bSM[name] = (o, width, P.buf("sm_" + name))
            o, w, b = bSM[name]
            return SMALL[:, o:o + w], b

        def cst(name):
            return CON[:, C.sl(name)]

        def vec(name):
            return VEC[:, V.sl(name)]

        wstate = {"i": 0}

        def next_block(expect_kind=None, expect_arg=None, avoid=None):
            i = wstate["i"]
            kind, arg = blocks_per_layer[i % len(blocks_per_layer)]
            if expect_kind is not None:
                assert kind == expect_kind and arg == expect_arg, (kind, arg, expect_kind, expect_arg)
            s = (wstate.get("last", -1) + 1) % NW
            if avoid is not None and s == avoid:
                s = (s + 1) % NW
            wstate["last"] = s
            wstate["slot"] = s
            wv3 = W[:, s, :].rearrange("p (k n) -> p k n", k=NCK)
            for c_ in range(2):
                P.dma("sp", wv3[:, :, c_ * 128:(c_ + 1) * 128].bitcast(F32R),
                      d_ws[i, c_].rearrange("p (k n) -> p k n", k=NCK).bitcast(F32R), w=[bW[s].bufs[c_]])
            wstate["i"] = i + 1
            return W[:, s, :].rearrange("p (k n) -> p k n", k=NCK), bW[s]

        def ew():
            rr["el"] += 1
            return P.dve if rr["el"] % 2 else P.pool

        P.dma("pool", VEC[:], d_vecs, w=[bVEC])
        P.dma("pool", CON[:].bitcast(F32R), d_consts.bitcast(F32R), w=[bCON])
        for c in range(NCK):
            P.dma("act", X[:, c, :], d_x[:, c, :], w=[bX[c]])

        ident = cst("ident")
        ones_r = cst("ones").bitcast(F32R)

        scond, b_scond = small("scond", 8)
        P.act(I("activation", out=scond, in_=vec("cond"), func=AF.Silu), r=[bVEC], w=[b_scond])

        def mm(out, lhsT, rhs, start, stop):
            return lambda e: e.matmul(out, lhsT, rhs, start=start, stop=stop)

        def proj(wblk, bw, col0, src, bsrc, th, nk=NCK):
            ps, bps = bank()
            for k in range(nk):
                P.pe(mm(ps, wblk[:, k, col0:col0 + 128].bitcast(F32R),
                        src[:, k, th * 512:(th + 1) * 512].bitcast(F32R), k == 0, k == nk - 1),
                     r=[bw.bufs[col0 // 128] if isinstance(bw, WB) else bw] + list(bsrc[:nk]), w=[bps])
            return ps, bps

        def rstd_of(src_fn, bsrcs, nch, th, scale, out_name):
            ss, bss = bank()
            for c in range(nch):
                s_ap, bs = sq()
                src = src_fn(c)
                P.act(I("activation", out=s_ap.bitcast(F32R), in_=src, func=AF.Square),
                      r=[bsrcs[c]], w=[bs])
                P.pe(mm(ss, ones_r, s_ap.bitcast(F32R), c == 0, c == nch - 1), r=[bs, bCON], w=[bss])
            sd, bsd = tmp()
            P.act(I("activation", out=sd, in_=ss, func=AF.Sqrt, bias=EPS, scale=scale), r=[bss], w=[bsd])
            rs, brs = bank()
            P.dve(I("reciprocal", out=rs, in_=sd), r=[bsd], w=[brs])
            return rs, brs

        for l in range(depth):
            P.epoch = l
            modps, bmodps = bank()
            for m2 in range(24):
                wb, bw = next_block("mod", m2)
                for cc in range(2):
                    m = m2 * 2 + cc
                    for k in range(NCK):
                        P.pe(mm(modps[:, m:m + 1], wb[:, k, cc * 128:(cc + 1) * 128], scond[:, k:k + 1],
                                k == 0, k == NCK - 1), r=[bw.bufs[cc], b_scond], w=[bmodps])
            modv, bmodv = small("modv", 48)
            P.dve(I("tensor_tensor", out=modv, in0=modps[:, 0:48], in1=vec(f"bmod{l}"), op=ALU.add),
                  r=[bmodps, bVEC], w=[bmodv])
            a1, ba1 = small("a1", 8)
            a2, ba2 = small("a2", 8)
            g1, bg1 = small("g1", 8)
            g2, bg2 = small("g2", 8)
            P.dve(I("scalar_tensor_tensor", out=a1, in0=modv[:, 8:16], scalar=1.0, in1=vec(f"g_pre_mix{l}"),
                                                        op0=ALU.add, op1=ALU.mult), r=[bmodv, bVEC], w=[ba1])
            P.dve(I("scalar_tensor_tensor", out=a2, in0=modv[:, 32:40], scalar=1.0, in1=vec(f"g_pre_mlp{l}"),
                                                        op0=ALU.add, op1=ALU.mult), r=[bmodv, bVEC], w=[ba2])
            P.dve(I("tensor_tensor", out=g1, in0=modv[:, 16:24], in1=vec(f"g_post_mix{l}"), op=ALU.mult),
                  r=[bmodv, bVEC], w=[bg1])
            P.dve(I("tensor_tensor", out=g2, in0=modv[:, 40:48], in1=vec(f"g_post_mlp{l}"), op=ALU.mult),
                  r=[bmodv, bVEC], w=[bg2])

            def pre_norm(scale_ap, bscale, shift_lo):
                for th in range(2):
                    hs = slice(th * 512, (th + 1) * 512)
                    rs, brs = rstd_of(lambda c: X[:, c, hs], bX, NCK, th, 1.0 / D, "pre")
                    for c in range(NCK):
                        t_ap, bt = tmp()
                        P.dve(I("tensor_tensor", out=t_ap, in0=X[:, c, hs], in1=rs, op=ALU.mult),
                              r=[bX[c], brs], w=[bt])
                        P.act(I("activation",
                            out=A[:, c, hs].bitcast(F32R), in_=t_ap, func=AF.Identity,
                            scale=scale_ap[:, c:c + 1], bias=modv[:, shift_lo + c:shift_lo + c + 1]),
                            r=[bt, bscale, bmodv], w=[bA[c]])

            def post_norm_add(SRC, bSRC, gate_ap, bgate):
                for th in range(2):
                    hs = slice(th * 512, (th + 1) * 512)
                    rs, brs = rstd_of(lambda c: SRC[:, c, hs], bSRC, NCK, th, 1.0 / D, "post")
                    for c in range(NCK):
                        t_ap, bt = tmp()
                        P.dve(I("tensor_tensor", out=t_ap, in0=SRC[:, c, hs], in1=rs, op=ALU.mult),
                              r=[bSRC[c], brs], w=[bt])
                        P.dve(I("scalar_tensor_tensor",
                            out=X[:, c, hs], in0=t_ap, scalar=gate_ap[:, c:c + 1], in1=X[:, c, hs],
                            op0=ALU.mult, op1=ALU.add), r=[bt, bgate, bX[c]], w=[bX[c]])

            pre_norm(a1, ba1, 0)
            if DEBUG_DUMPS and l == 0:
                for c in range(NCK):
                    P.dma("pool", d_dbgA[:, c, :], A[:, c, :], r=[bA[c]], sem_buf=bA[c], is_output=True)
                P.dma("pool", d_dbgM[:, 0:48], modv, r=[bmodv], sem_buf=bmodv, is_output=True)
                P.dma("pool", d_dbgM[:, 48:56], scond, r=[b_scond], sem_buf=b_scond, is_output=True)
                P.dma("pool", d_dbgM[:, 56:64], a1, r=[ba1], sem_buf=ba1, is_output=True)

            first_branch = {"v": True}

            def merge_branch(br, wname):
                first = first_branch["v"]
                first_branch["v"] = False
                for j in range(4):
                    wg, bwg = next_block("mg", (br, j))
                    wp, bwp = next_block(wname, j)
                    for cc in range(2):
                        c = 2 * j + cc
                        for th in range(2):
                            hs = slice(th * 512, (th + 1) * 512)
                            ps1, bps1 = proj(wg, bwg, cc * 128, A, bA, th)
                            g_ap, bg = tmp()
                            P.act(I("activation", out=g_ap, in_=ps1, func=AF.Sigmoid),
                                  r=[bps1], w=[bg])
                            ps, bps = proj(wp, bwp, cc * 128, B, bB, th)
                            if first:
                                P.dve(I("tensor_tensor",
                                    out=Cc[:, c, hs].bitcast(F32R), in0=ps, in1=g_ap, op=ALU.mult), r=[bps, bg], w=[bC[c]])
                            else:
                                t_ap, bt = tmp()
                                P.dve(I("tensor_tensor",
                                    out=t_ap, in0=ps, in1=g_ap, op=ALU.mult), r=[bps, bg], w=[bt])
                                P.pool(I("tensor_tensor",
                                    out=Cc[:, c, hs].bitcast(F32R), in0=Cc[:, c, hs], in1=t_ap, op=ALU.add),
                                    r=[bt, bC[c]], w=[bC[c]])

            if enable_dn:
                F = lambda ap: ap.bitcast(F32R)
                wgt, bwgt = next_block("gates", None)
                for tt in range(8):
                    ps, bps = bank()
                    for k in range(NCK):
                        P.pe(mm(ps[:, 0:256], A[:, k, tt * 128:(tt + 1) * 128].bitcast(F32R),
                                wgt[:, k, :].bitcast(F32R), k == 0, k == NCK - 1), r=[bwgt, bA[k]], w=[bps])
                    P.act(I("activation", out=GRAW[:, tt, :], in_=ps[:, 0:32], func=AF.Identity), r=[bps], w=[b_GRAW])
                P.act(I("activation", out=BETA, in_=GRAW[:, :, 0:16], func=AF.Sigmoid), r=[b_GRAW], w=[b_BETA])
                for tt in range(8):
                    P.dve(I("tensor_tensor", out=ZG[:, tt, :], in0=GRAW[:, tt, 16:32], in1=vec(f"dtb{l}"), op=ALU.add),
                          r=[b_GRAW, bVEC], w=[b_ZG])
                P.act(I("activation", out=ZG, in_=ZG, func=AF.Exp), r=[b_ZG], w=[b_ZG])
                P.act(I("activation", out=ZG, in_=ZG, func=AF.Ln, bias=1.0), r=[b_ZG], w=[b_ZG])
                P.act(I("activation", out=NA, in_=vec(f"alog{l}"), func=AF.Exp), r=[bVEC], w=[b_NA])
                P.dve(I("tensor_scalar", out=NA, in0=NA, scalar1=-1.0, scalar2=None, op0=ALU.mult), r=[b_NA], w=[b_NA])
                for tt in range(8):
                    P.dve(I("tensor_tensor", out=ZG[:, tt, :], in0=ZG[:, tt, :], in1=NA, op=ALU.mult),
                          r=[b_ZG, b_NA], w=[b_ZG])
                ps, bps = bank()
                psv = ps[:, 0:256].rearrange("p (t d n) -> p t d n", t=8, d=2)
                for tt in range(8):
                    P.pe(mm(psv[:, tt, 0, :], cst("tri_f"), ZG[:, tt, :], True, True), r=[bCON, b_ZG], w=[bps])
                    P.pe(mm(psv[:, tt, 1, :], cst("tri_b"), ZG[:, tt, :], True, True), r=[bCON, b_ZG], w=[bps])
                P.act(I("activation", out=GC[:, :, 0:8], in_=psv[:, :, 0, 0:8], func=AF.Identity), r=[bps], w=[b_GC])
                P.act(I("activation", out=GC[:, :, 8:16], in_=psv[:, :, 1, 8:16], func=AF.Identity), r=[bps], w=[b_GC])
                ps, bps = bank()
                psv = ps[:, 0:256].rearrange("p (t d n) -> p t d n", t=8, d=2)
                for tt in range(8):
                    P.pe(mm(psv[:, tt, 0, :], cst("last_f"), GC[:, tt, :], True, True), r=[bCON, b_GC], w=[bps])
                    P.pe(mm(psv[:, tt, 1, :], cst("last_b"), GC[:, tt, :], True, True), r=[bCON, b_GC], w=[bps])
                P.dve(I("tensor_tensor", out=EKD[:, :, 0:8], in0=psv[:, :, 0, 0:8], in1=GC[:, :, 0:8], op=ALU.subtract),
                      r=[bps, b_GC], w=[b_EKD])
                P.dve(I("tensor_tensor", out=EKD[:, :, 8:16], in0=psv[:, :, 1, 8:16], in1=GC[:, :, 8:16], op=ALU.subtract),
                      r=[bps, b_GC], w=[b_EKD])
                P.act(I("activation", out=EKD, in_=EKD, func=AF.Exp), r=[b_EKD], w=[b_EKD])
                ps, bps = bank()
                psv = ps[:, 0:512].rearrange("p (c d n) -> p c d n", c=16, d=2)
                for tt in range(8):
                    for cc in range(2):
                        P.pe(mm(psv[:, 2 * tt + cc, 0, :], cst(f"all_f{cc}"), GC[:, tt, :], True, True),
                             r=[bCON, b_GC], w=[bps])
                        P.pe(mm(psv[:, 2 * tt + cc, 1, :], cst(f"all_b{cc}"), GC[:, tt, :], True, True),
                             r=[bCON, b_GC], w=[bps])
                P.act(I("activation", out=GLB[:, :, 0:8], in_=psv[:, :, 0, 0:8], func=AF.Exp), r=[bps], w=[b_GLB])
                P.act(I("activation", out=GLB[:, :, 8:16], in_=psv[:, :, 1, 8:16], func=AF.Exp), r=[bps], w=[b_GLB])
                P.act(I("activation", out=BGC, in_=GC, func=AF.Exp), r=[b_GC], w=[b_BGC])
                P.dve(I("tensor_tensor", out=BGC, in0=BGC, in1=BETA, op=ALU.mult), r=[b_BGC, b_BETA], w=[b_BGC])
                P.dve(I("tensor_scalar", out=WCN, in0=vec(f"wc_dn{l}"), scalar1=vec("nfl"), scalar2=None, op0=ALU.mult),
                      r=[bVEC], w=[b_WCN])
                wcd = vec(f"wc_dn{l}")

                def slot(i):
                    return (Cc[:, i, :], bC[i]) if i < 8 else (SCR[:, i - 8, :], bSCR[i - 8])

                RAW, b_RAW = slot(0)
                ACC, b_ACC = slot(1)
                QT, b_QT = slot(2)
                KT, b_KT = slot(3)
                KTOK, b_KTOK = slot(4)
                VTOK, b_VTOK = slot(5)
                QKTM, b_QKTM = slot(6)
                UU, b_UU = slot(7)
                WT, b_WT = slot(8)
                KDEC, b_KDEC = slot(9)
                QDT, b_QDT = slot(10)
                OT, b_OT = slot(11)
                SIL, b_SIL = slot(12)
                t3 = lambda ap: ap.rearrange("p (t n) -> p t n", t=8)
                KTOK3, VTOK3, QKTM3, UU3, KDEC3 = t3(KTOK), t3(VTOK), t3(QKTM), t3(UU), t3(KDEC)
                ring = []
                for si in (13, 14, 15):
                    ap_, _b = slot(si)
                    for q_ in range(8):
                        ring.append((ap_[:, q_ * 128:(q_ + 1) * 128], P.buf(f"ut{l}_{si}_{q_}"), _b))
                rstate = {"i": 0}

                def ut():
                    i = rstate["i"] % len(ring)
                    rstate["i"] += 1
                    return ring[i][0], ring[i][1]

                for si in (13, 14, 15):
                    ap_, _b = slot(si)
                    for q_ in range(8):
                        rb = ring[(si - 13) * 8 + q_][1]
                        rb.prev = list(_b.writers) + list(_b.readers)
                        rb.writers = []
                        rb.readers = []

                def conv_silu(ps_list, chunk_idx, dst, bdst, do_conv=True):
                    if not do_conv:
                        for th, (ps, bps) in enumerate(ps_list):
                            hs = slice(th * 512, (th + 1) * 512)
                            P.act(I("activation", out=F(dst[:, hs]), in_=ps, func=AF.Silu), r=[bps], w=[bdst])
                        return
                    for th, (ps, bps) in enumerate(ps_list):
                        hs = slice(th * 512, (th + 1) * 512)
                        P.act(I("activation", out=F(RAW[:, hs]), in_=ps, func=AF.Identity), r=[bps], w=[b_RAW])
                    w0 = wcd[:, chunk_idx:chunk_idx + 1]
                    w1 = wcd[:, 24 + chunk_idx:25 + chunk_idx]
                    w2 = wcd[:, 48 + chunk_idx:49 + chunk_idx]
                    P.dve(I("tensor_scalar", out=F(ACC), in0=RAW, scalar1=w1, scalar2=None, op0=ALU.mult),
                          r=[b_RAW, bVEC], w=[b_ACC])
                    P.dve(I("scalar_tensor_tensor", out=F(ACC[:, 1:T]), in0=RAW[:, 0:T - 1], scalar=w0, in1=ACC[:, 1:T],
                            op0=ALU.mult, op1=ALU.add), r=[b_RAW, bVEC, b_ACC], w=[b_ACC])
                    P.dve(I("scalar_tensor_tensor", out=F(ACC[:, 0:T - 1]), in0=RAW[:, 1:T], scalar=w2, in1=ACC[:, 0:T - 1],
                            op0=ALU.mult, op1=ALU.add), r=[b_RAW, bVEC, b_ACC], w=[b_ACC])
                    r4 = RAW.rearrange("p (s w) -> p s w", w=SEG)
                    a4 = ACC.rearrange("p (s w) -> p s w", w=SEG)
                    P.dve(I("scalar_tensor_tensor", out=F(a4[:, 1:4, 0:1]), in0=r4[:, 0:3, SEG - 1:SEG],
                            scalar=WCN[:, chunk_idx:chunk_idx + 1], in1=a4[:, 1:4, 0:1], op0=ALU.mult, op1=ALU.add),
                          r=[b_RAW, b_WCN, b_ACC], w=[b_ACC])
                    P.dve(I("scalar_tensor_tensor", out=F(a4[:, 0:3, SEG - 1:SEG]), in0=r4[:, 1:4, 0:1],
                            scalar=WCN[:, 48 + chunk_idx:49 + chunk_idx], in1=a4[:, 0:3, SEG - 1:SEG],
                            op0=ALU.mult, op1=ALU.add), r=[b_RAW, b_WCN, b_ACC], w=[b_ACC])
                    P.act(I("activation", out=F(dst), in_=ACC, func=AF.Silu), r=[b_ACC], w=[bdst])

                def l2n(src, bsrc, dst, bdst, mul):
                    for th in range(2):
                        hs = slice(th * 512, (th + 1) * 512)
                        rs, brs = rstd_of(lambda c: src[:, hs], [bsrc], 1, th, 1.0, "l2")
                        P.dve(I("scalar_tensor_tensor", out=F(dst[:, hs]), in0=src[:, hs], scalar=mul, in1=rs,
                                op0=ALU.mult, op1=ALU.mult), r=[bsrc, brs], w=[bdst])

                def to_tok(src, bsrc, dst3, bdst):
                    for g_ in range(2):
                        ps, bps = bank()
                        for q_ in range(4):
                            tt = 4 * g_ + q_
                            P.pe(I("transpose", out=ps[:, q_ * 128:(q_ + 1) * 128], in_=src[:, tt * 128:(tt + 1) * 128],
                                   identity=ident), r=[bsrc, bCON], w=[bps])
                        P.act(I("activation", out=F(dst3[:, 4 * g_:4 * g_ + 4, :]),
                                in_=ps.rearrange("p (t n) -> p t n", t=4), func=AF.Identity), r=[bps], w=[bdst])

                def prep_gen(h):
                    wqk, bwqk = next_block("dn_qk", h)
                    wvg, bwvg = next_block("dn_vg", h)
                    pl = [proj(wqk, bwqk, 0, A, bA, th) for th in range(2)]
                    conv_silu(pl, h, SIL, b_SIL)
                    yield
                    l2n(SIL, b_SIL, QT, b_QT, 1.0 / math.sqrt(128.0))
                    yield
                    pl = [proj(wqk, bwqk, 128, A, bA, th) for th in range(2)]
                    conv_silu(pl, 8 + h, SIL, b_SIL)
                    yield
                    l2n(SIL, b_SIL, KT, b_KT, 1.0)
                    yield
                    to_tok(KT, b_KT, KTOK3, b_KTOK)
                    yield
                    pl = [proj(wvg, bwvg, 0, A, bA, th) for th in range(2)]
                    conv_silu(pl, 16 + h, SIL, b_SIL)
                    yield
                    to_tok(SIL, b_SIL, VTOK3, b_VTOK)
                    yield
                    pl = [proj(wvg, bwvg, 128, A, bA, th) for th in range(2)]
                    conv_silu(pl, 0, B[:, h, :], bB[h], do_conv=False)
                    yield

                def drain(g):
                    if g is not None:
                        for _ in g:
                            pass

                nxt = prep_gen(0)
                for h in range(8):
                    drain(nxt)
                    nxt = prep_gen(h + 1) if h < 7 else None

                    for d in range(2 if DN_STAGE >= 3 else 0):
                        dh = d * 8 + h
                        ni = cst("ni_f" if d == 0 else "ni_b")
                        ns = cst("ns_f" if d == 0 else "ns_b")
                        nsT = cst("ns_b" if d == 0 else "ns_f")
                        for tt in range(8):
                            cs = slice(tt * 128, (tt + 1) * 128)
                            gcp = GC[:, tt, dh:dh + 1]
                            di = rr["el"] % 2
                            rr["el"] += 1
                            dg, bdg = DIAG[:, di, :], b_DIAG[di]
                            P.dve(I("tensor_scalar", out=dg, in0=ident, scalar1=gcp, scalar2=None, op0=ALU.mult),
                                  r=[bCON, b_GC], w=[bdg])
                            G_, bG = bank()
                            P.pe(mm(G_[:, 0:128], cst("ones"), dg, True, True), r=[bCON, bdg], w=[bG])
                            Gv = G_[:, 0:128]
                            tA, btA = ut()
                            P.dve(I("scalar_tensor_tensor", out=F(tA), in0=Gv, scalar=gcp, in1=ns, op0=ALU.subtract,
                                    op1=ALU.subtract), r=[bG, b_GC, bCON], w=[btA])
                            P.act(I("activation", out=F(tA), in_=tA, func=AF.Exp, scale=-1.0), r=[btA], w=[btA])
                            tQ, btQ = ut()
                            P.dve(I("scalar_tensor_tensor", out=F(tQ), in0=Gv, scalar=gcp, in1=ni, op0=ALU.subtract,
                                    op1=ALU.add), r=[bG, b_GC, bCON], w=[btQ])
                            P.act(I("activation", out=F(tQ), in_=tQ, func=AF.Exp), r=[btQ], w=[btQ])
                            eg, beg = ut()
                            P.act(I("activation", out=F(eg), in_=Gv, func=AF.Exp), r=[bG], w=[beg])
                            P.dve(I("tensor_tensor", out=F(QDT[:, cs]), in0=QT[:, cs], in1=eg, op=ALU.mult),
                                   r=[b_QT, beg], w=[b_QDT])
                            if DN_SUB < 2:
                                continue
                            kk, bkk = bank()
                            P.pe(mm(kk[:, 0:128], F(KT[:, cs]), F(KT[:, cs]), True, True), r=[b_KT], w=[bkk])
                            Am, bAm = ut()
                            P.dve(I("scalar_tensor_tensor", out=F(Am), in0=kk[:, 0:128], scalar=BETA[:, tt, dh:dh + 1],
                                    in1=tA, op0=ALU.mult, op1=ALU.mult), r=[bkk, b_BETA, btA], w=[bAm])
                            kq, bkq = bank()
                            P.pe(mm(kq[:, 0:128], F(KT[:, cs]), F(QT[:, cs]), True, True), r=[b_KT, b_QT], w=[bkq])
                            P.dve(I("tensor_tensor", out=F(QKTM3[:, tt, :]), in0=kq[:, 0:128], in1=tQ, op=ALU.mult),
                                  r=[bkq, btQ], w=[b_QKTM])
                            if DN_SUB < 3:
                                continue
                            di2 = rr["el"] % 2
                            rr["el"] += 1
                            dg2, bdg2 = DIAG[:, di2, :], b_DIAG[di2]
                            P.dve(I("tensor_scalar", out=dg2, in0=ident, scalar1=BETA[:, tt, dh:dh + 1], scalar2=None,
                                    op0=ALU.mult), r=[bCON, b_BETA], w=[bdg2])
                            Bw, bBw = bank()
                            P.pe(mm(Bw[:, 0:128], cst("ones"), dg2, True, True), r=[bCON, bdg2], w=[bBw])
                            tN, btN = ut()
                            P.dve(I("scalar_tensor_tensor", out=F(tN), in0=Gv, scalar=gcp, in1=nsT, op0=ALU.subtract,
                                    op1=ALU.add), r=[bG, b_GC, bCON], w=[btN])
                            P.act(I("activation", out=F(tN), in_=tN, func=AF.Exp), r=[btN], w=[btN])
                            P.dve(I("tensor_tensor", out=F(tN), in0=kk[:, 0:128], in1=tN, op=ALU.mult),
                                  r=[bkk, btN], w=[btN])
                            Nm, bNm = ut()
                            P.dve(I("tensor_tensor", out=F(Nm), in0=Bw[:, 0:128], in1=tN, op=ALU.mult),
                                  r=[bBw, btN], w=[bNm])
                            Rm, bRm = ut()
                            P.dve(I("tensor_tensor", out=F(Rm), in0=ident, in1=Nm, op=ALU.subtract),
                                  r=[bCON, bNm], w=[bRm])
                            Np, bNp, Ap, bAp = Nm, bNm, Am, bAm
                            for step in range(DN_STEPS):
                                asq, basq = bank()
                                P.pe(mm(asq[:, 0:128], F(Np), F(Ap), True, True), r=[bNp, bAp], w=[basq])
                                An, bAn = ut()
                                P.act(I("activation", out=F(An), in_=asq[:, 0:128], func=AF.Identity), r=[basq], w=[bAn])
                                if step < DN_STEPS - 1:
                                    n2, bn2 = bank()
                                    P.pe(mm(n2[:, 0:128], F(Ap), F(Np), True, True), r=[bNp, bAp], w=[bn2])
                                    Nn, bNn = ut()
                                    P.dve(I("tensor_scalar", out=F(Nn), in0=n2[:, 0:128], scalar1=1.0, scalar2=None, op0=ALU.mult), r=[bn2], w=[bNn])
                                r2, br2 = bank()
                                P.pe(mm(r2[:, 0:128], F(An), F(Rm), True, True), r=[bAn, bRm], w=[br2])
                                Rn, bRn = ut()
                                P.dve(I("tensor_tensor", out=F(Rn), in0=r2[:, 0:128], in1=Rm, op=ALU.add),
                                      r=[br2, bRm], w=[bRn])
                                Rm, bRm = Rn, bRn
                                Ap, bAp = An, bAn
                                if step < DN_STEPS - 1:
                                    Np, bNp = Nn, bNn
                            if DN_SUB < 4:
                                continue
                            bv, bbv = ut()
                            P.dve(I("tensor_scalar", out=F(bv), in0=VTOK3[:, tt, :], scalar1=BETA[:, tt, dh:dh + 1],
                                     scalar2=None, op0=ALU.mult), r=[b_VTOK, b_BETA], w=[bbv])
                            up, bup = bank()
                            P.pe(mm(up[:, 0:128], F(Rm), F(bv), True, True), r=[bRm, bbv], w=[bup])
                            P.act(I("activation", out=F(UU3[:, tt, :]), in_=up[:, 0:128], func=AF.Identity), r=[bup], w=[b_UU])
                            kb, bkb = ut()
                            P.dve(I("tensor_scalar", out=F(kb), in0=KTOK3[:, tt, :], scalar1=BGC[:, tt, dh:dh + 1],
                                     scalar2=None, op0=ALU.mult), r=[b_KTOK, b_BGC], w=[bkb])
                            wp_, bwp_ = bank()
                            P.pe(mm(wp_[:, 0:128], F(kb), F(Rm), True, True), r=[bRm, bkb], w=[bwp_])
                            P.act(I("activation", out=F(WT[:, cs]), in_=wp_[:, 0:128], func=AF.Identity), r=[bwp_], w=[b_WT])
                            P.dve(I("tensor_scalar", out=F(KDEC3[:, tt, :]), in0=KTOK3[:, tt, :],
                                     scalar1=EKD[:, tt, dh:dh + 1], scalar2=None, op0=ALU.mult),
                                   r=[b_KTOK, b_EKD], w=[b_KDEC])
                        cur = 0
                        Sb = [(SST[:, d, 0, :], b_SST[d][0]), (SST[:, d, 1, :], b_SST[d][1])]
                        P.dma("pool", Sb[0][0], d_s0[:, l, dh, :], w=[Sb[0][1]])
                        order = list(range(16)) if d == 0 else list(range(15, -1, -1))
                        if DN_STAGE < 4:
                            order = []
                        for ci, c in enumerate(order):
                            tt, r0 = c // 2, (c % 2) * 64
                            S_, bS = Sb[cur]
                            Sn, bSn = Sb[1 - cur]
                            if ci > 0 and ci % 4 == 0:
                                P.dve(I("tensor_scalar", out=S_, in0=S_, scalar1=vec("fl"), scalar2=None, op0=ALU.mult),
                                      r=[bS, bVEC], w=[bS])
                            cs = slice(tt * 128, (tt + 1) * 128)
                            ccs = slice(c * 64, (c + 1) * 64)
                            ws_, bws = bank()
                            P.pe(mm(ws_[:, 0:128], WT[:, cs], S_, True, True), r=[b_WT, bS], w=[bws])
                            vn, bvn = ut()
                            P.dve(I("tensor_tensor", out=F(vn[r0:r0 + 64, :]), in0=UU3[r0:r0 + 64, tt, :],
                                    in1=ws_[r0:r0 + 64, 0:128], op=ALU.subtract), r=[b_UU, bws], w=[bvn])
                            op_, bop = bank()
                            P.pe(mm(op_[:, 0:64], S_, QDT[:, ccs], True, False), r=[bS, b_QDT], w=[bop])
                            P.pe(mm(op_[:, 0:64], vn[r0:r0 + 64, :], QKTM3[r0:r0 + 64, tt, r0:r0 + 64], False, True),
                                 r=[bvn, b_QKTM], w=[bop])
                            if d == 0:
                                P.act(I("activation", out=F(OT[:, ccs]), in_=op_[:, 0:64], func=AF.Identity),
                                      r=[bop], w=[b_OT])
                            else:
                                P.dve(I("tensor_tensor", out=F(OT[:, ccs]), in0=op_[:, 0:64], in1=OT[:, ccs], op=ALU.add),
                                      r=[bop, b_OT], w=[b_OT])
                            sp_, bsp = bank()
                            P.pe(mm(sp_[:, 0:128], KDEC3[r0:r0 + 64, tt, :], vn[r0:r0 + 64, :], True, True),
                                 r=[b_KDEC, bvn], w=[bsp])
                            P.dve(I("scalar_tensor_tensor", out=Sn, in0=S_, scalar=GLB[:, c, dh:dh + 1], in1=sp_[:, 0:128],
                                    op0=ALU.mult, op1=ALU.add), r=[bS, b_GLB, bsp], w=[bSn])
                            if ci % 4 == 3:
                                P.dma("pool", d_st[:, l, dh, c // 4, :], Sn, r=[bSn], sem_buf=bSn, is_output=True)
                            cur = 1 - cur
                            if d == 1 and nxt is not None and ci >= 1:
                                next(nxt, None)
                    for th in range(2):
                        hs = slice(th * 512, (th + 1) * 512)
                        rs, brs = rstd_of(lambda c: OT[:, hs], [b_OT], 1, th, 1.0 / 128.0, "dno")
                        t_ap, bt = tmp()
                        P.dve(I("scalar_tensor_tensor", out=t_ap, in0=OT[:, hs], scalar=vec(f"g_dn{l}"), in1=rs,
                                op0=ALU.mult, op1=ALU.mult), r=[b_OT, bVEC, brs], w=[bt])
                        P.dve(I("tensor_tensor", out=F(B[:, h, hs]), in0=t_ap, in1=B[:, h, hs], op=ALU.mult),
                              r=[bt, bB[h]], w=[bB[h]])
                for (_ap, rb, _b) in ring:
                    _b.readers.extend(rb.writers + rb.readers)
                merge_branch(0, "br_dn")

            if enable_sc:
                RAW = SCR[:, 0, :]
                rawp = SCR[:, 0:2, :].rearrange("p a t -> p (a t)")[:, 0:NSEG * 258].rearrange("p (s w) -> p s w", s=NSEG)
                b_raw = bSCR[0]
                acc = SCR[:, 2, :].rearrange("p (s w) -> p s w", s=NSEG)
                b_acc = bSCR[2]
                gcp = SCR[:, 3, :]
                b_gcp = bSCR[3]
                if True:
                    zsrc = cst("ones")[:, 0:4].rearrange("p (s w) -> p s w", w=1)
                    P.dve(I("tensor_scalar", out=rawp[:, :, 0:1].bitcast(F32R), in0=zsrc, scalar1=0.0, scalar2=None,
                                                    op0=ALU.mult), r=[bCON], w=[bSCR[0], bSCR[1]])
                    P.dve(I("tensor_scalar", out=rawp[:, :, 257:258].bitcast(F32R), in0=zsrc, scalar1=0.0,
                                                    scalar2=None, op0=ALU.mult), r=[bCON], w=[bSCR[0], bSCR[1]])
                wc = vec(f"wc_sc{l}")
                for jp in range(4):
                    blkx = next_block("sc_x", jp)
                    xslot = wstate["slot"]
                    for cc in range(2):
                        j = 2 * jp + cc
                        wbg, bwbg = next_block("sc_bg", j, avoid=xslot)
                        wx, bwx = blkx
                        for th in range(2):
                            hs = slice(th * 512, (th + 1) * 512)
                            ps, bps = proj(wbg, bwbg, 128, A, bA, th)
                            P.act(I("activation", out=gcp[:, hs].bitcast(F32R), in_=ps, func=AF.Identity),
                                  r=[bps], w=[b_gcp])
                        for th in range(2):
                            ps, bps = proj(wx, bwx, cc * 128, A, bA, th)
                            P.dve(I("tensor_tensor",
                                out=rawp[:, 2 * th:2 * th + 2, 1:257].bitcast(F32R),
                                in0=ps.rearrange("p (s w) -> p s w", s=2),
                                in1=gcp[:, th * 512:(th + 1) * 512].rearrange("p (s w) -> p s w", s=2),
                                op=ALU.mult), r=[bps, b_gcp], w=[bSCR[0], bSCR[1]])
                        P.dve(I("tensor_scalar", out=rawp[:, 1:4, 0:1].bitcast(F32R), in0=rawp[:, 0:3, 256:257],
                                                        scalar1=vec("fl"), scalar2=None, op0=ALU.mult),
                              r=[bSCR[0], bSCR[1], bVEC], w=[bSCR[0], bSCR[1]])
                        P.dve(I("tensor_scalar", out=rawp[:, 0:3, 257:258].bitcast(F32R), in0=rawp[:, 1:4, 1:2],
                                                        scalar1=vec("fl"), scalar2=None, op0=ALU.mult),
                              r=[bSCR[0], bSCR[1], bVEC], w=[bSCR[0], bSCR[1]])
                        P.dve(I("tensor_scalar", out=acc.bitcast(F32R), in0=rawp[:, :, 1:257], scalar1=wc[:, 8 + j:9 + j],
                                                             scalar2=None, op0=ALU.mult), r=[bSCR[0], bSCR[1], bVEC], w=[b_acc])
                        P.dve(I("scalar_tensor_tensor", out=acc.bitcast(F32R), in0=rawp[:, :, 0:256], scalar=wc[:, j:j + 1],
                                                                    in1=acc, op0=ALU.mult, op1=ALU.add),
                              r=[bSCR[0], bSCR[1], bVEC, b_acc], w=[b_acc])
                        P.dve(I("scalar_tensor_tensor", out=acc.bitcast(F32R), in0=rawp[:, :, 2:258],
                                                                    scalar=wc[:, 16 + j:17 + j],
                                                                    in1=acc, op0=ALU.mult, op1=ALU.add),
                              r=[bSCR[0], bSCR[1], bVEC, b_acc], w=[b_acc])
                        accf = SCR[:, 2, :]
                        for th in range(2):
                            hs = slice(th * 512, (th + 1) * 512)
                            ps, bps = proj(wbg, bwbg, 0, A, bA, th)
                            P.dve(I("tensor_tensor",
                                out=B[:, j, hs].bitcast(F32R), in0=ps, in1=accf[:, hs], op=ALU.mult),
                                r=[bps, b_acc], w=[bB[j]])
                merge_branch(1, "br_sc")

            if enable_att:
                KV5 = SCR[:, 2:7, :].rearrange("p a t -> p (a t)")
                KT = KV5[:, 0:2 * NKEY].rearrange("p (k n) -> p k n", k=2)
                b_KT = [bSCR[2], bSCR[3], bSCR[4]]
                VV = KV5[:, 2 * NKEY:2 * NKEY + NKT * 256].rearrange("p (k n) -> p k n", k=NKT)
                b_VV = [bSCR[4], bSCR[5], bSCR[6]]
                QR = SCR[:, 0, :]
                b_QR = bSCR[0]
                cosT = SCR[:, 1, :]
                sinT = SCR[:, 7, :]
                P.dma("pool", cosT.bitcast(F32R), d_rope[:, 0, :].bitcast(F32R), w=[bSCR[1]])
                P.dma("pool", sinT.bitcast(F32R), d_rope[:, 1, :].bitcast(F32R), w=[bSCR[7]])
                P.dma("pool", KT[:, :, 0:PAST].bitcast(F32R), d_ckT[:, l, :, :].bitcast(F32R), w=b_KT)
                P.dma("pool", VV[:, 0:2, :].bitcast(F32R), d_cv[:, l, :, :].bitcast(F32R), w=b_VV)
                rotT_r = cst("rotT")

                def norm_rope(wblk, bw, col0, gname, dst_fn, bdst, kout=None):
                    for th in range(2):
                        hs = slice(th * 512, (th + 1) * 512)
                        ps, bps = proj(wblk, bw, col0, A, bA, th)
                        qraw, bqraw = tmp()
                        P.act(I("activation", out=qraw, in_=ps, func=AF.Identity),
                              r=[bps], w=[bqraw])
                        rs, brs = rstd_of(lambda c, qraw=qraw: qraw, [bqraw], 1, th, 1.0 / HD, "qk")
                        qn, bqn = tmp()
                        P.dve(I("scalar_tensor_tensor",
                            out=qn, in0=qraw, scalar=vec(gname), in1=rs, op0=ALU.mult, op1=ALU.mult),
                            r=[bqraw, bVEC, brs], w=[bqn])
                        if kout is not None:
                            P.dma("pool", kout[:, hs], qn, r=[bqn], sem_buf=bqn, is_output=True)
                        rp, brp = bank()
                        P.pe(mm(rp, rotT_r, qn, True, True), r=[bqn, bCON], w=[brp])
                        t1, bt1 = tmp()
                        P.dve(I("tensor_tensor", out=t1, in0=rp, in1=sinT[:, hs], op=ALU.mult),
                              r=[brp, bSCR[7]], w=[bt1])
                        t2, bt2 = tmp()
                        P.pool(I("tensor_tensor", out=t2, in0=qn, in1=cosT[:, hs], op=ALU.mult),
                               r=[bqn, bSCR[1]], w=[bt2])
                        P.dve(I("tensor_tensor", out=dst_fn(hs).bitcast(F32R), in0=t1, in1=t2,
                                                                            op=ALU.add), r=[bt1, bt2], w=bdst)

                wk, bwk = next_block("ak", None)
                for kv in range(2):
                    norm_rope(wk, bwk, kv * 128, f"g_k{l}",
                              lambda hs, kv=kv: KT[:, kv, PAST + hs.start:PAST + hs.stop], b_KT,
                              kout=d_kout[:, l, kv, :])
                wv, bwv = next_block("av", None)
                for tt in range(8):
                    ps, bps = bank()
                    for k in range(NCK):
                        P.pe(mm(ps[:, 0:256], A[:, k, tt * 128:(tt + 1) * 128].bitcast(F32R),
                                wv[:, k, :].bitcast(F32R), k == 0, k == NCK - 1), r=[bwv, bA[k]], w=[bps])
                    P.act(I("activation", out=VV[:, 2 + tt, :].bitcast(F32R), in_=ps[:, 0:256], func=AF.Identity),
                          r=[bps], w=b_VV)
                P.dma("pool", d_vout[:, l, :, :], VV[:, 2:10, :], r=b_VV, sem_buf=b_VV[0], is_output=True)
                sm_scale = 1.0 / math.sqrt(HD)
                for hp in range(4):
                    wq, bwq = next_block("aq", hp)
                    for cc in range(2):
                        h = 2 * hp + cc
                        kv = h // 4
                        norm_rope(wq, bwq, cc * 128, f"g_q{l}", lambda hs: QR[:, hs], [b_QR])
                        for th in range(2):
                            hs = slice(th * 512, (th + 1) * 512)
                            ops_, bops = held_bank(0)
                            den, bden = held_bank(1)
                            for kt in range(NKT):
                                st, bst = bank()
                                P.pe(mm(st, KT[:, kv, kt * 128:(kt + 1) * 128].bitcast(F32R), QR[:, hs].bitcast(F32R),
                                        True, True), r=b_KT + [b_QR], w=[bst])
                                pt, ptb = sq()
                                if kt < 2:
                                    segs = [(0, 512, True)]
                                else:
                                    sk = (kt - 2) // 2
                                    segs = []
                                    for s_ in (2 * th, 2 * th + 1):
                                        segs.append(((s_ - 2 * th) * 256, (s_ - 2 * th + 1) * 256, s_ != sk))
                                    if segs[0][2] == segs[1][2]:
                                        segs = [(0, 512, segs[0][2])]
                                for (lo, hi, masked) in segs:
                                    if masked:
                                        P.act(I("activation",
                                            out=pt[:, lo:hi].bitcast(F32R), in_=st[:, lo:hi], func=AF.Exp,
                                            bias=vec("nb"), scale=sm_scale), r=[bst, bVEC], w=[ptb])
                                    else:
                                        P.act(I("activation",
                                            out=pt[:, lo:hi].bitcast(F32R), in_=st[:, lo:hi], func=AF.Exp,
                                            scale=sm_scale), r=[bst], w=[ptb])
                                P.pe(mm(ops_, VV[:, kt, kv * 128:(kv + 1) * 128].bitcast(F32R), pt.bitcast(F32R),
                                        kt == 0, kt == NKT - 1), r=b_VV + [ptb], w=[bops])
                                P.pe(mm(den, ones_r, pt.bitcast(F32R), kt == 0, kt == NKT - 1), r=[ptb, bCON], w=[bden])
                            rd, brd = tmp()
                            P.dve(I("reciprocal", out=rd, in_=den), r=[bden], w=[brd])
                            P.dve(I("tensor_tensor",
                                out=B[:, h, hs].bitcast(F32R), in0=ops_, in1=rd, op=ALU.mult), r=[bops, brd], w=[bB[h]])
                merge_branch(2, "br_att")

            for j in range(4):
                wo, bwo = next_block("out", j)
                for cc in range(2):
                    c = 2 * j + cc
                    for th in range(2):
                        hs = slice(th * 512, (th + 1) * 512)
                        ps, bps = proj(wo, bwo, cc * 128, Cc, bC, th)
                        P.act(I("activation", out=B[:, c, hs].bitcast(F32R), in_=ps, func=AF.Identity),
                              r=[bps], w=[bB[c]])
            post_norm_add(B, bB, g1, bg1)

            pre_norm(a2, ba2, 24)
            for q in range(4):
                for j in range(4):
                    wu, bwu = next_block("up", (q, j))
                    for cc in range(2):
                        c = 2 * j + cc
                        for th in range(2):
                            hs = slice(th * 512, (th + 1) * 512)
                            ps, bps = proj(wu, bwu, cc * 128, A, bA, th)
                            t_ap, bt = tmp()
                            P.act(I("activation", out=t_ap, in_=ps, func=AF.Relu),
                                  r=[bps], w=[bt])
                            ew()(I("tensor_tensor",
                                out=B[:, c, hs].bitcast(F32R), in0=t_ap, in1=t_ap, op=ALU.mult), r=[bt], w=[bB[c]])
                for j in range(4):
                    wd, bwd = next_block("down", (q, j))
                    for cc in range(2):
                        c = 2 * j + cc
                        for th in range(2):
                            hs = slice(th * 512, (th + 1) * 512)
                            ps, bps = proj(wd, bwd, cc * 128, B, bB, th)
                            if q == 0:
                                P.act(I("activation", out=Cc[:, c, hs].bitcast(F32R), in_=ps, func=AF.Identity),
                                      r=[bps], w=[bC[c]])
                            else:
                                P.dve(I("tensor_tensor",
                                    out=Cc[:, c, hs].bitcast(F32R), in0=ps, in1=Cc[:, c, hs], op=ALU.add), r=[bps, bC[c]], w=[bC[c]])
            post_norm_add(Cc, bC, g2, bg2)

        for c in range(NCK):
            P.dma("pool", d_y[:, c, :], X[:, c, :], r=[bX[c]], sem_buf=P.buf(f"yout{c}"), is_output=True)

        with nc.Block() as block:
            P.emit(block)
    return nc


def _pm(v):
    v = np.asarray(v, np.float32)
    return np.ascontiguousarray(v.reshape(-1, 128).T)


def _rope_tables(sample):
    tab = np.zeros((128, 2, T), np.float32)
    if not sample:
        tab[:, 0, :] = 1.0
        return tab
    t = np.arange(T)
    row = (t // 64).astype(np.float32)
    col = (t % 64).astype(np.float32)
    freqs = (np.float32(10000.0) ** (-np.arange(32, dtype=np.float32) / np.float32(32))).astype(np.float32)
    for d in range(128):
        pos = row if d // 64 == 0 else col
        ang = (pos * freqs[(d % 64) % 32]).astype(np.float32)
        tab[d, 0, :] = np.cos(ang)
        tab[d, 1, :] = np.sin(ang)
    return tab


def prepare_inputs(inp, depth=DEPTH, enable_dn=True):
    V = make_vec_layout(depth)
    blocks = layer_blocks(enable_dn)
    ws = np.empty((len(blocks) * depth, 2, 128, NCK * 128), np.float32)
    i = 0
    for l in range(depth):
        lw = {k: np.asarray(inp[k][l], np.float32) for k in
              ("w_mod", "w_in", "w_br_dn", "w_br_sc", "w_br_att", "w_out", "w_mlp_up", "w_mlp_down")}
        for kind, arg in blocks:
            blk = build_block(kind, arg, lw)
            ws[i] = np.asarray(blk, np.float32).reshape(NCK, 128, 2, 128).transpose(2, 1, 0, 3).reshape(2, 128, NCK * 128)
            i += 1
    consts = build_consts()

    def vecs_for(cond, sample):
        a = np.zeros((128, V.n), np.float32)
        a[:, V.sl("cond")] = _pm(cond)
        a[:, V.sl("fl")] = 1.0 if sample else 0.0
        a[:, V.sl("nb")] = 0.0 if sample else NEG
        a[:, V.sl("nfl")] = 0.0 if sample else -1.0
        for l in range(depth):
            a[:, V.sl(f"bmod{l}")] = _pm(inp["b_mod"][l])
            for g in ("g_pre_mix", "g_post_mix", "g_pre_mlp", "g_post_mlp"):
                a[:, V.sl(f"{g}{l}")] = _pm(inp[g][l])
            wc = np.asarray(inp["w_conv_dn"][l], np.float32)
            a[:, V.sl(f"wc_dn{l}")] = np.concatenate([_pm(wc[k]) for k in range(3)], axis=1)
            wc = np.asarray(inp["w_conv_sc"][l], np.float32)
            a[:, V.sl(f"wc_sc{l}")] = np.concatenate([_pm(wc[k]) for k in range(3)], axis=1)
            a[:, V.sl(f"g_dn{l}")] = np.asarray(inp["g_dn_out"][l], np.float32)[:, None]
            a[:, V.sl(f"g_q{l}")] = np.asarray(inp["g_q"][l], np.float32)[:, None]
            a[:, V.sl(f"g_k{l}")] = np.asarray(inp["g_k"][l], np.float32)[:, None]
            for nm, key in (("alog", "a_log"), ("dtb", "dt_bias")):
                row = np.asarray(inp[key][l], np.float32).reshape(16)
                a[:, V.sl(f"{nm}{l}")] = np.broadcast_to(row[None, :], (128, 16))
        return a

    xp = np.asarray(inp["x_prompt"], np.float32)
    xs = np.asarray(inp["x_sample"], np.float32)
    maps = []
    for core in range(N_CORES):
        slot = core if core < 6 else 0
        sample = slot >= 4
        if sample:
            b = slot - 4
            xx = xs[b]
            cond = np.asarray(inp["c"], np.float32)[b]
            ck = np.asarray(inp["cache_k"], np.float32)[b, :depth]
            cvv = np.asarray(inp["cache_v"], np.float32)[b, :depth]
            st = np.asarray(inp["state_dn"], np.float32)[b, :depth]
            ckT = np.ascontiguousarray(ck.transpose(3, 0, 2, 1))
            cv = np.ascontiguousarray(cvv.reshape(depth, 2, 128, 256).transpose(2, 0, 1, 3))
            s0 = np.ascontiguousarray(st.reshape(depth, 16, 128, 128).transpose(2, 0, 1, 3))
        else:
            xx = xp[4 * slot:4 * slot + 4].reshape(T, D)
            cond = np.asarray(inp["c_ctx"], np.float32)
            ckT = np.zeros((128, depth, 2, PAST), np.float32)
            cv = np.zeros((128, depth, 2, 256), np.float32)
            s0 = np.zeros((128, depth, 16, 128), np.float32)
        xT = np.ascontiguousarray(xx.reshape(T, NCK, 128).transpose(2, 1, 0))
        maps.append({"xT": xT, "vecs": vecs_for(cond, sample), "consts": consts, "ws": ws,
                     "rope": _rope_tables(sample), "ckT": ckT, "cv": cv, "s0": s0})
    return maps


def assemble_outputs(results, depth=DEPTH):
    y_prompt = np.zeros((16, SEG, D), np.float32)
    y_sample = np.zeros((2, T, D), np.float32)
    nk = np.zeros((16, depth, SEG, 2, HD), np.float32)
    nv = np.zeros((16, depth, SEG, 2, HD), np.float32)
    ns = np.zeros((16, depth, 2, 8, 128, 128), np.float32)
    for slot in range(6):
        r = results[slot]
        y = np.asarray(r["yT"]).transpose(2, 1, 0).reshape(T, D)
        if slot >= 4:
            y_sample[slot - 4] = y
            continue
        y_prompt[4 * slot:4 * slot + 4] = y.reshape(4, SEG, D)
        kT = np.asarray(r["kT_out"])
        vo = np.asarray(r["v_out"])
        so = np.asarray(r["st_out"])
        kk = kT.transpose(1, 3, 2, 0).reshape(depth, 4, SEG, 2, HD)
        nk[4 * slot:4 * slot + 4] = kk.transpose(1, 0, 2, 3, 4)
        vv = vo.transpose(1, 2, 0, 3).reshape(depth, T, 2, HD).reshape(depth, 4, SEG, 2, HD)
        nv[4 * slot:4 * slot + 4] = vv.transpose(1, 0, 2, 3, 4)
        ss = so.transpose(3, 1, 2, 0, 4).reshape(4, depth, 2, 8, 128, 128)
        ns[4 * slot:4 * slot + 4] = ss
    return y_prompt, y_sample, nk, nv, ns


_PROG_CACHE = {}


def kernel(**inputs):
    key = "full"
    if key not in _PROG_CACHE:
        _PROG_CACHE[key] = build_program(enable_dn=ENABLE_DN)
    nc = _PROG_CACHE[key]
    maps = prepare_inputs(inputs, enable_dn=ENABLE_DN)
    res = run_bass_kernel_spmd(nc, maps, core_ids=list(range(N_CORES)))
    return assemble_outputs(res.results)
```

```python
import math
from contextlib import ExitStack

import numpy as np
import concourse.bass as bass
import concourse.mybir as mybir
from concourse.bass_utils import run_bass_kernel_spmd

F32 = mybir.dt.float32
F32R = mybir.dt.float32r
AF = mybir.ActivationFunctionType
ALU = mybir.AluOpType

D = 1024
T = 1024
NCK = 8
DEPTH = 4
NSEG = 4
SEG = 256
HD = 128
PAST = 256
NKEY = PAST + T
NKT = NKEY // 128
EPS = 1e-6
NEG = -30000.0
BLK = 256
N_CORES = 8
NW = 2
import os as _os
DN_STAGE = int(_os.environ.get("DN_STAGE", "9"))
DN_SUB = int(_os.environ.get("DN_SUB", "9"))
DN_STEPS = int(_os.environ.get("DN_STEPS", "5"))
DN_X = int(_os.environ.get("DN_X", "0"))
DEBUG_DUMPS = False
ENABLE_DN = True

IN_OFF = {}
_o = 0
for _n, _s in (("dq", 1024), ("dk", 1024), ("dv", 1024), ("dgate", 1024), ("dbeta", 16), ("dalpha", 16),
               ("sb", 1024), ("scg", 1024), ("sx", 1024), ("aq", 1024), ("ak", 256), ("av", 256), ("mg", 3072)):
    IN_OFF[_n] = _o
    _o += _s
IN_TOTAL = _o


def _cols(name, start, n):
    return list(range(IN_OFF[name] + start, IN_OFF[name] + start + n))


def layer_blocks(enable_dn=True):
    blocks = []
    for m in range(24):
        blocks.append(("mod", m))
    if enable_dn:
        blocks.append(("gates", None))
        for h in range(8):
            blocks.append(("dn_qk", h))
            blocks.append(("dn_vg", h))
        for j in range(4):
            blocks.append(("mg", (0, j)))
            blocks.append(("br_dn", j))
    for jp in range(4):
        blocks.append(("sc_x", jp))
        blocks.append(("sc_bg", 2 * jp))
        blocks.append(("sc_bg", 2 * jp + 1))
    for j in range(4):
        blocks.append(("mg", (1, j)))
        blocks.append(("br_sc", j))
    blocks.append(("ak", None))
    blocks.append(("av", None))
    for hp in range(4):
        blocks.append(("aq", hp))
    for j in range(4):
        blocks.append(("mg", (2, j)))
        blocks.append(("br_att", j))
    for j in range(4):
        blocks.append(("out", j))
    for q in range(4):
        for j in range(4):
            blocks.append(("up", (q, j)))
        for j in range(4):
            blocks.append(("down", (q, j)))
    return blocks


def build_block(kind, arg, lw):
    w_in = lw["w_in"]
    if kind == "mod":
        return lw["w_mod"][:, arg * 256:(arg + 1) * 256]
    if kind == "gates":
        blk = np.zeros((1024, 256), np.float32)
        b0, a0 = IN_OFF["dbeta"], IN_OFF["dalpha"]
        blk[:, 0:16] = w_in[:, b0:b0 + 16]
        blk[:, 16:32] = w_in[:, a0:a0 + 16]
        return blk
    if kind == "dn_qk":
        return w_in[:, _cols("dq", arg * 128, 128) + _cols("dk", arg * 128, 128)]
    if kind == "dn_vg":
        return w_in[:, _cols("dv", arg * 128, 128) + _cols("dgate", arg * 128, 128)]
    if kind == "mg":
        br, j = arg
        return w_in[:, _cols("mg", br * 1024 + j * 256, 256)]
    if kind == "br_dn":
        return lw["w_br_dn"][:, arg * 256:(arg + 1) * 256]
    if kind == "br_sc":
        return lw["w_br_sc"][:, arg * 256:(arg + 1) * 256]
    if kind == "br_att":
        return lw["w_br_att"][:, arg * 256:(arg + 1) * 256]
    if kind == "sc_bg":
        return w_in[:, _cols("sb", arg * 128, 128) + _cols("scg", arg * 128, 128)]
    if kind == "sc_x":
        return w_in[:, _cols("sx", arg * 256, 256)]
    if kind == "ak":
        return w_in[:, _cols("ak", 0, 256)]
    if kind == "av":
        return w_in[:, _cols("av", 0, 256)]
    if kind == "aq":
        return w_in[:, _cols("aq", arg * 256, 256)]
    if kind == "out":
        return lw["w_out"][:, arg * 256:(arg + 1) * 256]
    if kind == "up":
        q, j = arg
        return lw["w_mlp_up"][:, q * 1024 + j * 256: q * 1024 + (j + 1) * 256]
    if kind == "down":
        q, j = arg
        return lw["w_mlp_down"][q * 1024:(q + 1) * 1024, j * 256:(j + 1) * 256]
    raise ValueError(kind)


class VecLayout:
    def __init__(self):
        self.off = {}
        self.n = 0

    def add(self, name, width):
        self.off[name] = (self.n, width)
        self.n += width

    def sl(self, name):
        o, w = self.off[name]
        return slice(o, o + w)


def make_vec_layout(depth):
    V = VecLayout()
    V.add("cond", 8)
    V.add("fl", 1)
    V.add("nb", 1)
    V.add("nfl", 1)
    for l in range(depth):
        V.add(f"bmod{l}", 48)
        for g in ("g_pre_mix", "g_post_mix", "g_pre_mlp", "g_post_mlp"):
            V.add(f"{g}{l}", 8)
        V.add(f"wc_dn{l}", 72)
        V.add(f"wc_sc{l}", 24)
        V.add(f"g_dn{l}", 1)
        V.add(f"g_q{l}", 1)
        V.add(f"g_k{l}", 1)
        V.add(f"alog{l}", 16)
        V.add(f"dtb{l}", 16)
    return V


class ConstLayout(VecLayout):
    pass


def make_const_layout():
    C = ConstLayout()
    for nm in ("ident", "ones", "rotT", "ni_f", "ni_b", "ns_f", "ns_b", "tri_f", "tri_b",
               "last_f", "last_b", "all_f0", "all_f1", "all_b0", "all_b1"):
        C.add(nm, 128)
    return C


def build_consts():
    C = make_const_layout()
    a = np.zeros((128, C.n), np.float32)
    a[:, C.sl("ident")] = np.eye(128, dtype=np.float32)
    a[:, C.sl("ones")] = 1.0
    R = np.zeros((128, 128), np.float32)
    for d in range(128):
        idx = d % 64
        if idx < 32:
            R[d, d + 32] = -1.0
        else:
            R[d, d - 32] = 1.0
    a[:, C.sl("rotT")] = R.T
    p = np.arange(128)[:, None]
    f = np.arange(128)[None, :]
    same = (p // 64) == (f // 64)
    a[:, C.sl("ni_f")] = np.where(same & (f >= p), 0.0, NEG)
    a[:, C.sl("ni_b")] = np.where(same & (f <= p), 0.0, NEG)
    a[:, C.sl("ns_f")] = np.where(same & (f < p), 0.0, NEG)
    a[:, C.sl("ns_b")] = np.where(same & (f > p), 0.0, NEG)
    a[:, C.sl("tri_f")] = np.where(same & (f >= p), 1.0, 0.0)
    a[:, C.sl("tri_b")] = np.where(same & (f <= p), 1.0, 0.0)
    cend = (f // 64) * 64 + 63
    cstart = (f // 64) * 64
    a[:, C.sl("last_f")] = (p == cend).astype(np.float32)
    a[:, C.sl("last_b")] = (p == cstart).astype(np.float32)
    a[:, C.sl("all_f0")] = (p == 63).astype(np.float32) * np.ones((1, 128), np.float32)
    a[:, C.sl("all_f1")] = (p == 127).astype(np.float32) * np.ones((1, 128), np.float32)
    a[:, C.sl("all_b0")] = (p == 0).astype(np.float32) * np.ones((1, 128), np.float32)
    a[:, C.sl("all_b1")] = (p == 64).astype(np.float32) * np.ones((1, 128), np.float32)
    return a


def I(method, **kw):
    return lambda e: getattr(e, method)(**kw)


class WB:
    def __init__(self, b0, b1):
        self.bufs = [b0, b1]


def _flat(lst):
    out = []
    for b in lst:
        if isinstance(b, WB):
            out.extend(b.bufs)
        else:
            out.append(b)
    return out


class Buf:
    __slots__ = ("name", "writers", "readers", "prev", "sem", "dma_n")

    def __init__(self, name):
        self.name = name
        self.writers = []
        self.readers = []
        self.prev = []
        self.sem = None
        self.dma_n = 0


class Op:
    __slots__ = ("eng", "fn", "waits", "milestone", "count", "dma_sem", "is_dma", "epoch")

    def __init__(self, eng, fn):
        self.eng = eng
        self.fn = fn
        self.epoch = 0
        self.waits = []
        self.milestone = False
        self.count = 0
        self.dma_sem = None
        self.is_dma = False


class Prog:
    ENGS = ("pe", "act", "dve", "pool", "sp")

    def __init__(self, nc, stack):
        self.nc = nc
        self.stack = stack
        self.ops = {e: [] for e in self.ENGS}
        self.sems = {}
        self.epoch = 0
        self.out_tokens = []

    def esem(self, eng, epoch):
        k = (eng, epoch)
        if k not in self.sems:
            self.sems[k] = self.stack.enter_context(self.nc.semaphore(f"s_{eng}_{epoch}"))
        return self.sems[k]

    def buf(self, name):
        return Buf(name)

    def _dma_sem(self, b):
        if b.sem is None:
            b.sem = self.stack.enter_context(self.nc.semaphore(f"d_{b.name}"))
        return b.sem

    def _add(self, eng, fn, r, w, dma_buf=None):
        op = Op(eng, fn)
        op.epoch = self.epoch
        r = _flat(r)
        w = _flat(w)
        waits = []
        for b in r:
            waits.extend(b.writers)
        for b in w:
            if b.readers:
                b.prev = b.writers + b.readers
                b.writers = []
                b.readers = []
            waits.extend(b.prev)
        seen = set()
        for tok in waits:
            if tok[0] == "e":
                o = tok[1]
                if o.eng == eng and eng in ("pe", "sp"):
                    continue
                if id(o) in seen:
                    continue
                seen.add(id(o))
                o.milestone = True
                op.waits.append(tok)
            else:
                key = (id(tok[1]), tok[2])
                if key in seen:
                    continue
                seen.add(key)
                op.waits.append(tok)
        if dma_buf is not None:
            op.is_dma = True
            op.dma_sem = self._dma_sem(dma_buf)
            dma_buf.dma_n += 1
            tok = ("d", op.dma_sem, 16 * dma_buf.dma_n)
        else:
            tok = ("e", op)
        for b in r:
            b.readers.append(tok)
        for b in w:
            b.writers.append(tok)
        self.ops[eng].append(op)
        return tok

    def pe(self, fn, r=(), w=()):
        return self._add("pe", fn, r, w)

    def act(self, fn, r=(), w=()):
        return self._add("act", fn, r, w)

    def dve(self, fn, r=(), w=()):
        return self._add("dve", fn, r, w)

    def pool(self, fn, r=(), w=()):
        return self._add("pool", fn, r, w)

    def dma(self, eng, out, in_, r=(), w=(), sem_buf=None, is_output=False):
        sb = sem_buf if sem_buf is not None else (w[0] if w else r[0])
        tok = self._add(eng, lambda e: e.dma_start(out=out, in_=in_), r, w, dma_buf=sb)
        if is_output:
            self.out_tokens.append(tok)
        return tok

    def emit(self, block):
        for e in self.ENGS:
            cnt = {}
            for op in self.ops[e]:
                if op.milestone:
                    cnt[op.epoch] = cnt.get(op.epoch, 0) + 1
                    op.count = cnt[op.epoch]
                    self.esem(e, op.epoch)
        final = {}
        for tok in self.out_tokens:
            k = id(tok[1])
            if k not in final or final[k][2] < tok[2]:
                final[k] = tok
        finals = list(final.values())
        handles = {"pe": block.tensor, "act": block.scalar, "dve": block.vector, "pool": block.gpsimd,
                   "sp": block.sync}
        for e in self.ENGS:
            ops = self.ops[e]
            sems = self.sems

            def body(eng, ops=ops, e=e):
                seen = {}
                for op in ops:
                    need = {}
                    for tok in op.waits:
                        if tok[0] == "e":
                            o = tok[1]
                            s, v = sems[(o.eng, o.epoch)], o.count
                        else:
                            s, v = tok[1], tok[2]
                        k = id(s)
                        if k not in need or need[k][1] < v:
                            need[k] = (s, v)
                    for k, (s, v) in need.items():
                        if seen.get(k, 0) >= v:
                            continue
                        seen[k] = v
                        eng.wait_ge(s, v)
                    ins = op.fn(eng)
                    if op.is_dma:
                        ins.then_inc(op.dma_sem, 16)
                    elif op.milestone:
                        ins.then_inc(sems[(e, op.epoch)], 1)
                if e == "sp":
                    for tok in finals:
                        eng.wait_ge(tok[1], tok[2])

            handles[e](body)


def build_program(depth=DEPTH, enable_dn=True, enable_sc=True, enable_att=True):
    nc = bass.Bass("TRN2", target_bir_lowering=False)
    nc.dge_precook = False
    V = make_vec_layout(depth)
    C = make_const_layout()
    blocks_per_layer = layer_blocks(enable_dn)
    NB = len(blocks_per_layer) * depth

    d_x = nc.dram_tensor("xT", [128, NCK, T], F32, kind="ExternalInput").ap()
    d_vecs = nc.dram_tensor("vecs", [128, V.n], F32, kind="ExternalInput").ap()
    d_consts = nc.dram_tensor("consts", [128, C.n], F32, kind="ExternalInput").ap()
    d_ws = nc.dram_tensor("ws", [NB, 2, 128, NCK * 128], F32, kind="ExternalInput").ap()
    d_rope = nc.dram_tensor("rope", [128, 2, T], F32, kind="ExternalInput").ap()
    d_ckT = nc.dram_tensor("ckT", [128, depth, 2, PAST], F32, kind="ExternalInput").ap()
    d_cv = nc.dram_tensor("cv", [128, depth, 2, 256], F32, kind="ExternalInput").ap()
    d_s0 = nc.dram_tensor("s0", [128, depth, 16, 128], F32, kind="ExternalInput").ap()
    d_y = nc.dram_tensor("yT", [128, NCK, T], F32, kind="ExternalOutput").ap()
    d_kout = nc.dram_tensor("kT_out", [128, depth, 2, T], F32, kind="ExternalOutput").ap()
    d_vout = nc.dram_tensor("v_out", [128, depth, 8, 256], F32, kind="ExternalOutput").ap()
    d_st = nc.dram_tensor("st_out", [128, depth, 16, NSEG, 128], F32, kind="ExternalOutput").ap()
    if DEBUG_DUMPS:
        d_dbgA = nc.dram_tensor("dbgA", [128, NCK, T], F32, kind="ExternalOutput").ap()
        d_dbgM = nc.dram_tensor("dbgM", [128, 64], F32, kind="ExternalOutput").ap()

    with ExitStack() as stack:
        P = Prog(nc, stack)

        def sb(name, shape, dt=F32):
            return stack.enter_context(nc.sbuf_tensor(name, shape, dt))

        X = sb("X", [128, NCK, T])
        A = sb("A", [128, NCK, T])
        B = sb("B", [128, NCK, T])
        Cc = sb("C", [128, NCK, T])
        W = sb("W", [128, NW, NCK * BLK])
        VEC = sb("VEC", [128, V.n])
        CON = sb("CON", [128, C.n])
        SCR = sb("SCR", [128, 8, T])
        TMP = sb("TMP", [128, 4, 512])
        SQ = sb("SQ", [128, 2, 512])
        SMALL = sb("SMALL", [128, 256])
        DNS = sb("DNS", [128, 2008])
        _o = [0]

        def carve(n):
            ap_ = DNS[:, _o[0]:_o[0] + n]
            _o[0] += n
            return ap_
        GRAW = carve(256).rearrange("p (t n) -> p t n", t=8)
        BETA = carve(128).rearrange("p (t n) -> p t n", t=8)
        GC = carve(128).rearrange("p (t n) -> p t n", t=8)
        BGC = carve(128).rearrange("p (t n) -> p t n", t=8)
        EKD = carve(128).rearrange("p (t n) -> p t n", t=8)
        ZG = carve(128).rearrange("p (t n) -> p t n", t=8)
        GLB = carve(256).rearrange("p (c n) -> p c n", c=16)
        WCN = carve(72)
        NA = carve(16)
        DIAG = carve(256).rearrange("p (a n) -> p a n", a=2)
        SST = carve(512).rearrange("p (d a n) -> p d a n", d=2, a=2)
        b_GRAW, b_BETA, b_GC, b_BGC, b_EKD, b_ZG, b_GLB, b_WCN, b_NA = [P.buf(n_) for n_ in
            ("GRAW", "BETA", "GC", "BGC", "EKD", "ZG", "GLB", "WCN", "NA")]
        b_DIAG = [P.buf("DIAG0"), P.buf("DIAG1")]
        b_SST = [[P.buf("S00"), P.buf("S01")], [P.buf("S10"), P.buf("S11")]]
        PS = stack.enter_context(nc.psum_tensor("PS", [128, 8, 512], F32))

        bX = [P.buf(f"X{c}") for c in range(NCK)]
        bA = [P.buf(f"A{c}") for c in range(NCK)]
        bB = [P.buf(f"B{c}") for c in range(NCK)]
        bC = [P.buf(f"C{c}") for c in range(NCK)]
        bW = [WB(P.buf(f"W{i}a"), P.buf(f"W{i}b")) for i in range(NW)]
        bVEC = P.buf("VEC")
        bCON = P.buf("CON")
        bSCR = [P.buf(f"SCR{i}") for i in range(8)]
        bTMP = [P.buf(f"TMP{i}") for i in range(4)]
        bSQ = [P.buf(f"SQ{i}") for i in range(2)]
        bSM = {}
        bPS = [P.buf(f"PS{i}") for i in range(8)]
        rr = {"ps": 0, "tmp": 0, "sq": 0, "w": 0, "el": 0}

        def bank():
            i = rr["ps"] % 6
            rr["ps"] += 1
            return PS[:, i, :], bPS[i]

        def held_bank(i):
            return PS[:, 6 + i, :], bPS[6 + i]

        def tmp():
            i = rr["tmp"] % 4
            rr["tmp"] += 1
            return TMP[:, i, :], bTMP[i]

        def sq():
            i = rr["sq"] % 2
            rr["sq"] += 1
            return SQ[:, i, :], bSQ[i]

        def small(name, width):
            if name not in bSM:
                o = sum(w for (_, w, _) in bSM.values())
                assert o + width <= 256
                bSM[name] = (o, width, P.buf("sm_" + name))
            o, w, b = bSM[name]
            return SMALL[:, o:o + w], b

        def cst(name):
            return CON[:, C.sl(name)]

        def vec(name):
            return VEC[:, V.sl(name)]

        wstate = {"i": 0}

        def next_block(expect_kind=None, expect_arg=None, avoid=None):
            i = wstate["i"]
            kind, arg = blocks_per_layer[i % len(blocks_per_layer)]
            if expect_kind is not None:
                assert kind == expect_kind and arg == expect_arg, (kind, arg, expect_kind, expect_arg)
            s = (wstate.get("last", -1) + 1) % NW
            if avoid is not None and s == avoid:
                s = (s + 1) % NW
            wstate["last"] = s
            wstate["slot"] = s
            wv3 = W[:, s, :].rearrange("p (k n) -> p k n", k=NCK)
            for c_ in range(2):
                P.dma("sp", wv3[:, :, c_ * 128:(c_ + 1) * 128].bitcast(F32R),
                      d_ws[i, c_].rearrange("p (k n) -> p k n", k=NCK).bitcast(F32R), w=[bW[s].bufs[c_]])
            wstate["i"] = i + 1
            return W[:, s, :].rearrange("p (k n) -> p k n", k=NCK), bW[s]

        def ew():
            rr["el"] += 1
            return P.dve if rr["el"] % 2 else P.pool

        P.dma("pool", VEC[:], d_vecs, w=[bVEC])
        P.dma("pool", CON[:].bitcast(F32R), d_consts.bitcast(F32R), w=[bCON])
        for c in range(NCK):
            P.dma("act", X[:, c, :], d_x[:, c, :], w=[bX[c]])

        ident = cst("ident")
        ones_r = cst("ones").bitcast(F32R)

        scond, b_scond = small("scond", 8)
        P.act(I("activation", out=scond, in_=vec("cond"), func=AF.Silu), r=[bVEC], w=[b_scond])

        def mm(out, lhsT, rhs, start, stop):
            return lambda e: e.matmul(out, lhsT, rhs, start=start, stop=stop)

        def proj(wblk, bw, col0, src, bsrc, th, nk=NCK):
            ps, bps = bank()
            for k in range(nk):
                P.pe(mm(ps, wblk[:, k, col0:col0 + 128].bitcast(F32R),
                        src[:, k, th * 512:(th + 1) * 512].bitcast(F32R), k == 0, k == nk - 1),
                     r=[bw.bufs[col0 // 128] if isinstance(bw, WB) else bw] + list(bsrc[:nk]), w=[bps])
            return ps, bps

        def rstd_of(src_fn, bsrcs, nch, th, scale, out_name):
            ss, bss = bank()
            for c in range(nch):
                s_ap, bs = sq()
                src = src_fn(c)
                P.act(I("activation", out=s_ap.bitcast(F32R), in_=src, func=AF.Square),
                      r=[bsrcs[c]], w=[bs])
                P.pe(mm(ss, ones_r, s_ap.bitcast(F32R), c == 0, c == nch - 1), r=[bs, bCON], w=[bss])
            sd, bsd = tmp()
            P.act(I("activation", out=sd, in_=ss, func=AF.Sqrt, bias=EPS, scale=scale), r=[bss], w=[bsd])
            rs, brs = bank()
            P.dve(I("reciprocal", out=rs, in_=sd), r=[bsd], w=[brs])
            return rs, brs

        for l in range(depth):
            P.epoch = l
            modps, bmodps = bank()
            for m2 in range(24):
                wb, bw = next_block("mod", m2)
                for cc in range(2):
                    m = m2 * 2 + cc
                    for k in range(NCK):
                        P.pe(mm(modps[:, m:m + 1], wb[:, k, cc * 128:(cc + 1) * 128], scond[:, k:k + 1],
                                k == 0, k == NCK - 1), r=[bw.bufs[cc], b_scond], w=[bmodps])
            modv, bmodv = small("modv", 48)
            P.dve(I("tensor_tensor", out=modv, in0=modps[:, 0:48], in1=vec(f"bmod{l}"), op=ALU.add),
                  r=[bmodps, bVEC], w=[bmodv])
            a1, ba1 = small("a1", 8)
            a2, ba2 = small("a2", 8)
            g1, bg1 = small("g1", 8)
            g2, bg2 = small("g2", 8)
            P.dve(I("scalar_tensor_tensor", out=a1, in0=modv[:, 8:16], scalar=1.0, in1=vec(f"g_pre_mix{l}"),
                                                        op0=ALU.add, op1=ALU.mult), r=[bmodv, bVEC], w=[ba1])
            P.dve(I("scalar_tensor_tensor", out=a2, in0=modv[:, 32:40], scalar=1.0, in1=vec(f"g_pre_mlp{l}"),
                                                        op0=ALU.add, op1=ALU.mult), r=[bmodv, bVEC], w=[ba2])
            P.dve(I("tensor_tensor", out=g1, in0=modv[:, 16:24], in1=vec(f"g_post_mix{l}"), op=ALU.mult),
                  r=[bmodv, bVEC], w=[bg1])
            P.dve(I("tensor_tensor", out=g2, in0=modv[:, 40:48], in1=vec(f"g_post_mlp{l}"), op=ALU.mult),
                  r=[bmodv, bVEC], w=[bg2])

            def pre_norm(scale_ap, bscale, shift_lo):
                for th in range(2):
                    hs = slice(th * 512, (th + 1) * 512)
                    rs, brs = rstd_of(lambda c: X[:, c, hs], bX, NCK, th, 1.0 / D, "pre")
                    for c in range(NCK):
                        t_ap, bt = tmp()
                        P.dve(I("tensor_tensor", out=t_ap, in0=X[:, c, hs], in1=rs, op=ALU.mult),
                              r=[bX[c], brs], w=[bt])
                        P.act(I("activation",
                            out=A[:, c, hs].bitcast(F32R), in_=t_ap, func=AF.Identity,
                            scale=scale_ap[:, c:c + 1], bias=modv[:, shift_lo + c:shift_lo + c + 1]),
                            r=[bt, bscale, bmodv], w=[bA[c]])

            def post_norm_add(SRC, bSRC, gate_ap, bgate):
                for th in range(2):
                    hs = slice(th * 512, (th + 1) * 512)
                    rs, brs = rstd_of(lambda c: SRC[:, c, hs], bSRC, NCK, th, 1.0 / D, "post")
                    for c in range(NCK):
                        t_ap, bt = tmp()
                        P.dve(I("tensor_tensor", out=t_ap, in0=SRC[:, c, hs], in1=rs, op=ALU.mult),
                              r=[bSRC[c], brs], w=[bt])
                        P.dve(I("scalar_tensor_tensor",
                            out=X[:, c, hs], in0=t_ap, scalar=gate_ap[:, c:c + 1], in1=X[:, c, hs],
                            op0=ALU.mult, op1=ALU.add), r=[bt, bgate, bX[c]], w=[bX[c]])

            pre_norm(a1, ba1, 0)
            if DEBUG_DUMPS and l == 0:
                for c in range(NCK):
                    P.dma("pool", d_dbgA[:, c, :], A[:, c, :], r=[bA[c]], sem_buf=bA[c], is_output=True)
                P.dma("pool", d_dbgM[:, 0:48], modv, r=[bmodv], sem_buf=bmodv, is_output=True)
                P.dma("pool", d_dbgM[:, 48:56], scond, r=[b_scond], sem_buf=b_scond, is_output=True)
                P.dma("pool", d_dbgM[:, 56:64], a1, r=[ba1], sem_buf=ba1, is_output=True)

            first_branch = {"v": True}

            def merge_branch(br, wname):
                first = first_branch["v"]
                first_branch["v"] = False
                for j in range(4):
                    wg, bwg = next_block("mg", (br, j))
                    wp, bwp = next_block(wname, j)
                    for cc in range(2):
                        c = 2 * j + cc
                        for th in range(2):
                            hs = slice(th * 512, (th + 1) * 512)
                            ps1, bps1 = proj(wg, bwg, cc * 128, A, bA, th)
                            g_ap, bg = tmp()
                            P.act(I("activation", out=g_ap, in_=ps1, func=AF.Sigmoid),
                                  r=[bps1], w=[bg])
                            ps, bps = proj(wp, bwp, cc * 128, B, bB, th)
                            if first:
                                P.dve(I("tensor_tensor",
                                    out=Cc[:, c, hs].bitcast(F32R), in0=ps, in1=g_ap, op=ALU.mult), r=[bps, bg], w=[bC[c]])
                            else:
                                t_ap, bt = tmp()
                                P.dve(I("tensor_tensor",
                                    out=t_ap, in0=ps, in1=g_ap, op=ALU.mult), r=[bps, bg], w=[bt])
                                P.pool(I("tensor_tensor",
                                    out=Cc[:, c, hs].bitcast(F32R), in0=Cc[:, c, hs], in1=t_ap, op=ALU.add),
                                    r=[bt, bC[c]], w=[bC[c]])

            if enable_dn:
                F = lambda ap: ap.bitcast(F32R)
                wgt, bwgt = next_block("gates", None)
                for tt in range(8):
                    ps, bps = bank()
                    for k in range(NCK):
                        P.pe(mm(ps[:, 0:256], A[:, k, tt * 128:(tt + 1) * 128].bitcast(F32R),
                                wgt[:, k, :].bitcast(F32R), k == 0, k == NCK - 1), r=[bwgt, bA[k]], w=[bps])
                    P.act(I("activation", out=GRAW[:, tt, :], in_=ps[:, 0:32], func=AF.Identity), r=[bps], w=[b_GRAW])
                P.act(I("activation", out=BETA, in_=GRAW[:, :, 0:16], func=AF.Sigmoid), r=[b_GRAW], w=[b_BETA])
                for tt in range(8):
                    P.dve(I("tensor_tensor", out=ZG[:, tt, :], in0=GRAW[:, tt, 16:32], in1=vec(f"dtb{l}"), op=ALU.add),
                          r=[b_GRAW, bVEC], w=[b_ZG])
                P.act(I("activation", out=ZG, in_=ZG, func=AF.Exp), r=[b_ZG], w=[b_ZG])
                P.act(I("activation", out=ZG, in_=ZG, func=AF.Ln, bias=1.0), r=[b_ZG], w=[b_ZG])
                P.act(I("activation", out=NA, in_=vec(f"alog{l}"), func=AF.Exp), r=[bVEC], w=[b_NA])
                P.dve(I("tensor_scalar", out=NA, in0=NA, scalar1=-1.0, scalar2=None, op0=ALU.mult), r=[b_NA], w=[b_NA])
                for tt in range(8):
                    P.dve(I("tensor_tensor", out=ZG[:, tt, :], in0=ZG[:, tt, :], in1=NA, op=ALU.mult),
                          r=[b_ZG, b_NA], w=[b_ZG])
                ps, bps = bank()
                psv = ps[:, 0:256].rearrange("p (t d n) -> p t d n", t=8, d=2)
                for tt in range(8):
                    P.pe(mm(psv[:, tt, 0, :], cst("tri_f"), ZG[:, tt, :], True, True), r=[bCON, b_ZG], w=[bps])
                    P.pe(mm(psv[:, tt, 1, :], cst("tri_b"), ZG[:, tt, :], True, True), r=[bCON, b_ZG], w=[bps])
                P.act(I("activation", out=GC[:, :, 0:8], in_=psv[:, :, 0, 0:8], func=AF.Identity), r=[bps], w=[b_GC])
                P.act(I("activation", out=GC[:, :, 8:16], in_=psv[:, :, 1, 8:16], func=AF.Identity), r=[bps], w=[b_GC])
                ps, bps = bank()
                psv = ps[:, 0:256].rearrange("p (t d n) -> p t d n", t=8, d=2)
                for tt in range(8):
                    P.pe(mm(psv[:, tt, 0, :], cst("last_f"), GC[:, tt, :], True, True), r=[bCON, b_GC], w=[bps])
                    P.pe(mm(psv[:, tt, 1, :], cst("last_b"), GC[:, tt, :], True, True), r=[bCON, b_GC], w=[bps])
                P.dve(I("tensor_tensor", out=EKD[:, :, 0:8], in0=psv[:, :, 0, 0:8], in1=GC[:, :, 0:8], op=ALU.subtract),
                      r=[bps, b_GC], w=[b_EKD])
                P.dve(I("tensor_tensor", out=EKD[:, :, 8:16], in0=psv[:, :, 1, 8:16], in1=GC[:, :, 8:16], op=ALU.subtract),
                      r=[bps, b_GC], w=[b_EKD])
                P.act(I("activation", out=EKD, in_=EKD, func=AF.Exp), r=[b_EKD], w=[b_EKD])
                ps, bps = bank()
                psv = ps[:, 0:512].rearrange("p (c d n) -> p c d n", c=16, d=2)
                for tt in range(8):
                    for cc in range(2):
                        P.pe(mm(psv[:, 2 * tt + cc, 0, :], cst(f"all_f{cc}"), GC[:, tt, :], True, True),
                             r=[bCON, b_GC], w=[bps])
                        P.pe(mm(psv[:, 2 * tt + cc, 1, :], cst(f"all_b{cc}"), GC[:, tt, :], True, True),
                             r=[bCON, b_GC], w=[bps])
                P.act(I("activation", out=GLB[:, :, 0:8], in_=psv[:, :, 0, 0:8], func=AF.Exp), r=[bps], w=[b_GLB])
                P.act(I("activation", out=GLB[:, :, 8:16], in_=psv[:, :, 1, 8:16], func=AF.Exp), r=[bps], w=[b_GLB])
                P.act(I("activation", out=BGC, in_=GC, func=AF.Exp), r=[b_GC], w=[b_BGC])
                P.dve(I("tensor_tensor", out=BGC, in0=BGC, in1=BETA, op=ALU.mult), r=[b_BGC, b_BETA], w=[b_BGC])
                P.dve(I("tensor_scalar", out=WCN, in0=vec(f"wc_dn{l}"), scalar1=vec("nfl"), scalar2=None, op0=ALU.mult),
                      r=[bVEC], w=[b_WCN])
                wcd = vec(f"wc_dn{l}")

                def slot(i):
                    return (Cc[:, i, :], bC[i]) if i < 8 else (SCR[:, i - 8, :], bSCR[i - 8])

                RAW, b_RAW = slot(0)
                ACC, b_ACC = slot(1)
                QT, b_QT = slot(2)
                KT, b_KT = slot(3)
                KTOK, b_KTOK = slot(4)
                VTOK, b_VTOK = slot(5)
                QKTM, b_QKTM = slot(6)
                UU, b_UU = slot(7)
                WT, b_WT = slot(8)
                KDEC, b_KDEC = slot(9)
                QDT, b_QDT = slot(10)
                OT, b_OT = slot(11)
                SIL, b_SIL = slot(12)
                t3 = lambda ap: ap.rearrange("p (t n) -> p t n", t=8)
                tile_bufs = {}
                for nm_, sb_ in (("QKTM", b_QKTM), ("UU", b_UU), ("WT", b_WT), ("KDEC", b_KDEC), ("QDT", b_QDT)):
                    lst = [P.buf(f"{nm_}{l}_{t_}") for t_ in range(8)]
                    for b_ in lst:
                        b_.prev = list(sb_.writers) + list(sb_.readers)
                    tile_bufs[nm_] = (lst, sb_)
                tb_QKTM, tb_UU, tb_WT, tb_KDEC, tb_QDT = [tile_bufs[n_][0] for n_ in ("QKTM", "UU", "WT", "KDEC", "QDT")]
                KTOK3, VTOK3, QKTM3, UU3, KDEC3 = t3(KTOK), t3(VTOK), t3(QKTM), t3(UU), t3(KDEC)
                ring = []
                for si in (13, 14, 15):
                    ap_, _b = slot(si)
                    for q_ in range(8):
                        ring.append((ap_[:, q_ * 128:(q_ + 1) * 128], P.buf(f"ut{l}_{si}_{q_}"), _b))
                rstate = {"i": 0}

                def ut():
                    i = rstate["i"] % len(ring)
                    rstate["i"] += 1
                    return ring[i][0], ring[i][1]

                for si in (13, 14, 15):
                    ap_, _b = slot(si)
                    for q_ in range(8):
                        rb = ring[(si - 13) * 8 + q_][1]
                        rb.prev = list(_b.writers) + list(_b.readers)
                        rb.writers = []
                        rb.readers = []

                def conv_silu(ps_list, chunk_idx, dst, bdst, do_conv=True):
                    if not do_conv:
                        for th, (ps, bps) in enumerate(ps_list):
                            hs = slice(th * 512, (th + 1) * 512)
                            P.act(I("activation", out=F(dst[:, hs]), in_=ps, func=AF.Silu), r=[bps], w=[bdst])
                        return
                    for th, (ps, bps) in enumerate(ps_list):
                        hs = slice(th * 512, (th + 1) * 512)
                        P.act(I("activation", out=F(RAW[:, hs]), in_=ps, func=AF.Identity), r=[bps], w=[b_RAW])
                    w0 = wcd[:, chunk_idx:chunk_idx + 1]
                    w1 = wcd[:, 24 + chunk_idx:25 + chunk_idx]
                    w2 = wcd[:, 48 + chunk_idx:49 + chunk_idx]
                    P.dve(I("tensor_scalar", out=F(ACC), in0=RAW, scalar1=w1, scalar2=None, op0=ALU.mult),
                          r=[b_RAW, bVEC], w=[b_ACC])
                    P.dve(I("scalar_tensor_tensor", out=F(ACC[:, 1:T]), in0=RAW[:, 0:T - 1], scalar=w0, in1=ACC[:, 1:T],
                            op0=ALU.mult, op1=ALU.add), r=[b_RAW, bVEC, b_ACC], w=[b_ACC])
                    P.dve(I("scalar_tensor_tensor", out=F(ACC[:, 0:T - 1]), in0=RAW[:, 1:T], scalar=w2, in1=ACC[:, 0:T - 1],
                            op0=ALU.mult, op1=ALU.add), r=[b_RAW, bVEC, b_ACC], w=[b_ACC])
                    r4 = RAW.rearrange("p (s w) -> p s w", w=SEG)
                    a4 = ACC.rearrange("p (s w) -> p s w", w=SEG)
                    P.dve(I("scalar_tensor_tensor", out=F(a4[:, 1:4, 0:1]), in0=r4[:, 0:3, SEG - 1:SEG],
                            scalar=WCN[:, chunk_idx:chunk_idx + 1], in1=a4[:, 1:4, 0:1], op0=ALU.mult, op1=ALU.add),
                          r=[b_RAW, b_WCN, b_ACC], w=[b_ACC])
                    P.dve(I("scalar_tensor_tensor", out=F(a4[:, 0:3, SEG - 1:SEG]), in0=r4[:, 1:4, 0:1],
                            scalar=WCN[:, 48 + chunk_idx:49 + chunk_idx], in1=a4[:, 0:3, SEG - 1:SEG],
                            op0=ALU.mult, op1=ALU.add), r=[b_RAW, b_WCN, b_ACC], w=[b_ACC])
                    P.act(I("activation", out=F(dst), in_=ACC, func=AF.Silu), r=[b_ACC], w=[bdst])

                def l2n(src, bsrc, dst, bdst, mul):
                    for th in range(2):
                        hs = slice(th * 512, (th + 1) * 512)
                        rs, brs = rstd_of(lambda c: src[:, hs], [bsrc], 1, th, 1.0, "l2")
                        P.dve(I("scalar_tensor_tensor", out=F(dst[:, hs]), in0=src[:, hs], scalar=mul, in1=rs,
                                op0=ALU.mult, op1=ALU.mult), r=[bsrc, brs], w=[bdst])

                def to_tok(src, bsrc, dst3, bdst):
                    for g_ in range(2):
                        ps, bps = bank()
                        for q_ in range(4):
                            tt = 4 * g_ + q_
                            P.pe(I("transpose", out=ps[:, q_ * 128:(q_ + 1) * 128], in_=src[:, tt * 128:(tt + 1) * 128],
                                   identity=ident), r=[bsrc, bCON], w=[bps])
                        P.act(I("activation", out=F(dst3[:, 4 * g_:4 * g_ + 4, :]),
                                in_=ps.rearrange("p (t n) -> p t n", t=4), func=AF.Identity), r=[bps], w=[bdst])

                def prep_gen(h):
                    wqk, bwqk = next_block("dn_qk", h)
                    wvg, bwvg = next_block("dn_vg", h)
                    pl = [proj(wqk, bwqk, 0, A, bA, th) for th in range(2)]
                    conv_silu(pl, h, SIL, b_SIL)
                    yield
                    l2n(SIL, b_SIL, QT, b_QT, 1.0 / math.sqrt(128.0))
                    yield
                    pl = [proj(wqk, bwqk, 128, A, bA, th) for th in range(2)]
                    conv_silu(pl, 8 + h, SIL, b_SIL)
                    yield
                    l2n(SIL, b_SIL, KT, b_KT, 1.0)
                    yield
                    to_tok(KT, b_KT, KTOK3, b_KTOK)
                    yield
                    pl = [proj(wvg, bwvg, 0, A, bA, th) for th in range(2)]
                    conv_silu(pl, 16 + h, SIL, b_SIL)
                    yield
                    to_tok(SIL, b_SIL, VTOK3, b_VTOK)
                    yield
                    pl = [proj(wvg, bwvg, 128, A, bA, th) for th in range(2)]
                    conv_silu(pl, 0, B[:, h, :], bB[h], do_conv=False)
                    yield

                def drain(g):
                    if g is not None:
                        for _ in g:
                            pass

                nxt = prep_gen(0)
                for h in range(8):
                    drain(nxt)
                    nxt = prep_gen(h + 1) if h < 7 else None

                    def dir_gen(d):
                        dh = d * 8 + h
                        ni = cst("ni_f" if d == 0 else "ni_b")
                        ns = cst("ns_f" if d == 0 else "ns_b")
                        nsT = cst("ns_b" if d == 0 else "ns_f")
                        for tt in range(8):
                            cs = slice(tt * 128, (tt + 1) * 128)
                            gcp = GC[:, tt, dh:dh + 1]
                            di = rr["el"] % 2
                            rr["el"] += 1
                            dg, bdg = DIAG[:, di, :], b_DIAG[di]
                            P.dve(I("tensor_scalar", out=dg, in0=ident, scalar1=gcp, scalar2=None, op0=ALU.mult),
                                  r=[bCON, b_GC], w=[bdg])
                            G_, bG = bank()
                            P.pe(mm(G_[:, 0:128], cst("ones"), dg, True, True), r=[bCON, bdg], w=[bG])
                            Gv = G_[:, 0:128]
                            tA, btA = ut()
                            P.dve(I("scalar_tensor_tensor", out=F(tA), in0=Gv, scalar=gcp, in1=ns, op0=ALU.subtract,
                                    op1=ALU.subtract), r=[bG, b_GC, bCON], w=[btA])
                            P.act(I("activation", out=F(tA), in_=tA, func=AF.Exp, scale=-1.0), r=[btA], w=[btA])
                            tQ, btQ = ut()
                            P.dve(I("scalar_tensor_tensor", out=F(tQ), in0=Gv, scalar=gcp, in1=ni, op0=ALU.subtract,
                                    op1=ALU.add), r=[bG, b_GC, bCON], w=[btQ])
                            P.act(I("activation", out=F(tQ), in_=tQ, func=AF.Exp), r=[btQ], w=[btQ])
                            eg, beg = ut()
                            P.act(I("activation", out=F(eg), in_=Gv, func=AF.Exp), r=[bG], w=[beg])
                            P.dve(I("tensor_tensor", out=F(QDT[:, cs]), in0=QT[:, cs], in1=eg, op=ALU.mult),
                                   r=[b_QT, beg], w=[tb_QDT[tt]])
                            if DN_SUB < 2:
                                continue
                            kk, bkk = bank()
                            P.pe(mm(kk[:, 0:128], F(KT[:, cs]), F(KT[:, cs]), True, True), r=[b_KT], w=[bkk])
                            Am, bAm = ut()
                            P.dve(I("scalar_tensor_tensor", out=F(Am), in0=kk[:, 0:128], scalar=BETA[:, tt, dh:dh + 1],
                                    in1=tA, op0=ALU.mult, op1=ALU.mult), r=[bkk, b_BETA, btA], w=[bAm])
                            kq, bkq = bank()
                            P.pe(mm(kq[:, 0:128], F(KT[:, cs]), F(QT[:, cs]), True, True), r=[b_KT, b_QT], w=[bkq])
                            P.dve(I("tensor_tensor", out=F(QKTM3[:, tt, :]), in0=kq[:, 0:128], in1=tQ, op=ALU.mult),
                                  r=[bkq, btQ], w=[tb_QKTM[tt]])
                            if DN_SUB < 3:
                                continue
                            di2 = rr["el"] % 2
                            rr["el"] += 1
                            dg2, bdg2 = DIAG[:, di2, :], b_DIAG[di2]
                            P.dve(I("tensor_scalar", out=dg2, in0=ident, scalar1=BETA[:, tt, dh:dh + 1], scalar2=None,
                                    op0=ALU.mult), r=[bCON, b_BETA], w=[bdg2])
                            Bw, bBw = bank()
                            P.pe(mm(Bw[:, 0:128], cst("ones"), dg2, True, True), r=[bCON, bdg2], w=[bBw])
                            tN, btN = ut()
                            P.dve(I("scalar_tensor_tensor", out=F(tN), in0=Gv, scalar=gcp, in1=nsT, op0=ALU.subtract,
                                    op1=ALU.add), r=[bG, b_GC, bCON], w=[btN])
                            P.act(I("activation", out=F(tN), in_=tN, func=AF.Exp), r=[btN], w=[btN])
                            P.dve(I("tensor_tensor", out=F(tN), in0=kk[:, 0:128], in1=tN, op=ALU.mult),
                                  r=[bkk, btN], w=[btN])
                            Nm, bNm = ut()
                            P.dve(I("tensor_tensor", out=F(Nm), in0=Bw[:, 0:128], in1=tN, op=ALU.mult),
                                  r=[bBw, btN], w=[bNm])
                            Rm, bRm = ut()
                            P.dve(I("tensor_tensor", out=F(Rm), in0=ident, in1=Nm, op=ALU.subtract),
                                  r=[bCON, bNm], w=[bRm])
                            yield
                            Np, bNp, Ap, bAp = Nm, bNm, Am, bAm
                            for step in range(DN_STEPS):
                                asq, basq = bank()
                                P.pe(mm(asq[:, 0:128], F(Np), F(Ap), True, True), r=[bNp, bAp], w=[basq])
                                An, bAn = ut()
                                P.act(I("activation", out=F(An), in_=asq[:, 0:128], func=AF.Identity), r=[basq], w=[bAn])
                                if step < DN_STEPS - 1:
                                    n2, bn2 = bank()
                                    P.pe(mm(n2[:, 0:128], F(Ap), F(Np), True, True), r=[bNp, bAp], w=[bn2])
                                    Nn, bNn = ut()
                                    P.dve(I("tensor_scalar", out=F(Nn), in0=n2[:, 0:128], scalar1=1.0, scalar2=None, op0=ALU.mult), r=[bn2], w=[bNn])
                                r2, br2 = bank()
                                P.pe(mm(r2[:, 0:128], F(An), F(Rm), True, True), r=[bAn, bRm], w=[br2])
                                Rn, bRn = ut()
                                P.dve(I("tensor_tensor", out=F(Rn), in0=r2[:, 0:128], in1=Rm, op=ALU.add),
                                      r=[br2, bRm], w=[bRn])
                                Rm, bRm = Rn, bRn
                                Ap, bAp = An, bAn
                                if step < DN_STEPS - 1:
                                    Np, bNp = Nn, bNn
                                yield
                            if DN_SUB < 4:
                                continue
                            yield
                            bv, bbv = ut()
                            P.dve(I("tensor_scalar", out=F(bv), in0=VTOK3[:, tt, :], scalar1=BETA[:, tt, dh:dh + 1],
                                     scalar2=None, op0=ALU.mult), r=[b_VTOK, b_BETA], w=[bbv])
                            up, bup = bank()
                            P.pe(mm(up[:, 0:128], F(Rm), F(bv), True, True), r=[bRm, bbv], w=[bup])
                            P.act(I("activation", out=F(UU3[:, tt, :]), in_=up[:, 0:128], func=AF.Identity), r=[bup], w=[tb_UU[tt]])
                            kb, bkb = ut()
                            P.dve(I("tensor_scalar", out=F(kb), in0=KTOK3[:, tt, :], scalar1=BGC[:, tt, dh:dh + 1],
                                     scalar2=None, op0=ALU.mult), r=[b_KTOK, b_BGC], w=[bkb])
                            wp_, bwp_ = bank()
                            P.pe(mm(wp_[:, 0:128], F(kb), F(Rm), True, True), r=[bRm, bkb], w=[bwp_])
                            P.act(I("activation", out=F(WT[:, cs]), in_=wp_[:, 0:128], func=AF.Identity), r=[bwp_], w=[tb_WT[tt]])
                            P.dve(I("tensor_scalar", out=F(KDEC3[:, tt, :]), in0=KTOK3[:, tt, :],
                                     scalar1=EKD[:, tt, dh:dh + 1], scalar2=None, op0=ALU.mult),
                                   r=[b_KTOK, b_EKD], w=[tb_KDEC[tt]])
                            yield "unit_done"
                        cur = 0
                        Sb = [(SST[:, d, 0, :], b_SST[d][0]), (SST[:, d, 1, :], b_SST[d][1])]
                        P.dma("pool", Sb[0][0], d_s0[:, l, dh, :], w=[Sb[0][1]])
                        order = list(range(16)) if d == 0 else list(range(15, -1, -1))
                        if DN_STAGE < 4:
                            order = []
                        for ci, c in enumerate(order):
                            tt, r0 = c // 2, (c % 2) * 64
                            S_, bS = Sb[cur]
                            Sn, bSn = Sb[1 - cur]
                            if ci > 0 and ci % 4 == 0:
                                P.dve(I("tensor_scalar", out=S_, in0=S_, scalar1=vec("fl"), scalar2=None, op0=ALU.mult),
                                      r=[bS, bVEC], w=[bS])
                            cs = slice(tt * 128, (tt + 1) * 128)
                            ccs = slice(c * 64, (c + 1) * 64)
                            ws_, bws = bank()
                            P.pe(mm(ws_[:, 0:128], WT[:, cs], S_, True, True), r=[tb_WT[tt], bS], w=[bws])
                            vn, bvn = ut()
                            P.dve(I("tensor_tensor", out=F(vn[r0:r0 + 64, :]), in0=UU3[r0:r0 + 64, tt, :],
                                    in1=ws_[r0:r0 + 64, 0:128], op=ALU.subtract), r=[tb_UU[tt], bws], w=[bvn])
                            op_, bop = bank()
                            P.pe(mm(op_[:, 0:64], S_, QDT[:, ccs], True, False), r=[bS, tb_QDT[tt]], w=[bop])
                            P.pe(mm(op_[:, 0:64], vn[r0:r0 + 64, :], QKTM3[r0:r0 + 64, tt, r0:r0 + 64], False, True),
                                 r=[bvn, tb_QKTM[tt]], w=[bop])
                            if d == 0:
                                P.act(I("activation", out=F(OT[:, ccs]), in_=op_[:, 0:64], func=AF.Identity),
                                      r=[bop], w=[b_OT])
                            else:
                                P.dve(I("tensor_tensor", out=F(OT[:, ccs]), in0=op_[:, 0:64], in1=OT[:, ccs], op=ALU.add),
                                      r=[bop, b_OT], w=[b_OT])
                            sp_, bsp = bank()
                            P.pe(mm(sp_[:, 0:128], KDEC3[r0:r0 + 64, tt, :], vn[r0:r0 + 64, :], True, True),
                                 r=[tb_KDEC[tt], bvn], w=[bsp])
                            P.dve(I("scalar_tensor_tensor", out=Sn, in0=S_, scalar=GLB[:, c, dh:dh + 1], in1=sp_[:, 0:128],
                                    op0=ALU.mult, op1=ALU.add), r=[bS, b_GLB, bsp], w=[bSn])
                            if ci % 4 == 3:
                                P.dma("pool", d_st[:, l, dh, c // 4, :], Sn, r=[bSn], sem_buf=bSn, is_output=True)
                            cur = 1 - cur
                            if d == 1 and nxt is not None and ci >= 1:
                                next(nxt, None)
                            yield "step_done"
                    def run_until(g, tag, n):
                        k_ = 0
                        for v_ in g:
                            if v_ == tag:
                                k_ += 1
                                if k_ >= n:
                                    return True
                        return False

                    gen0, gen1 = dir_gen(0), dir_gen(1)
                    run_until(gen0, "unit_done", 8)
                    run_until(gen0, "step_done", 2)
                    for tt_ in range(8):
                        steps_left = 2 if tt_ < 7 else 0
                        unit_alive = True
                        while steps_left > 0 or unit_alive:
                            if steps_left > 0:
                                run_until(gen0, "step_done", 1)
                                steps_left -= 1
                            if unit_alive:
                                for _ in range(3):
                                    v_ = next(gen1, "unit_done")
                                    if v_ == "unit_done":
                                        unit_alive = False
                                        break
                    for _ in gen0:
                        pass
                    for _ in gen1:
                        pass
                    for th in range(2):
                        hs = slice(th * 512, (th + 1) * 512)
                        rs, brs = rstd_of(lambda c: OT[:, hs], [b_OT], 1, th, 1.0 / 128.0, "dno")
                        t_ap, bt = tmp()
                        P.dve(I("scalar_tensor_tensor", out=t_ap, in0=OT[:, hs], scalar=vec(f"g_dn{l}"), in1=rs,
                                op0=ALU.mult, op1=ALU.mult), r=[b_OT, bVEC, brs], w=[bt])
                        P.dve(I("tensor_tensor", out=F(B[:, h, hs]), in0=t_ap, in1=B[:, h, hs], op=ALU.mult),
                              r=[bt, bB[h]], w=[bB[h]])
                for (_ap, rb, _b) in ring:
                    _b.readers.extend(rb.writers + rb.readers)
                for (lst, sb_) in tile_bufs.values():
                    for b_ in lst:
                        sb_.readers.extend(b_.writers + b_.readers)
                merge_branch(0, "br_dn")

            if enable_sc:
                RAW = SCR[:, 0, :]
                rawp = SCR[:, 0:2, :].rearrange("p a t -> p (a t)")[:, 0:NSEG * 258].rearrange("p (s w) -> p s w", s=NSEG)
                b_raw = bSCR[0]
                acc = SCR[:, 2, :].rearrange("p (s w) -> p s w", s=NSEG)
                b_acc = bSCR[2]
                gcp = SCR[:, 3, :]
                b_gcp = bSCR[3]
                if True:
                    zsrc = cst("ones")[:, 0:4].rearrange("p (s w) -> p s w", w=1)
                    P.dve(I("tensor_scalar", out=rawp[:, :, 0:1].bitcast(F32R), in0=zsrc, scalar1=0.0, scalar2=None,
                                                    op0=ALU.mult), r=[bCON], w=[bSCR[0], bSCR[1]])
                    P.dve(I("tensor_scalar", out=rawp[:, :, 257:258].bitcast(F32R), in0=zsrc, scalar1=0.0,
                                                    scalar2=None, op0=ALU.mult), r=[bCON], w=[bSCR[0], bSCR[1]])
                wc = vec(f"wc_sc{l}")
                for jp in range(4):
                    blkx = next_block("sc_x", jp)
                    xslot = wstate["slot"]
                    for cc in range(2):
                        j = 2 * jp + cc
                        wbg, bwbg = next_block("sc_bg", j, avoid=xslot)
                        wx, bwx = blkx
                        for th in range(2):
                            hs = slice(th * 512, (th + 1) * 512)
                            ps, bps = proj(wbg, bwbg, 128, A, bA, th)
                            P.act(I("activation", out=gcp[:, hs].bitcast(F32R), in_=ps, func=AF.Identity),
                                  r=[bps], w=[b_gcp])
                        for th in range(2):
                            ps, bps = proj(wx, bwx, cc * 128, A, bA, th)
                            P.dve(I("tensor_tensor",
                                out=rawp[:, 2 * th:2 * th + 2, 1:257].bitcast(F32R),
                                in0=ps.rearrange("p (s w) -> p s w", s=2),
                                in1=gcp[:, th * 512:(th + 1) * 512].rearrange("p (s w) -> p s w", s=2),
                                op=ALU.mult), r=[bps, b_gcp], w=[bSCR[0], bSCR[1]])
                        P.dve(I("tensor_scalar", out=rawp[:, 1:4, 0:1].bitcast(F32R), in0=rawp[:, 0:3, 256:257],
                                                        scalar1=vec("fl"), scalar2=None, op0=ALU.mult),
                              r=[bSCR[0], bSCR[1], bVEC], w=[bSCR[0], bSCR[1]])
                        P.dve(I("tensor_scalar", out=rawp[:, 0:3, 257:258].bitcast(F32R), in0=rawp[:, 1:4, 1:2],
                                                        scalar1=vec("fl"), scalar2=None, op0=ALU.mult),
                              r=[bSCR[0], bSCR[1], bVEC], w=[bSCR[0], bSCR[1]])
                        P.dve(I("tensor_scalar", out=acc.bitcast(F32R), in0=rawp[:, :, 1:257], scalar1=wc[:, 8 + j:9 + j],
                                                             scalar2=None, op0=ALU.mult), r=[bSCR[0], bSCR[1], bVEC], w=[b_acc])
                        P.dve(I("scalar_tensor_tensor", out=acc.bitcast(F32R), in0=rawp[:, :, 0:256], scalar=wc[:, j:j + 1],
                                                                    in1=acc, op0=ALU.mult, op1=ALU.add),
                              r=[bSCR[0], bSCR[1], bVEC, b_acc], w=[b_acc])
                        P.dve(I("scalar_tensor_tensor", out=acc.bitcast(F32R), in0=rawp[:, :, 2:258],
                                                                    scalar=wc[:, 16 + j:17 + j],
                                                                    in1=acc, op0=ALU.mult, op1=ALU.add),
                              r=[bSCR[0], bSCR[1], bVEC, b_acc], w=[b_acc])
                        accf = SCR[:, 2, :]
                        for th in range(2):
                            hs = slice(th * 512, (th + 1) * 512)
                            ps, bps = proj(wbg, bwbg, 0, A, bA, th)
                            P.dve(I("tensor_tensor",
                                out=B[:, j, hs].bitcast(F32R), in0=ps, in1=accf[:, hs], op=ALU.mult),
                                r=[bps, b_acc], w=[bB[j]])
                merge_branch(1, "br_sc")

            if enable_att:
                KV5 = SCR[:, 2:7, :].rearrange("p a t -> p (a t)")
                KT = KV5[:, 0:2 * NKEY].rearrange("p (k n) -> p k n", k=2)
                b_KT = [bSCR[2], bSCR[3], bSCR[4]]
                VV = KV5[:, 2 * NKEY:2 * NKEY + NKT * 256].rearrange("p (k n) -> p k n", k=NKT)
                b_VV = [bSCR[4], bSCR[5], bSCR[6]]
                QR = SCR[:, 0, :]
                b_QR = bSCR[0]
                cosT = SCR[:, 1, :]
                sinT = SCR[:, 7, :]
                P.dma("pool", cosT.bitcast(F32R), d_rope[:, 0, :].bitcast(F32R), w=[bSCR[1]])
                P.dma("pool", sinT.bitcast(F32R), d_rope[:, 1, :].bitcast(F32R), w=[bSCR[7]])
                P.dma("pool", KT[:, :, 0:PAST].bitcast(F32R), d_ckT[:, l, :, :].bitcast(F32R), w=b_KT)
                P.dma("pool", VV[:, 0:2, :].bitcast(F32R), d_cv[:, l, :, :].bitcast(F32R), w=b_VV)
                rotT_r = cst("rotT")

                def norm_rope(wblk, bw, col0, gname, dst_fn, bdst, kout=None):
                    for th in range(2):
                        hs = slice(th * 512, (th + 1) * 512)
                        ps, bps = proj(wblk, bw, col0, A, bA, th)
                        qraw, bqraw = tmp()
                        P.act(I("activation", out=qraw, in_=ps, func=AF.Identity),
                              r=[bps], w=[bqraw])
                        rs, brs = rstd_of(lambda c, qraw=qraw: qraw, [bqraw], 1, th, 1.0 / HD, "qk")
                        qn, bqn = tmp()
                        P.dve(I("scalar_tensor_tensor",
                            out=qn, in0=qraw, scalar=vec(gname), in1=rs, op0=ALU.mult, op1=ALU.mult),
                            r=[bqraw, bVEC, brs], w=[bqn])
                        if kout is not None:
                            P.dma("pool", kout[:, hs], qn, r=[bqn], sem_buf=bqn, is_output=True)
                        rp, brp = bank()
                        P.pe(mm(rp, rotT_r, qn, True, True), r=[bqn, bCON], w=[brp])
                        t1, bt1 = tmp()
                        P.dve(I("tensor_tensor", out=t1, in0=rp, in1=sinT[:, hs], op=ALU.mult),
                              r=[brp, bSCR[7]], w=[bt1])
                        t2, bt2 = tmp()
                        P.pool(I("tensor_tensor", out=t2, in0=qn, in1=cosT[:, hs], op=ALU.mult),
                               r=[bqn, bSCR[1]], w=[bt2])
                        P.dve(I("tensor_tensor", out=dst_fn(hs).bitcast(F32R), in0=t1, in1=t2,
                                                                            op=ALU.add), r=[bt1, bt2], w=bdst)

                wk, bwk = next_block("ak", None)
                for kv in range(2):
                    norm_rope(wk, bwk, kv * 128, f"g_k{l}",
                              lambda hs, kv=kv: KT[:, kv, PAST + hs.start:PAST + hs.stop], b_KT,
                              kout=d_kout[:, l, kv, :])
                wv, bwv = next_block("av", None)
                for tt in range(8):
                    ps, bps = bank()
                    for k in range(NCK):
                        P.pe(mm(ps[:, 0:256], A[:, k, tt * 128:(tt + 1) * 128].bitcast(F32R),
                                wv[:, k, :].bitcast(F32R), k == 0, k == NCK - 1), r=[bwv, bA[k]], w=[bps])
                    P.act(I("activation", out=VV[:, 2 + tt, :].bitcast(F32R), in_=ps[:, 0:256], func=AF.Identity),
                          r=[bps], w=b_VV)
                P.dma("pool", d_vout[:, l, :, :], VV[:, 2:10, :], r=b_VV, sem_buf=b_VV[0], is_output=True)
                sm_scale = 1.0 / math.sqrt(HD)
                for hp in range(4):
                    wq, bwq = next_block("aq", hp)
                    for cc in range(2):
                        h = 2 * hp + cc
                        kv = h // 4
                        norm_rope(wq, bwq, cc * 128, f"g_q{l}", lambda hs: QR[:, hs], [b_QR])
                        for th in range(2):
                            hs = slice(th * 512, (th + 1) * 512)
                            ops_, bops = held_bank(0)
                            den, bden = held_bank(1)
                            for kt in range(NKT):
                                st, bst = bank()
                                P.pe(mm(st, KT[:, kv, kt * 128:(kt + 1) * 128].bitcast(F32R), QR[:, hs].bitcast(F32R),
                                        True, True), r=b_KT + [b_QR], w=[bst])
                                pt, ptb = sq()
                                if kt < 2:
                                    segs = [(0, 512, True)]
                                else:
                                    sk = (kt - 2) // 2
                                    segs = []
                                    for s_ in (2 * th, 2 * th + 1):
                                        segs.append(((s_ - 2 * th) * 256, (s_ - 2 * th + 1) * 256, s_ != sk))
                                    if segs[0][2] == segs[1][2]:
                                        segs = [(0, 512, segs[0][2])]
                                for (lo, hi, masked) in segs:
                                    if masked:
                                        P.act(I("activation",
                                            out=pt[:, lo:hi].bitcast(F32R), in_=st[:, lo:hi], func=AF.Exp,
                                            bias=vec("nb"), scale=sm_scale), r=[bst, bVEC], w=[ptb])
                                    else:
                                        P.act(I("activation",
                                            out=pt[:, lo:hi].bitcast(F32R), in_=st[:, lo:hi], func=AF.Exp,
                                            scale=sm_scale), r=[bst], w=[ptb])
                                P.pe(mm(ops_, VV[:, kt, kv * 128:(kv + 1) * 128].bitcast(F32R), pt.bitcast(F32R),
                                        kt == 0, kt == NKT - 1), r=b_VV + [ptb], w=[bops])
                                P.pe(mm(den, ones_r, pt.bitcast(F32R), kt == 0, kt == NKT - 1), r=[ptb, bCON], w=[bden])
                            rd, brd = tmp()
                            P.dve(I("reciprocal", out=rd, in_=den), r=[bden], w=[brd])
                            P.dve(I("tensor_tensor",
                                out=B[:, h, hs].bitcast(F32R), in0=ops_, in1=rd, op=ALU.mult), r=[bops, brd], w=[bB[h]])
                merge_branch(2, "br_att")

            for j in range(4):
                wo, bwo = next_block("out", j)
                for cc in range(2):
                    c = 2 * j + cc
                    for th in range(2):
                        hs = slice(th * 512, (th + 1) * 512)
                        ps, bps = proj(wo, bwo, cc * 128, Cc, bC, th)
                        P.act(I("activation", out=B[:, c, hs].bitcast(F32R), in_=ps, func=AF.Identity),
                              r=[bps], w=[bB[c]])
            post_norm_add(B, bB, g1, bg1)

            pre_norm(a2, ba2, 24)
            for q in range(4):
                for j in range(4):
                    wu, bwu = next_block("up", (q, j))
                    for cc in range(2):
                        c = 2 * j + cc
                        for th in range(2):
                            hs = slice(th * 512, (th + 1) * 512)
                            ps, bps = proj(wu, bwu, cc * 128, A, bA, th)
                            t_ap, bt = tmp()
                            P.act(I("activation", out=t_ap, in_=ps, func=AF.Relu),
                                  r=[bps], w=[bt])
                            ew()(I("tensor_tensor",
                                out=B[:, c, hs].bitcast(F32R), in0=t_ap, in1=t_ap, op=ALU.mult), r=[bt], w=[bB[c]])
                for j in range(4):
                    wd, bwd = next_block("down", (q, j))
                    for cc in range(2):
                        c = 2 * j + cc
                        for th in range(2):
                            hs = slice(th * 512, (th + 1) * 512)
                            ps, bps = proj(wd, bwd, cc * 128, B, bB, th)
                            if q == 0:
                                P.act(I("activation", out=Cc[:, c, hs].bitcast(F32R), in_=ps, func=AF.Identity),
                                      r=[bps], w=[bC[c]])
                            else:
                                P.dve(I("tensor_tensor",
                                    out=Cc[:, c, hs].bitcast(F32R), in0=ps, in1=Cc[:, c, hs], op=ALU.add), r=[bps, bC[c]], w=[bC[c]])
            post_norm_add(Cc, bC, g2, bg2)

        for c in range(NCK):
            P.dma("pool", d_y[:, c, :], X[:, c, :], r=[bX[c]], sem_buf=P.buf(f"yout{c}"), is_output=True)

        with nc.Block() as block:
            P.emit(block)
    return nc


def _pm(v):
    v = np.asarray(v, np.float32)
    return np.ascontiguousarray(v.reshape(-1, 128).T)


def _rope_tables(sample):
    tab = np.zeros((128, 2, T), np.float32)
    if not sample:
        tab[:, 0, :] = 1.0
        return tab
    t = np.arange(T)
    row = (t // 64).astype(np.float32)
    col = (t % 64).astype(np.float32)
    freqs = (np.float32(10000.0) ** (-np.arange(32, dtype=np.float32) / np.float32(32))).astype(np.float32)
    for d in range(128):
        pos = row if d // 64 == 0 else col
        ang = (pos * freqs[(d % 64) % 32]).astype(np.float32)
        tab[d, 0, :] = np.cos(ang)
        tab[d, 1, :] = np.sin(ang)
    return tab


def prepare_inputs(inp, depth=DEPTH, enable_dn=True):
    V = make_vec_layout(depth)
    blocks = layer_blocks(enable_dn)
    ws = np.empty((len(blocks) * depth, 2, 128, NCK * 128), np.float32)
    i = 0
    for l in range(depth):
        lw = {k: np.asarray(inp[k][l], np.float32) for k in
              ("w_mod", "w_in", "w_br_dn", "w_br_sc", "w_br_att", "w_out", "w_mlp_up", "w_mlp_down")}
        for kind, arg in blocks:
            blk = build_block(kind, arg, lw)
            ws[i] = np.asarray(blk, np.float32).reshape(NCK, 128, 2, 128).transpose(2, 1, 0, 3).reshape(2, 128, NCK * 128)
            i += 1
    consts = build_consts()

    def vecs_for(cond, sample):
        a = np.zeros((128, V.n), np.float32)
        a[:, V.sl("cond")] = _pm(cond)
        a[:, V.sl("fl")] = 1.0 if sample else 0.0
        a[:, V.sl("nb")] = 0.0 if sample else NEG
        a[:, V.sl("nfl")] = 0.0 if sample else -1.0
        for l in range(depth):
            a[:, V.sl(f"bmod{l}")] = _pm(inp["b_mod"][l])
            for g in ("g_pre_mix", "g_post_mix", "g_pre_mlp", "g_post_mlp"):
                a[:, V.sl(f"{g}{l}")] = _pm(inp[g][l])
            wc = np.asarray(inp["w_conv_dn"][l], np.float32)
            a[:, V.sl(f"wc_dn{l}")] = np.concatenate([_pm(wc[k]) for k in range(3)], axis=1)
            wc = np.asarray(inp["w_conv_sc"][l], np.float32)
            a[:, V.sl(f"wc_sc{l}")] = np.concatenate([_pm(wc[k]) for k in range(3)], axis=1)
            a[:, V.sl(f"g_dn{l}")] = np.asarray(inp["g_dn_out"][l], np.float32)[:, None]
            a[:, V.sl(f"g_q{l}")] = np.asarray(inp["g_q"][l], np.float32)[:, None]
            a[:, V.sl(f"g_k{l}")] = np.asarray(inp["g_k"][l], np.float32)[:, None]
            for nm, key in (("alog", "a_log"), ("dtb", "dt_bias")):
                row = np.asarray(inp[key][l], np.float32).reshape(16)
                a[:, V.sl(f"{nm}{l}")] = np.broadcast_to(row[None, :], (128, 16))
        return a

    xp = np.asarray(inp["x_prompt"], np.float32)
    xs = np.asarray(inp["x_sample"], np.float32)
    maps = []
    for core in range(N_CORES):
        slot = core if core < 6 else 0
        sample = slot >= 4
        if sample:
            b = slot - 4
            xx = xs[b]
            cond = np.asarray(inp["c"], np.float32)[b]
            ck = np.asarray(inp["cache_k"], np.float32)[b, :depth]
            cvv = np.asarray(inp["cache_v"], np.float32)[b, :depth]
            st = np.asarray(inp["state_dn"], np.float32)[b, :depth]
            ckT = np.ascontiguousarray(ck.transpose(3, 0, 2, 1))
            cv = np.ascontiguousarray(cvv.reshape(depth, 2, 128, 256).transpose(2, 0, 1, 3))
            s0 = np.ascontiguousarray(st.reshape(depth, 16, 128, 128).transpose(2, 0, 1, 3))
        else:
            xx = xp[4 * slot:4 * slot + 4].reshape(T, D)
            cond = np.asarray(inp["c_ctx"], np.float32)
            ckT = np.zeros((128, depth, 2, PAST), np.float32)
            cv = np.zeros((128, depth, 2, 256), np.float32)
            s0 = np.zeros((128, depth, 16, 128), np.float32)
        xT = np.ascontiguousarray(xx.reshape(T, NCK, 128).transpose(2, 1, 0))
        maps.append({"xT": xT, "vecs": vecs_for(cond, sample), "consts": consts, "ws": ws,
                     "rope": _rope_tables(sample), "ckT": ckT, "cv": cv, "s0": s0})
    return maps


def assemble_outputs(results, depth=DEPTH):
    y_prompt = np.zeros((16, SEG, D), np.float32)
    y_sample = np.zeros((2, T, D), np.float32)
    nk = np.zeros((16, depth, SEG, 2, HD), np.float32)
    nv = np.zeros((16, depth, SEG, 2, HD), np.float32)
    ns = np.zeros((16, depth, 2, 8, 128, 128), np.float32)
    for slot in range(6):
        r = results[slot]
        y = np.asarray(r["yT"]).transpose(2, 1, 0).reshape(T, D)
        if slot >= 4:
            y_sample[slot - 4] = y
            continue
        y_prompt[4 * slot:4 * slot + 4] = y.reshape(4, SEG, D)
        kT = np.asarray(r["kT_out"])
        vo = np.asarray(r["v_out"])
        so = np.asarray(r["st_out"])
        kk = kT.transpose(1, 3, 2, 0).reshape(depth, 4, SEG, 2, HD)
        nk[4 * slot:4 * slot + 4] = kk.transpose(1, 0, 2, 3, 4)
        vv = vo.transpose(1, 2, 0, 3).reshape(depth, T, 2, HD).reshape(depth, 4, SEG, 2, HD)
        nv[4 * slot:4 * slot + 4] = vv.transpose(1, 0, 2, 3, 4)
        ss = so.transpose(3, 1, 2, 0, 4).reshape(4, depth, 2, 8, 128, 128)
        ns[4 * slot:4 * slot + 4] = ss
    return y_prompt, y_sample, nk, nv, ns


_PROG_CACHE = {}


def kernel(**inputs):
    key = "full"
    if key not in _PROG_CACHE:
        _PROG_CACHE[key] = build_program(enable_dn=ENABLE_DN)
    nc = _PROG_CACHE[key]
    maps = prepare_inputs(inputs, enable_dn=ENABLE_DN)
    res = run_bass_kernel_spmd(nc, maps, core_ids=list(range(N_CORES)))
    return assemble_outputs(res.results)
```
